# Optimizing a Trainium2 kernel written in Bass

```python
import jax, jax.numpy as jnp
from jax import lax
import numpy as np

D_MODEL = 1024
BATCH = 8
SEQ = 2048
DEPTH = 4
DEC_BATCH = 128
DEC_SEQ = 4
PAST_LEN = 16384
PAGE_SIZE = 128

N_MIXERS = 4
D_FF = 2816
FFN_HALF = 0.5
EPS = 1e-6
CONV_A_WIDTH = 31
CONV_B_WIDTH = 3
GMLP_WIDTH = D_MODEL
CHUNK = 128
C_HEADS = 8
C_HEAD_DIM = GMLP_WIDTH // C_HEADS
POOL_WINDOWS = (2, 4, 8, 16)
POOL_GROUPS = len(POOL_WINDOWS)
POOL_GROUP_DIM = D_MODEL // POOL_GROUPS
POOL_PAD = max(POOL_WINDOWS) - 1

kernel_name = "hybrid_conv_gmlp_pool_decoder_step"


def rmsnorm(x, g):
    xf = x.astype(jnp.float32)
    y = xf * lax.rsqrt(jnp.mean(xf * xf, axis=-1, keepdims=True) + EPS)
    return (y * g.astype(jnp.float32)).astype(x.dtype)


def layernorm(x, g, b):
    xf = x.astype(jnp.float32)
    xc = xf - jnp.mean(xf, axis=-1, keepdims=True)
    y = xc * lax.rsqrt(jnp.mean(xc * xc, axis=-1, keepdims=True) + EPS)
    return (y * g.astype(jnp.float32) + b.astype(jnp.float32)).astype(x.dtype)


def depthwise_conv(ext, w):
    return lax.conv_general_dilated(
        ext, w.astype(ext.dtype)[:, None, :], window_strides=(1,), padding="VALID",
        dimension_numbers=("NWC", "WIO", "NWC"), feature_group_count=ext.shape[-1])


def swiglu(x, w_in, w_down):
    a = x @ w_in
    gate, up = jnp.split(a, 2, axis=-1)
    return (jax.nn.silu(gate) * up) @ w_down


def conformer_conv(x, buf, w_in, conv_w, conv_b, ln_g, ln_b, w_out):
    a = x @ w_in
    glu = a[..., :D_MODEL] * jax.nn.sigmoid(a[..., D_MODEL:])
    ext = jnp.concatenate([buf.astype(glu.dtype), glu], axis=1)
    c = depthwise_conv(ext, conv_w) + conv_b
    c = jax.nn.silu(layernorm(c, ln_g, ln_b))
    return c @ w_out, ext[:, -(CONV_A_WIDTH - 1):]


def short_gated_conv(x, buf, w_in, conv_w, w_out):
    p = x @ w_in
    b_gate, c_gate, xin = jnp.split(p, 3, axis=-1)
    z = c_gate * xin
    ext = jnp.concatenate([buf.astype(z.dtype), z], axis=1)
    c = depthwise_conv(ext, conv_w)
    return (b_gate * c) @ w_out, ext[:, -(CONV_B_WIDTH - 1):]


def chunk_spatial(v, ws, bs):
    bsz, length, _ = v.shape
    pad = (-length) % CHUNK
    vp = jnp.pad(v, ((0, 0), (0, pad), (0, 0)))
    n_chunks = (length + pad) // CHUNK
    vp = vp.reshape(bsz, n_chunks, CHUNK, C_HEADS, C_HEAD_DIM)
    mask = jnp.tril(jnp.ones((CHUNK, CHUNK), dtype=bool))
    wm = jnp.where(mask[None], ws, jnp.zeros_like(ws))
    out = jnp.einsum("hij,bnjhc->bnihc", wm, vp) + bs.T[None, None, :, :, None]
    return out.reshape(bsz, n_chunks * CHUNK, GMLP_WIDTH)[:, :length]


def chunk_gmlp(x, w_in, ln_g, ln_b, ws, bs, w_out):
    z = jax.nn.gelu(x @ w_in, approximate=False)
    u, v = jnp.split(z, 2, axis=-1)
    v = layernorm(v, ln_g, ln_b)
    s = chunk_spatial(v, ws, bs)
    return (u * s) @ w_out, v


def multiscale_pool(x, buf, pos0, w_group, scale):
    bsz, length, _ = x.shape
    ext = jnp.concatenate([buf.astype(x.dtype), x], axis=1)
    xf = ext.astype(jnp.float32)
    cs = jnp.concatenate([jnp.zeros((bsz, 1, D_MODEL), jnp.float32), jnp.cumsum(xf, axis=1)], axis=1)
    end = cs[:, POOL_PAD + 1:]
    pos = pos0 + jnp.arange(length)
    pooled = []
    for g, w in enumerate(POOL_WINDOWS):
        sl = slice(g * POOL_GROUP_DIM, (g + 1) * POOL_GROUP_DIM)
        start = cs[:, POOL_PAD + 1 - w: POOL_PAD + 1 - w + length, sl]
        cnt = jnp.minimum(w, pos + 1).astype(jnp.float32)[None, :, None]
        pooled.append((end[..., sl] - start) / cnt)
    diff = jnp.concatenate(pooled, axis=-1) - x.astype(jnp.float32)
    diff = diff.astype(x.dtype).reshape(bsz, length, POOL_GROUPS, POOL_GROUP_DIM)
    y = jnp.einsum("blgc,gcd->blgd", diff, w_group).reshape(bsz, length, D_MODEL)
    return y * scale, ext[:, -POOL_PAD:]


def trunk(x, buf_a, buf_b, buf_pool, pos0, norm_g, ffn_w_in, ffn_w_down,
          a_w_in, a_conv_w, a_conv_b, a_ln_g, a_ln_b, a_w_out,
          b_w_in, b_conv_w, b_w_out,
          c_w_in, c_ln_g, c_ln_b, c_ws, c_bs, c_w_out,
          d_w_group, d_scale):
    h = x
    st_a = st_b = st_c = st_d = None
    for i in range(DEPTH):
        g = norm_g[i]
        f = swiglu(rmsnorm(h, g[0]), ffn_w_in[i, 0], ffn_w_down[i, 0])
        h = h + FFN_HALF * rmsnorm(f, g[1])
        xm = rmsnorm(h, g[2])
        kind = i % N_MIXERS
        if kind == 0:
            m, st_a = conformer_conv(xm, buf_a, a_w_in, a_conv_w, a_conv_b, a_ln_g, a_ln_b, a_w_out)
        elif kind == 1:
            m, st_b = short_gated_conv(xm, buf_b, b_w_in, b_conv_w, b_w_out)
        elif kind == 2:
            m, st_c = chunk_gmlp(xm, c_w_in, c_ln_g, c_ln_b, c_ws, c_bs, c_w_out)
        else:
            m, st_d = multiscale_pool(xm, buf_pool, pos0, d_w_group, d_scale)
        h = h + rmsnorm(m, g[3])
        f = swiglu(rmsnorm(h, g[4]), ffn_w_in[i, 1], ffn_w_down[i, 1])
        h = h + FFN_HALF * rmsnorm(f, g[5])
    return h, st_a, st_b, st_c, st_d


def setup_inputs(seed: int = 0) -> dict:
    key = jax.random.key(seed)
    ks = jax.random.split(key, 26)

    def nrm(k, shape, scale):
        return jax.random.normal(k, shape, jnp.float32) * scale

    D = D_MODEL
    return {
        "x_prompt": nrm(ks[0], (BATCH, SEQ, D), 1.0),
        "x_sample": nrm(ks[1], (DEC_BATCH, DEC_SEQ, D), 1.0),
        "state_conv_a": nrm(ks[2], (DEC_BATCH, CONV_A_WIDTH - 1, D), 0.5),
        "state_conv_b": nrm(ks[3], (DEC_BATCH, CONV_B_WIDTH - 1, D), 0.5),
        "state_pool": nrm(ks[4], (DEC_BATCH, POOL_PAD, D), 1.0),
        "norm_g": 1.0 + nrm(ks[5], (DEPTH, 6, D), 0.05),
        "ffn_w_in": nrm(ks[6], (DEPTH, 2, D, 2 * D_FF), D ** -0.5),
        "ffn_w_down": nrm(ks[7], (DEPTH, 2, D_FF, D), D_FF ** -0.5),
        "a_w_in": nrm(ks[8], (D, 2 * D), D ** -0.5),
        "a_conv_w": nrm(ks[9], (CONV_A_WIDTH, D), CONV_A_WIDTH ** -0.5),
        "a_conv_b": nrm(ks[10], (D,), 0.02),
        "a_ln_g": 1.0 + nrm(ks[11], (D,), 0.05),
        "a_ln_b": nrm(ks[12], (D,), 0.02),
        "a_w_out": nrm(ks[13], (D, D), D ** -0.5),
        "b_w_in": nrm(ks[14], (D, 3 * D), D ** -0.5),
        "b_conv_w": nrm(ks[15], (CONV_B_WIDTH, D), CONV_B_WIDTH ** -0.5),
        "b_w_out": nrm(ks[16], (D, D), D ** -0.5),
        "c_w_in": nrm(ks[17], (D, 2 * GMLP_WIDTH), D ** -0.5),
        "c_ln_g": 1.0 + nrm(ks[18], (GMLP_WIDTH,), 0.05),
        "c_ln_b": nrm(ks[19], (GMLP_WIDTH,), 0.02),
        "c_ws": nrm(ks[20], (C_HEADS, CHUNK, CHUNK), CHUNK ** -0.5),
        "c_bs": 1.0 + nrm(ks[21], (C_HEADS, CHUNK), 0.1),
        "c_w_out": nrm(ks[22], (GMLP_WIDTH, D), GMLP_WIDTH ** -0.5),
        "d_w_group": nrm(ks[23], (POOL_GROUPS, POOL_GROUP_DIM, POOL_GROUP_DIM), POOL_GROUP_DIM ** -0.5),
        "d_scale": 1.0 + nrm(ks[24], (D,), 0.1),
    }


def reference(x_prompt, x_sample, state_conv_a, state_conv_b, state_pool,
              norm_g, ffn_w_in, ffn_w_down,
              a_w_in, a_conv_w, a_conv_b, a_ln_g, a_ln_b, a_w_out,
              b_w_in, b_conv_w, b_w_out,
              c_w_in, c_ln_g, c_ln_b, c_ws, c_bs, c_w_out,
              d_w_group, d_scale):
    weights = (norm_g, ffn_w_in, ffn_w_down,
               a_w_in, a_conv_w, a_conv_b, a_ln_g, a_ln_b, a_w_out,
               b_w_in, b_conv_w, b_w_out,
               c_w_in, c_ln_g, c_ln_b, c_ws, c_bs, c_w_out,
               d_w_group, d_scale)
    bp = x_prompt.shape[0]
    zero_a = jnp.zeros((bp, CONV_A_WIDTH - 1, D_MODEL), x_prompt.dtype)
    zero_b = jnp.zeros((bp, CONV_B_WIDTH - 1, D_MODEL), x_prompt.dtype)
    zero_pool = jnp.zeros((bp, POOL_PAD, D_MODEL), x_prompt.dtype)
    y_prompt, sa_p, sb_p, _, sp_p = trunk(x_prompt, zero_a, zero_b, zero_pool, 0, *weights)
    y_sample, sa_s, sb_s, sc_s, sp_s = trunk(x_sample, state_conv_a, state_conv_b, state_pool, PAST_LEN, *weights)
    return (y_prompt, y_sample, sa_p, sa_s, sb_p, sb_s, sc_s, sp_p, sp_s)
```

```python
import contextlib
import numpy as np
import concourse.bass as bass
import concourse.mybir as mybir
from concourse.bass_utils import run_bass_kernel_spmd

F32 = mybir.dt.float32
BF16 = mybir.dt.bfloat16
AF = mybir.ActivationFunctionType
ALU = mybir.AluOpType

SAME_ENGINE_SYNC = True
NCORES = 8
D = 1024
DFF = 2816
KC = 8
NJ = 22
SEQ = 2048
NS = 64
EPS = 1e-6
TS = 512
NPT = 1
GP = TS * NPT
NG = SEQ // GP
NT = GP + NS
RING = 4
SLOT_BYTES = 11264
DEBUG_TAPS = None
DBG_NG = None
DBG_CORES = None
DBG_LAYERS = 4
DBG_STOP = 99
DBG_VAR = ''
DBG_SUB = ('f0', 'm', 'f1')

R_NG = 0
R_ACW = 24
R_ACB = 55
R_ALG = 56
R_ALB = 57
R_BCW = 58
R_CLG = 61
R_CLB = 62
R_DSC = 63


class Prog:
    ENG = ('pe', 'act', 'dve', 'pool', 'sp')

    def __init__(self, nc, stack):
        self.nc = nc
        self.stack = stack
        self.streams = {e: [] for e in self.ENG}
        self.semh = {}
        self.ecount = {e: 0 for e in self.ENG}
        for e in self.ENG:
            self.semh[('e', e)] = stack.enter_context(nc.semaphore("es_" + e))
        self.waited = {e: {} for e in self.ENG}
        self.dcount = {}
        self.lastw = {}
        self.readers = {}

    def _deps(self, reads, writes, extra):
        deps = {}

        def add(t):
            if t is None:
                return
            sk, v = t
            if deps.get(sk, 0) < v:
                deps[sk] = v
        for t in extra:
            add(t)
        for b in reads:
            add(self.lastw.get(b))
        for b in writes:
            add(self.lastw.get(b))
            for sk, v in self.readers.get(b, {}).items():
                add((sk, v))
        return deps

    def _emit_waits(self, eng, deps):
        for sk in sorted(deps, key=str):
            v = deps[sk]
            if sk == ('e', eng) and (eng in ('pe', 'sp') or not SAME_ENGINE_SYNC):
                continue
            if self.waited[eng].get(sk, 0) >= v:
                continue
            self.waited[eng][sk] = v
            sem = self.semh[sk]
            self.streams[eng].append(lambda e, sem=sem, v=v: e.wait_ge(sem, v))

    def _record(self, tok, reads, writes):
        sk, v = tok
        for b in writes:
            self.lastw[b] = tok
            self.readers[b] = {}
        for b in reads:
            d = self.readers.setdefault(b, {})
            if d.get(sk, 0) < v:
                d[sk] = v

    def op(self, eng, fns, reads=(), writes=(), extra=()):
        writes = list(writes) + [b for b in reads if isinstance(b, tuple) and b[0] == 'ps']
        deps = self._deps(reads, writes, extra)
        self._emit_waits(eng, deps)
        if callable(fns):
            fns = [fns]
        self.ecount[eng] += 1
        c = self.ecount[eng]
        sem = self.semh[('e', eng)]
        for f in fns[:-1]:
            self.streams[eng].append(f)
        last = fns[-1]
        self.streams[eng].append(lambda e, last=last, sem=sem: last(e).then_inc(sem, 1))
        tok = (('e', eng), c)
        self._record(tok, reads, writes)
        return tok

    def dma(self, q, out, in_, key, reads=(), writes=(), extra=()):
        deps = self._deps(reads, writes, extra)
        self._emit_waits(q, deps)
        sk = ('d', key)
        if sk not in self.semh:
            self.semh[sk] = self.stack.enter_context(self.nc.semaphore("ds_%d" % len(self.semh)))
            self.dcount[sk] = 0
        self.dcount[sk] += 16
        sem = self.semh[sk]
        self.streams[q].append(lambda e, out=out, in_=in_, sem=sem: e.dma_start(out=out, in_=in_).then_inc(sem, 16))
        tok = (sk, self.dcount[sk])
        self._record(tok, reads, writes)
        return tok

    def dma_multi(self, q, pairs, key, reads=(), writes=()):
        deps = self._deps(reads, writes, ())
        self._emit_waits(q, deps)
        sk = ('d', key)
        if sk not in self.semh:
            self.semh[sk] = self.stack.enter_context(self.nc.semaphore("ds_%d" % len(self.semh)))
            self.dcount[sk] = 0
        sem = self.semh[sk]
        for out, in_ in pairs:
            self.dcount[sk] += 16
            self.streams[q].append(lambda e, out=out, in_=in_, sem=sem: e.dma_start(out=out, in_=in_).then_inc(sem, 16))
        tok = (sk, self.dcount[sk])
        self._record(tok, reads, writes)
        return tok

    def wait_all(self, eng, toks):
        deps = {}
        for sk, v in toks:
            if deps.get(sk, 0) < v:
                deps[sk] = v
        self._emit_waits(eng, deps)

    def barrier(self):
        engs = ('pe', 'act', 'dve', 'pool')
        toks = [(('e', e), self.ecount[e]) for e in engs if self.ecount[e] > 0]
        for e in engs:
            self.wait_all(e, [t for t in toks if t[0] != ('e', e)])

    def emit(self):
        streams = self.streams
        with self.nc.Block() as block:
            @block.tensor
            def _(e):
                for f in streams['pe']:
                    f(e)

            @block.scalar
            def _(e):
                for f in streams['act']:
                    f(e)

            @block.vector
            def _(e):
                for f in streams['dve']:
                    f(e)

            @block.gpsimd
            def _(e):
                for f in streams['pool']:
                    f(e)

            @block.sync
            def _(e):
                for f in streams['sp']:
                    f(e)


class _I:
    def __getattr__(self, name):
        def mk(*args, **kwargs):
            return lambda e: getattr(e, name)(*args, **kwargs)
        return mk


I = _I()


class Arena:
    def __init__(self, big, nbytes):
        self.big = big
        self.nbytes = nbytes
        self.top = 0

    def at(self, off, shape, dt, parts=128):
        n = int(np.prod(shape))
        esz = 4 if dt == F32 else 2
        nb = n * esz
        assert off % 4 == 0 and nb % 4 == 0, (off, nb)
        assert off + nb <= self.nbytes, ("SBUF overflow", off, nb, self.nbytes)
        v = self.big[0:parts, off // 4:(off + nb) // 4]
        if dt != F32:
            v = v.bitcast(dt)
        if len(shape) == 2:
            v = v.rearrange("p (a b) -> p a b", a=shape[0])
        elif len(shape) == 3:
            v = v.rearrange("p (a b c) -> p a b c", a=shape[0], b=shape[1])
        elif len(shape) == 4:
            v = v.rearrange("p (a b c d) -> p a b c d", a=shape[0], b=shape[1], c=shape[2])
        return v

    def alloc(self, shape, dt, parts=128):
        n = int(np.prod(shape))
        esz = 4 if dt == F32 else 2
        nb = (n * esz + 31) // 32 * 32
        off = self.top
        self.top += nb
        return self.at(off, shape, dt, parts), off

    def reserve(self, nbytes):
        off = self.top
        self.top += (nbytes + 31) // 32 * 32
        assert self.top <= self.nbytes, ("SBUF overflow", self.top)
        return off


class TileT:
    def __init__(self, kind, idx, off, n, tok0):
        self.kind = kind
        self.idx = idx
        self.off = off
        self.n = n
        self.tok0 = tok0

    @property
    def sl(self):
        return slice(self.off, self.off + self.n)


def ids(name, t, n=8):
    return [(name, t, c) for c in range(n)]


def build_program():
    nc = bass.Bass("TRN2", target_bir_lowering=False)

    def din(name, shape):
        return nc.dram_tensor(name, shape, F32, kind="ExternalInput").ap()

    def dout(name, shape):
        return nc.dram_tensor(name, shape, F32, kind="ExternalOutput").ap()

    xp = din("xp", [SEQ, D])
    xs = din("xs", [NS, D])
    sa = din("sa", [16 * 30, D])
    sbi = din("sb", [16 * 2, D])
    spi = din("sp", [16 * 15, D])
    vecs = din("vecs", [64, D])
    ffn_w_in = din("ffn_w_in", [4, 2, D, 2 * DFF])
    ffn_w_down = din("ffn_w_down", [4, 2, DFF, D])
    a_w_in = din("a_w_in", [D, 2 * D])
    a_w_out = din("a_w_out", [D, D])
    b_w_in = din("b_w_in", [D, 3 * D])
    b_w_out = din("b_w_out", [D, D])
    c_w_in = din("c_w_in", [D, 2 * D])
    c_w_out = din("c_w_out", [D, D])
    c_ws = din("c_ws", [8, 128, 128])
    c_bs = din("c_bs", [8, 128])
    d_w_group = din("d_w_group", [4, 256, 256])

    yp = dout("yp", [SEQ, D])
    ys = dout("ys", [NS, D])
    o_sa_p = dout("o_sa_p", [30, D])
    o_sa_s = dout("o_sa_s", [16 * 30, D])
    o_sb_p = dout("o_sb_p", [2, D])
    o_sb_s = dout("o_sb_s", [16 * 2, D])
    o_sc_s = dout("o_sc_s", [NS, D])
    o_sp_p = dout("o_sp_p", [15, D])
    o_sp_s = dout("o_sp_s", [16 * 15, D])
    taps = {}
    if DEBUG_TAPS:
        for nm in sorted(DEBUG_TAPS):
            taps[nm] = dout("tap_" + nm, [128, 8, SEQ + NS])

    with contextlib.ExitStack() as st:
        p = Prog(nc, st)
        SB_BYTES = 211968
        big = st.enter_context(nc.sbuf_tensor("big", [128, SB_BYTES // 4], F32))
        A = Arena(big, SB_BYTES)
        ps = [st.enter_context(nc.psum_tensor("ps%d" % i, [128, 512], F32)) for i in range(8)]
        bank_rr = [0]

        def bank():
            b = bank_rr[0]
            bank_rr[0] = (b + 1) % 8
            return b

        ident, _ = A.alloc([128], F32)
        identb, _ = A.alloc([128], BF16)
        ones_b, _ = A.alloc([128], BF16)
        neg_half, _ = A.alloc([TS], F32)
        ones_f, _ = A.alloc([TS], F32)
        vT, _ = A.alloc([8, 64], F32)
        hg, _ = A.alloc([8, 8], F32)
        dg, _ = A.alloc([8], F32)
        wmT, _ = A.alloc([8, 128], BF16)
        bsb, _ = A.alloc([8, 128], F32)
        wsb, _ = A.alloc([8, 16], F32)
        bs4, _ = A.alloc([8, 4], F32)
        invcnt, _ = A.alloc([4, 16], F32)
        eps_t, _ = A.alloc([8], F32)
        dummy, _ = A.alloc([8], F32)
        glu_halo, _ = A.alloc([8, 30], BF16)
        z_halo, _ = A.alloc([8, 2], F32)
        xm_halo, _ = A.alloc([8, 15], F32)
        gluf_tail, _ = A.alloc([8, 30], F32)
        h, _ = A.alloc([8, NT], F32)
        xn, _ = A.alloc([8, NT], BF16)
        f, _ = A.alloc([8, NT], F32)
        yb, _ = A.alloc([8, NT], BF16)
        NTL = NPT + 1
        tsz = [TS] * NPT + [NS]
        sqb = [A.alloc([8, tsz[i]], BF16)[0] for i in range(NTL)]
        cbb = [A.alloc([8, tsz[i]], BF16)[0] for i in range(NTL)]
        st_t = [A.alloc([max(tsz[i], 8)], F32)[0] for i in range(NTL)]
        st_r = [A.alloc([tsz[i]], F32)[0] for i in range(NTL)]
        st_m = [A.alloc([tsz[i]], F32)[0] for i in range(NTL)]
        st_n = [A.alloc([tsz[i]], F32)[0] for i in range(NTL)]
        NSG = 3
        sg = [A.alloc([TS], F32)[0] for _ in range(NSG)]
        sg2 = [A.alloc([TS], F32)[0] for _ in range(NSG)]
        stage_in = [A.alloc([D], F32)[0] for _ in range(2)]
        stage_out = [A.alloc([D], F32)[0] for _ in range(2)]
        ring_off = [A.reserve(SLOT_BYTES) for _ in range(RING)]
        R0 = A.reserve(36 * 1024)
        assert A.top <= SB_BYTES, A.top
        print('SBUF bytes used', A.top, 'of', SB_BYTES)
        gbuf = A.at(R0, [NJ, NT], BF16)
        o = R0
        glub = A.at(o, [8, 30 + GP], BF16); o += 8 * (30 + GP) * 2
        exts = A.at(o, [8, 34, 16], BF16); o += 8 * 34 * 16 * 2
        glus = A.at(o, [8, NS], F32); o += 8 * NS * 4
        diag = []
        for i in range(2):
            diag.append(A.at(o, [31, 128], BF16)); o += 31 * 128 * 2
        assert o <= R0 + 36 * 1024, o - R0
        o = R0
        zp = A.at(o, [8, 2 + GP], F32); o += 8 * (2 + GP) * 4
        zs = A.at(o, [8, 6, 16], F32); o += 8 * 6 * 16 * 4
        assert o <= R0 + 36 * 1024
        o = R0
        ubuf = A.at(o, [8, NT], F32); o += 8 * NT * 4
        vtok = []
        for i in range(GP // 128):
            vtok.append(A.at(o, [D], BF16)); o += D * 2
        accs = A.at(o, [NS], F32); o += NS * 4
        assert o <= R0 + 36 * 1024, o - R0
        o = R0
        xme_p = A.at(o, [8, 15 + GP], F32); o += 8 * (15 + GP) * 4
        xme_s = A.at(o, [8, 19, 16], F32); o += 8 * 19 * 16 * 4
        ptmp = []
        for i in range(2):
            ptmp.append(A.at(o, [2, max(15 + TS, 304)], F32)); o += 2 * max(15 + TS, 304) * 4
        assert o <= R0 + 36 * 1024, o - R0

        def vrow(c, r):
            return vT[:, c, r:r + 1]

        si_rr = [0]
        so_rr = [0]
        out_toks = []

        def load_T(src, R, dst_fn, dst_ids, eng='dve', src_list=None):
            s = si_rr[0]
            si_rr[0] ^= 1
            stg = stage_in[s]
            sid = ('stgin', s)
            if src_list is None:
                src_list = [(0, R, src)]
            for (r0, nr, ap) in src_list:
                p.dma('sp', stg[r0:r0 + nr, :], ap, ('ldin', s, r0), writes=[sid])
            for half in range(2):
                b = bank()
                fns = [(I.transpose(ps[b][:, k4 * R:(k4 + 1) * R],
                                                          stg[0:R, (half * 4 + k4) * 128:(half * 4 + k4 + 1) * 128],
                                                          ident[0:R, 0:R])) for k4 in range(4)]
                p.op('pe', fns, reads=[sid, 'ident'], writes=[('ps', b)])
                for k4 in range(4):
                    c = half * 4 + k4
                    dst, src_view = dst_fn(c, ps[b][:, k4 * R:(k4 + 1) * R])
                    p.op(eng, I.tensor_copy(out=dst, in_=src_view),
                         reads=[('ps', b)], writes=[dst_ids[c]])

        def store_T(src_fn, src_ids, R, dsts, eng='act'):
            s = so_rr[0]
            so_rr[0] ^= 1
            stg = stage_out[s]
            sid = ('stgout', s)
            for half in range(2):
                b = bank()
                fns = [(I.transpose(ps[b][0:R, k4 * 128:(k4 + 1) * 128],
                                                          src_fn(half * 4 + k4), ident[:, :])) for k4 in range(4)]
                p.op('pe', fns, reads=[src_ids[half * 4 + k4] for k4 in range(4)] + ['ident'], writes=[('ps', b)])
                if eng == 'act':
                    p.op('act', I.activation(out=stg[0:R, half * 512:(half + 1) * 512],
                                                                       in_=ps[b][0:R, :], func=AF.Copy),
                         reads=[('ps', b)], writes=[sid])
                else:
                    p.op('dve', I.tensor_copy(out=stg[0:R, half * 512:(half + 1) * 512],
                                                                        in_=ps[b][0:R, :]),
                         reads=[('ps', b)], writes=[sid])
            for (r0, nr, ap) in dsts:
                out_toks.append(p.dma('sp', ap, stg[r0:r0 + nr, :], ('st', s, r0), reads=[sid]))

        def mm(bank_ap, pairs, reads, writes):
            n = len(pairs)
            fns = [(I.matmul(bank_ap, lhsT=l, rhs=r, start=(i == 0), stop=(i == n - 1)))
                   for i, (l, r) in enumerate(pairs)]
            return p.op('pe', fns, reads=reads, writes=writes)

        def wplan_group():
            plan = []

            def cols(w2d, c0, width):
                return w2d.rearrange("(kc p) n -> p kc n", p=128)[:, :, c0:c0 + width]

            for l in range(DBG_LAYERS):
                for s_ in range(2):
                    if s_ == 1 and 'm' in DBG_SUB:
                        if l == 0:
                            for q in range(2):
                                plan.append((('a_in', q, 0), cols(a_w_in, q * 512, 512), [8, 512]))
                                plan.append((('a_in', q, 1), cols(a_w_in, D + q * 512, 512), [8, 512]))
                            for q in range(2):
                                plan.append((('a_out', q), cols(a_w_out, q * 512, 512), [8, 512]))
                        elif l == 1:
                            for q in range(2):
                                for part in range(3):
                                    plan.append((('b_in', q, part), cols(b_w_in, part * D + q * 512, 512), [8, 512]))
                            for q in range(2):
                                plan.append((('b_out', q), cols(b_w_out, q * 512, 512), [8, 512]))
                        elif l == 2:
                            for q in range(2):
                                plan.append((('c_in', q, 0), cols(c_w_in, q * 512, 512), [8, 512]))
                                plan.append((('c_in', q, 1), cols(c_w_in, D + q * 512, 512), [8, 512]))
                            for q in range(2):
                                plan.append((('c_out', q), cols(c_w_out, q * 512, 512), [8, 512]))
                        else:
                            plan.append((('d_w', 0), d_w_group.rearrange("g (cc p) n -> p g cc n", p=128), [4, 2, 256]))
                    if ('f%d' % s_) not in DBG_SUB:
                        continue
                    for q in range(6):
                        width = 512 if q < 5 else 256
                        plan.append((('f_in', l, s_, q, 0), cols(ffn_w_in[l, s_], q * 512, width), [8, width]))
                        plan.append((('f_in', l, s_, q, 1), cols(ffn_w_in[l, s_], DFF + q * 512, width), [8, width]))
                    for q in range(2):
                        for kh in range(2):
                            plan.append((('f_dn', l, s_, q, kh),
                                         ffn_w_down[l, s_].rearrange("(kc p) n -> p kc n", p=128)[:, kh * 11:(kh + 1) * 11, q * 512:(q + 1) * 512], [11, 512]))
            return plan

        plan = []
        NGR = NG if DBG_NG is None else DBG_NG
        for g_ in range(NGR):
            plan += [((g_,) + k, ap, shp) for (k, ap, shp) in wplan_group()]

        NBLK = len(wplan_group())
        SLOT_ELEMS = SLOT_BYTES // 2
        wscr = None
        if NGR > 1:
            wscr = nc.dram_tensor("wscr", [NBLK, 128, SLOT_ELEMS], BF16, kind="Internal").ap()

        class WS:
            nxt = 0
            cur = 0
            views = {}

        def w_issue():
            k = WS.nxt
            key, src, shp = plan[k]
            s = k % RING
            kk = k % NBLK
            nel = int(np.prod(shp))
            view = A.at(ring_off[s], shp, BF16)
            flat = A.at(ring_off[s], [nel], BF16)
            if key[0] == 0:
                if shp[0] > 8:
                    pairs = [(view[:, k0:min(k0 + 8, shp[0]), :], src[:, k0:min(k0 + 8, shp[0]), :]) for k0 in range(0, shp[0], 8)]
                else:
                    pairs = [(view, src)]
                p.dma_multi('pool', pairs, ('w', s), writes=[('wslot', s)])
                if wscr is not None:
                    p.dma('sp', wscr[kk][:, 0:nel], flat, ('wb', s), reads=[('wslot', s)], writes=[('wscr', kk)])
            else:
                p.dma('sp', flat, wscr[kk][:, 0:nel], ('wh', s), reads=[('wscr', kk)], writes=[('wslot', s)])
            WS.views[k] = view
            WS.nxt += 1

        def w_getn(keys):
            out = []
            for i, key in enumerate(keys):
                k = WS.cur + i
                assert plan[k][0] == key, (plan[k][0], key)
                while WS.nxt <= k:
                    w_issue()
                out.append((WS.views[k], ('wslot', k % RING)))
            return out

        def w_get(key):
            return w_getn([key])[0]

        def wids(wid):
            return [wid]

        def w_done(n=1):
            for _ in range(n):
                WS.cur += 1
            while WS.nxt < len(plan) and WS.nxt < WS.cur + RING:
                w_issue()

        p.op('pool', I.memset(ident[:, :], 0.0), writes=['ident'])
        p.op('pool', I.affine_select(out=ident[:, :], in_=ident[:, :], compare_op=ALU.not_equal, fill=1.0,
                                     base=0, pattern=[[-1, 128]], channel_multiplier=1),
             reads=['ident'], writes=['ident'])
        p.op('dve', I.tensor_copy(out=identb[:, :], in_=ident[:, :]), reads=['ident'], writes=['identb'])
        p.op('dve', I.memset(ones_b[:, :], 1.0 / 1024.0), writes=['ones'])
        p.op('dve', I.memset(neg_half[:, :], -0.5), writes=['nh'])
        p.op('dve', I.memset(ones_f[:, :], 1.0), writes=['ones_f'])
        p.op('dve', I.memset(eps_t[:, :], EPS), writes=['eps_t'])
        p.op('dve', I.memset(glu_halo[:, :, :], 0.0), writes=['glu_halo'])
        p.op('dve', I.memset(z_halo[:, :, :], 0.0), writes=['z_halo'])
        p.op('dve', I.memset(xm_halo[:, :, :], 0.0), writes=['xm_halo'])
        fns = []
        for gi, w in enumerate((2, 4, 8, 16)):
            for t in range(16):
                fns.append(I.memset(invcnt[:, gi, t:t + 1], 1.0 / min(w, t + 1)))
        p.op('dve', fns, writes=['invcnt'])
        for i in range(min(RING, len(plan))):
            w_issue()
        vids = [('vT', c) for c in range(8)]
        load_T(vecs[:, :], 64, lambda c, src: (vT[:, c, :], src), vids)
        for l in range(4):
            for wi, i in enumerate((1, 5)):
                p.op('dve', I.tensor_scalar(out=hg[:, :, l * 2 + wi], in0=vT[:, :, R_NG + l * 6 + i],
                                                                      scalar1=0.5, scalar2=None, op0=ALU.mult),
                     reads=vids, writes=['hg'])
        p.op('dve', I.tensor_tensor(out=dg[:, :], in0=vT[:, :, R_DSC], in1=vT[:, :, R_NG + 3 * 6 + 3], op=ALU.mult),
             reads=vids, writes=['dg'])
        for hh in range(8):
            s = si_rr[0]
            si_rr[0] ^= 1
            stg = stage_in[s]
            sid = ('stgin', s)
            p.dma('sp', stg[:, 0:128], c_ws[hh], ('ldin', s, 0), writes=[sid])
            b = bank()
            p.op('pe', I.transpose(ps[b][:, 0:128], stg[:, 0:128], ident[:, :]),
                 reads=[sid, 'ident'], writes=[('ps', b)])
            p.op('dve', I.tensor_copy(out=wmT[:, hh, :], in_=ps[b][:, 0:128]),
                 reads=[('ps', b)], writes=[('wmT', hh)])
            p.op('pool', I.affine_select(out=wmT[:, hh, :], in_=wmT[:, hh, :], compare_op=ALU.is_ge, fill=0.0,
                                                          base=0, pattern=[[1, 128]], channel_multiplier=-1),
                 reads=[('wmT', hh)], writes=[('wmT', hh)])
        with nc.allow_non_contiguous_dma(reason="tiny broadcast loads"):
            cpairs = []
            for hh in range(8):
                cpairs.append((bsb[:, hh, :], c_bs[hh:hh + 1, :].partition_broadcast(128)))
                cpairs.append((bs4[:, hh, :], c_bs[hh:hh + 1, 0:4].partition_broadcast(128)))
                for i in range(4):
                    cpairs.append((wsb[:, hh, i * 4:(i + 1) * 4], c_ws[hh, i:i + 1, 0:4].partition_broadcast(128)))
            p.dma_multi('sp', cpairs, 'cst', writes=['cst'])
        out_toks.append(p.dma('sp', o_sa_s.rearrange("(s r) n -> s r n", r=30)[:, 0:26, :],
                              sa.rearrange("(s r) n -> s r n", r=30)[:, 4:30, :], 'd2d'))

        def tap(name, tiles, g_):
            if not DEBUG_TAPS or name not in DEBUG_TAPS:
                return
            flush_all()
            for t in tiles:
                col = g_ * GP + t.off if t.kind == 'p' else SEQ
                out_toks.append(p.dma('sp', taps[name][:, :, col:col + t.n], h[:, :, t.sl], ('tap', name, t.idx), reads=ids('h', t.idx)))

        NTLX = NPT + 1
        G_IDS = [('g', ti, j) for ti in range(NTLX) for j in range(NJ)]

        def guard(write_ids):
            tok = p.op('dve', I.memset(dummy[:, 0:1], 0.0), writes=list(write_ids) + ['dummy'])
            p.wait_all('act', [tok])
            p.wait_all('pool', [tok])

        def tok_sums(t, srcs, reads):
            n = t.n
            b = bank()
            fns = []
            for i, src in enumerate(srcs):
                for blk in range((n + 127) // 128):
                    P = min(128, n - blk * 128)
                    for c in range(8):
                        fns.append(I.matmul(ps[b][0:P, i * 4 + blk:i * 4 + blk + 1], lhsT=src[:, c, blk * 128:blk * 128 + P],
                                            rhs=ones_b[:, 0:1], start=(c == 0), stop=(c == 7)))
            p.op('pe', fns, reads=reads + ['ones'], writes=[('ps', b)])
            return b

        def bcast_tok(t, col, dst, dst_id):
            n = t.n
            nblk = (n + 127) // 128
            for blk in range(nblk):
                P = min(128, n - blk * 128)
                p.op('dve', I.scalar_tensor_tensor(out=st_m[t.idx][0:P, blk * 128:blk * 128 + P], in0=ident[0:P, 0:P],
                                                   scalar=st_t[t.idx][0:P, col + blk:col + blk + 1], in1=ones_f[0:P, 0:P],
                                                   op0=ALU.mult, op1=ALU.mult),
                     reads=[('st_t', t.idx), 'ident', 'ones_f'], writes=[('st_m', t.idx, blk)])
            b2 = bank()
            fns = []
            for blk in range(nblk):
                P = min(128, n - blk * 128)
                fns.append(I.matmul(ps[b2][:, blk * 128:blk * 128 + P], lhsT=ones_f[0:P, 0:128],
                                    rhs=st_m[t.idx][0:P, blk * 128:blk * 128 + P], start=True, stop=True))
            p.op('pe', fns, reads=[('st_m', t.idx, blk) for blk in range(nblk)] + ['ones_f'], writes=[('ps', b2)])
            p.op('act', I.activation(out=dst, in_=ps[b2][:, 0:n], func=AF.Copy), reads=[('ps', b2)], writes=[dst_id])

        def stats_rstd_gen(t, sq_src, sq_ids):
            n = t.n
            nblk = (n + 127) // 128
            P = min(128, n)
            b = tok_sums(t, [sq_src], sq_ids)
            p.op('dve', I.tensor_scalar(out=st_t[t.idx][0:P, 0:nblk], in0=ps[b][0:P, 0:nblk], scalar1=EPS, scalar2=None, op0=ALU.add),
                 reads=[('ps', b)], writes=[('st_t', t.idx)])
            p.op('pool', I.tensor_tensor(out=st_t[t.idx][0:P, 0:nblk], in0=st_t[t.idx][0:P, 0:nblk], in1=neg_half[0:P, 0:nblk], op=ALU.pow),
                 reads=[('st_t', t.idx), 'nh'], writes=[('st_t', t.idx)])
            yield
            yield
            bcast_tok(t, 0, st_r[t.idx][:, 0:n], ('st_r', t.idx))

        def prenorm_gen(t, grow, dst_fn, dst_ids):
            n = t.n
            for hf in range(2):
                cs = slice(4 * hf, 4 * hf + 4)
                p.op('act', I.activation(out=sqb[t.idx][:, cs, 0:n], in_=h[:, cs, t.sl], func=AF.Square),
                     reads=ids('h', t.idx)[cs], writes=ids('sqb', t.idx)[cs])
            yield
            yield
            yield from stats_rstd_gen(t, sqb[t.idx], ids('sqb', t.idx))
            yield
            for c in range(8):
                p.op('dve', I.scalar_tensor_tensor(out=dst_fn(c), in0=h[:, c, t.sl], scalar=vrow(c, grow),
                                                   in1=st_r[t.idx][:, 0:n], op0=ALU.mult, op1=ALU.mult),
                     reads=[('h', t.idx, c), ('st_r', t.idx), ('vT', c)], writes=[dst_ids[c]])

        def prenorm(t, grow, dst_fn, dst_ids):
            for _ in prenorm_gen(t, grow, dst_fn, dst_ids):
                pass

        def postnorm_gen(t):
            n = t.n
            yield
            yield from stats_rstd_gen(t, sqb[t.idx], ids('sqb', t.idx))
            yield
            rb = st_r[t.idx][:, 0:n].unsqueeze(1).broadcast_to([128, 4, n])
            for hf in range(2):
                cs = slice(4 * hf, 4 * hf + 4)
                p.op('dve', I.tensor_tensor(out=f[:, cs, t.sl], in0=f[:, cs, t.sl], in1=rb, op=ALU.mult),
                     reads=ids('f', t.idx)[cs] + [('st_r', t.idx)], writes=ids('f', t.idx)[cs])
                p.op('dve', I.tensor_tensor(out=h[:, cs, t.sl], in0=h[:, cs, t.sl], in1=f[:, cs, t.sl], op=ALU.add),
                     reads=ids('f', t.idx)[cs] + ids('h', t.idx)[cs], writes=ids('h', t.idx)[cs])

        def postnorm_residual(t):
            for _ in postnorm_gen(t):
                pass

        pending = {}

        def chain_gen(t, pre_row):
            yield from postnorm_gen(t)
            if pre_row is not None:
                yield from prenorm_gen(t, pre_row, lambda c, t=t: xn[:, c, t.sl], ids('xn', t.idx))

        def start_chain(t, pre_row):
            need(t)
            pending[t.idx] = chain_gen(t, pre_row)

        def pump():
            for k in list(pending):
                try:
                    next(pending[k])
                except StopIteration:
                    del pending[k]

        def need(t):
            g = pending.pop(t.idx, None)
            if g is not None:
                for _ in g:
                    pass

        def flush_all():
            for k in list(pending):
                g = pending.pop(k)
                for _ in g:
                    pass

        def out_proj(g_, tiles, wkey, gain_fn, after):
            for q in range(2):
                wv, wid = w_get((g_,) + wkey + (q,))
                order = [(mmi, t) for mmi in range(4) for t in tiles] if q == 0 else [(mmi, t) for t in tiles for mmi in range(4)]
                for (mmi, t) in order:
                    m = q * 4 + mmi
                    b = bank()
                    mm(ps[b][:, 0:t.n], [(wv[:, kc, mmi * 128:(mmi + 1) * 128], yb[:, kc, t.sl]) for kc in range(8)],
                       reads=ids('yb', t.idx) + wids(wid), writes=[('ps', b)])
                    pump()
                    p.op('dve', I.scalar_tensor_tensor(out=f[:, m, t.sl], in0=ps[b][:, 0:t.n], scalar=gain_fn(m),
                                                       in1=ones_f[:, 0:t.n], op0=ALU.mult, op1=ALU.mult),
                         reads=[('ps', b), 'hg', 'dg', 'ones_f'] + vids, writes=[('f', t.idx, m)])
                    p.op('act', I.activation(out=sqb[t.idx][:, m, 0:t.n], in_=ps[b][:, 0:t.n], func=AF.Square),
                         reads=[('ps', b)], writes=[('sqb', t.idx, m)])
                    if q == 1 and mmi == 3:
                        after(t)
                w_done()

        sg_rr = [0]

        def ffn(g_, l, s_, tiles, after):
            for q in range(6):
                (gv, gid), (uv, uid) = w_getn([(g_, 'f_in', l, s_, q, 0), (g_, 'f_in', l, s_, q, 1)])
                njj = 4 if q < 5 else 2
                order = [(jj, t) for t in tiles for jj in range(njj)] if q == 0 else [(jj, t) for jj in range(njj) for t in tiles]
                for (jj, t) in order:
                    j = q * 4 + jj
                    n = t.n
                    need(t)
                    bg_ = bank()
                    mm(ps[bg_][:, 0:n], [(gv[:, kc, jj * 128:(jj + 1) * 128], xn[:, kc, t.sl]) for kc in range(8)],
                       reads=ids('xn', t.idx) + [gid], writes=[('ps', bg_)])
                    bu_ = bank()
                    mm(ps[bu_][:, 0:n], [(uv[:, kc, jj * 128:(jj + 1) * 128], xn[:, kc, t.sl]) for kc in range(8)],
                       reads=ids('xn', t.idx) + [uid], writes=[('ps', bu_)])
                    pump()
                    k = sg_rr[0]
                    sg_rr[0] = (k + 1) % NSG
                    p.op('act', I.activation(out=sg[k][:, 0:n], in_=ps[bg_][:, 0:n], func=AF.Silu),
                         reads=[('ps', bg_)], writes=[('sg', k)])
                    p.op('dve', I.tensor_tensor(out=gbuf[:, j, t.sl], in0=ps[bu_][:, 0:n], in1=sg[k][:, 0:n], op=ALU.mult),
                         reads=[('ps', bu_), ('sg', k)], writes=[('g', t.idx, j)])
                w_done(2)
            gi = l * 2 + s_
            for q in range(2):
                wvs = w_getn([(g_, 'f_dn', l, s_, q, 0), (g_, 'f_dn', l, s_, q, 1)])
                order = [(mmi, t) for mmi in range(4) for t in tiles] if q == 0 else [(mmi, t) for t in tiles for mmi in range(4)]
                for (mmi, t) in order:
                    m = q * 4 + mmi
                    n = t.n
                    b = bank()
                    mm(ps[b][:, 0:n], [(wvs[j // 11][0][:, j % 11, mmi * 128:(mmi + 1) * 128], gbuf[:, j, t.sl]) for j in range(NJ)],
                       reads=ids('g', t.idx, NJ) + [wvs[0][1], wvs[1][1]], writes=[('ps', b)])
                    pump()
                    p.op('dve', I.scalar_tensor_tensor(out=f[:, m, t.sl], in0=ps[b][:, 0:n], scalar=hg[:, m, gi:gi + 1],
                                                       in1=ones_f[:, 0:n], op0=ALU.mult, op1=ALU.mult),
                         reads=[('ps', b), 'hg', 'ones_f'], writes=[('f', t.idx, m)])
                    p.op('act', I.activation(out=sqb[t.idx][:, m, 0:n], in_=ps[b][:, 0:n], func=AF.Square),
                         reads=[('ps', b)], writes=[('sqb', t.idx, m)])
                    if q == 1 and mmi == 3:
                        after(t)
                w_done(2)

        def ln_stats(t, src_f_ids):
            n = t.n
            nblk = (n + 127) // 128
            P = min(128, n)
            S = st_t[t.idx]
            b = tok_sums(t, [cbb[t.idx], sqb[t.idx]], ids('cbb', t.idx) + ids('sqb', t.idx))
            p.op('dve', I.tensor_copy(out=S[0:P, 8:8 + nblk], in_=ps[b][0:P, 0:nblk]), reads=[('ps', b)], writes=[('st_t', t.idx)])
            p.op('dve', I.tensor_tensor(out=S[0:P, 16:16 + nblk], in0=S[0:P, 8:8 + nblk], in1=S[0:P, 8:8 + nblk], op=ALU.mult),
                 reads=[('st_t', t.idx)], writes=[('st_t', t.idx)])
            p.op('dve', I.scalar_tensor_tensor(out=S[0:P, 0:nblk], in0=ps[b][0:P, 4:4 + nblk], scalar=EPS, in1=S[0:P, 16:16 + nblk],
                                               op0=ALU.add, op1=ALU.subtract),
                 reads=[('ps', b), ('st_t', t.idx)], writes=[('st_t', t.idx)])
            p.op('pool', I.tensor_tensor(out=S[0:P, 0:nblk], in0=S[0:P, 0:nblk], in1=neg_half[0:P, 0:nblk], op=ALU.pow),
                 reads=[('st_t', t.idx), 'nh'], writes=[('st_t', t.idx)])
            p.op('dve', I.scalar_tensor_tensor(out=S[0:P, 16:16 + nblk], in0=S[0:P, 8:8 + nblk], scalar=-1.0, in1=S[0:P, 0:nblk],
                                               op0=ALU.mult, op1=ALU.mult),
                 reads=[('st_t', t.idx)], writes=[('st_t', t.idx)])
            bcast_tok(t, 0, st_r[t.idx][:, 0:n], ('st_r', t.idx))
            bcast_tok(t, 16, st_n[t.idx][:, 0:n], ('st_n', t.idx))
            rb = st_r[t.idx][:, 0:n].unsqueeze(1).broadcast_to([128, 8, n])
            nb_ = st_n[t.idx][:, 0:n].unsqueeze(1).broadcast_to([128, 8, n])
            p.op('dve', I.tensor_tensor(out=f[:, :, t.sl], in0=f[:, :, t.sl], in1=rb, op=ALU.mult),
                 reads=ids('f', t.idx) + [('st_r', t.idx)], writes=ids('f', t.idx))
            p.op('dve', I.tensor_tensor(out=f[:, :, t.sl], in0=f[:, :, t.sl], in1=nb_, op=ALU.add),
                 reads=ids('f', t.idx) + [('st_n', t.idx)], writes=ids('f', t.idx))

        def mixer_a(g_, tiles, after):
            lrow = R_NG + 0 * 6
            guard(G_IDS)
            ptiles = [t for t in tiles if t.kind == 'p']
            stiles = [t for t in tiles if t.kind == 's']
            last = (g_ == NG - 1)
            p.op('dve', I.tensor_copy(out=glub[:, :, 0:30], in_=glu_halo[:, :, :]), reads=['glu_halo'], writes=['glub_halo'])
            for q in range(2):
                (wv0, wid0), (wv1, wid1) = w_getn([(g_, 'a_in', q, 0), (g_, 'a_in', q, 1)])
                order = [(jj, t) for t in tiles for jj in range(4)] if q == 0 else [(jj, t) for jj in range(4) for t in tiles]
                for (jj, t) in order:
                    c = q * 4 + jj
                    need(t)
                    pump()
                    if True:
                        n = t.n
                        bv = bank()
                        mm(ps[bv][:, 0:n], [(wv0[:, kc, jj * 128:(jj + 1) * 128], xn[:, kc, t.sl]) for kc in range(8)],
                           reads=ids('xn', t.idx) + [wid0], writes=[('ps', bv)])
                        bg_ = bank()
                        mm(ps[bg_][:, 0:n], [(wv1[:, kc, jj * 128:(jj + 1) * 128], xn[:, kc, t.sl]) for kc in range(8)],
                           reads=ids('xn', t.idx) + [wid1], writes=[('ps', bg_)])
                        k = sg_rr[0]
                        sg_rr[0] = (k + 1) % NSG
                        p.op('act', I.activation(out=sg[k][:, 0:n], in_=ps[bg_][:, 0:n], func=AF.Tanh, scale=0.5),
                             reads=[('ps', bg_)], writes=[('sg', k)])
                        p.op('dve', I.scalar_tensor_tensor(out=sg2[k][:, 0:n], in0=sg[k][:, 0:n], scalar=1.0,
                                                                                       in1=ps[bv][:, 0:n], op0=ALU.add, op1=ALU.mult),
                             reads=[('ps', bv), ('sg', k)], writes=[('sg2', k)])
                        if t.kind == 'p':
                            p.op('act', I.activation(out=glub[:, c, 30 + t.off:30 + t.off + n], in_=sg2[k][:, 0:n],
                                                                                   func=AF.Copy, scale=0.5),
                                 reads=[('sg2', k)], writes=[('glub', t.idx, c)])
                            if last and t.idx == NPT - 1:
                                p.op('act', I.activation(out=gluf_tail[:, c, :], in_=sg2[k][:, n - 30:n], func=AF.Copy, scale=0.5),
                                     reads=[('sg2', k)], writes=[('gluf_tail', c)])
                        else:
                            p.op('act', I.activation(out=exts[:, c, 30:34, :], in_=sg2[k][:, 0:n].rearrange("p (a b) -> p a b", a=4),
                                                                              func=AF.Copy, scale=0.5),
                                 reads=[('sg2', k)], writes=[('exts', c)])
                            p.op('act', I.activation(out=glus[:, c, :], in_=sg2[k][:, 0:n], func=AF.Copy, scale=0.5),
                                 reads=[('sg2', k)], writes=[('glus', c)])
                w_done(2)
            if stiles:
                for blk in range(4):
                    load_T(sa[blk * 120:(blk + 1) * 120, :], 120,
                           lambda c, src, blk=blk: (exts[:, c, 0:30, blk * 4:(blk + 1) * 4].rearrange("p r s -> p s r"),
                                                    src.rearrange("p (s r) -> p s r", s=4)),
                           [('exts_st', c) for c in range(8)])
            def build_diag(c):
                p.op('dve', I.tensor_tensor(out=diag[c % 2][:, :, :], in0=identb[:, :].unsqueeze(1).broadcast_to([128, 31, 128]),
                                            in1=vT[:, c, R_ACW:R_ACW + 31].unsqueeze(2).broadcast_to([128, 31, 128]), op=ALU.mult),
                     reads=['identb', ('vT', c)], writes=[('diag', c % 2)])

            build_diag(0)
            for c in range(8):
                dc_ = diag[c % 2]
                did = ('diag', c % 2)
                if c + 1 < 8:
                    build_diag(c + 1)
                for t in tiles:
                    n = t.n
                    b = bank()
                    if t.kind == 'p':
                        pairs = [(dc_[:, k, :], glub[:, c, t.off + k:t.off + k + n]) for k in range(31)]
                        rd = [('glub', tt.idx, c) for tt in ptiles if tt.idx <= t.idx] + ['glub_halo']
                    else:
                        pairs = [(dc_[:, k, :], exts[:, c, k:k + 4, :].rearrange("p a b -> p (a b)")) for k in range(31)]
                        rd = [('exts', c), ('exts_st', c)]
                    mm(ps[b][:, 0:n], pairs, reads=rd + [did], writes=[('ps', b)])
                    p.op('act', I.activation(out=f[:, c, t.sl], in_=ps[b][:, 0:n], func=AF.Identity, bias=vrow(c, R_ACB)),
                         reads=[('ps', b), ('vT', c)], writes=[('f', t.idx, c)])
                    p.op('act', I.activation(out=sqb[t.idx][:, c, 0:n], in_=ps[b][:, 0:n], func=AF.Square, bias=vrow(c, R_ACB)),
                         reads=[('ps', b), ('vT', c)], writes=[('sqb', t.idx, c)])
                    p.op('dve', I.tensor_copy(out=cbb[t.idx][:, c, 0:n], in_=f[:, c, t.sl]),
                         reads=[('f', t.idx, c)], writes=[('cbb', t.idx, c)])
            p.op('dve', I.tensor_copy(out=glu_halo[:, :, :], in_=glub[:, :, GP:GP + 30]),
                 reads=[('glub', NPT - 1, c) for c in range(8)], writes=['glu_halo'])
            for t in tiles:
                n = t.n
                ln_stats(t, None)
                for c in range(8):
                    p.op('act', I.activation(out=yb[:, c, t.sl], in_=f[:, c, t.sl], func=AF.Silu,
                                                                 scale=vrow(c, R_ALG), bias=vrow(c, R_ALB)),
                         reads=[('f', t.idx, c), ('vT', c)], writes=[('yb', t.idx, c)])
            if stiles:
                store_T(lambda c: glus[:, c, :], [('glus', c) for c in range(8)], NS,
                        [(tt * 16, 16, o_sa_s.rearrange("(s r) n -> r s n", r=30)[26 + tt]) for tt in range(4)])
            if last:
                store_T(lambda c: gluf_tail[:, c, :], [('gluf_tail', c) for c in range(8)], 30, [(0, 30, o_sa_p[:, :])])
            out_proj(g_, tiles, ('a_out',), lambda m: vrow(m, lrow + 3), after)
            guard(['glub_halo'] + [('glub', ti, c) for ti in range(NPT) for c in range(8)] + [(nm, c) for nm in ('exts', 'exts_st', 'glus') for c in range(8)] + [('diag', 0), ('diag', 1)])

        def mixer_b(g_, tiles, after):
            lrow = R_NG + 1 * 6
            guard(G_IDS)
            ptiles = [t for t in tiles if t.kind == 'p']
            stiles = [t for t in tiles if t.kind == 's']
            last = (g_ == NG - 1)
            p.op('dve', I.tensor_copy(out=zp[:, :, 0:2], in_=z_halo[:, :, :]), reads=['z_halo'], writes=['zp_halo'])
            if stiles:
                load_T(sbi[:, :], 32,
                       lambda c, src: (zs[:, c, 0:2, :].rearrange("p r s -> p s r"), src.rearrange("p (s r) -> p s r", s=16)),
                       [('zs_st', c) for c in range(8)])
            for q in range(2):
                wv3 = w_getn([(g_, 'b_in', q, part) for part in range(3)])
                order = [(jj, t) for t in tiles for jj in range(4)] if q == 0 else [(jj, t) for jj in range(4) for t in tiles]
                for (jj, t) in order:
                    c = q * 4 + jj
                    need(t)
                    pump()
                    if True:
                        n = t.n
                        banks3 = []
                        for part in range(3):
                            b = bank()
                            mm(ps[b][:, 0:n], [(wv3[part][0][:, kc, jj * 128:(jj + 1) * 128], xn[:, kc, t.sl]) for kc in range(8)],
                               reads=ids('xn', t.idx) + [wv3[part][1]], writes=[('ps', b)])
                            banks3.append(b)
                        bB, bC, bX = banks3
                        k = sg_rr[0]
                        sg_rr[0] = (k + 1) % NSG
                        p.op('act', I.activation(out=sg[k][:, 0:n], in_=ps[bX][:, 0:n], func=AF.Copy),
                             reads=[('ps', bX)], writes=[('sg', k)])
                        if t.kind == 'p':
                            zdst = zp[:, c, 2 + t.off:2 + t.off + n]
                            zid = ('zp', t.idx, c)
                        else:
                            zdst = zs[:, c, 2:6, :].rearrange("p a b -> p (a b)")
                            zid = ('zs', c)
                        p.op('dve', I.tensor_tensor(out=zdst, in0=ps[bC][:, 0:n], in1=sg[k][:, 0:n], op=ALU.mult),
                             reads=[('ps', bC), ('sg', k)], writes=[zid])
                        if t.kind == 'p':
                            def zsh(s_, c=c, t=t, n=n):
                                return zp[:, c, t.off + s_:t.off + s_ + n]
                            rd = [('zp', tt.idx, c) for tt in ptiles if tt.idx <= t.idx] + ['zp_halo']
                        else:
                            def zsh(s_, c=c):
                                return zs[:, c, s_:s_ + 4, :].rearrange("p a b -> p (a b)")
                            rd = [('zs', c), ('zs_st', c)]
                        p.op('dve', I.scalar_tensor_tensor(out=sg2[k][:, 0:n], in0=zsh(0), scalar=vrow(c, R_BCW + 0),
                                                           in1=ones_f[:, 0:n], op0=ALU.mult, op1=ALU.mult),
                             reads=rd + [('vT', c), 'ones_f'], writes=[('sg2', k)])
                        for s_ in (1, 2):
                            p.op('dve', I.scalar_tensor_tensor(out=sg2[k][:, 0:n], in0=zsh(s_), scalar=vrow(c, R_BCW + s_),
                                                                                                        in1=sg2[k][:, 0:n], op0=ALU.mult, op1=ALU.add),
                                 reads=rd + [('vT', c), ('sg2', k)], writes=[('sg2', k)])
                        p.op('dve', I.tensor_tensor(out=yb[:, c, t.sl], in0=ps[bB][:, 0:n], in1=sg2[k][:, 0:n], op=ALU.mult),
                             reads=[('ps', bB), ('sg2', k)], writes=[('yb', t.idx, c)])
                w_done(3)
            p.op('dve', I.tensor_copy(out=z_halo[:, :, :], in_=zp[:, :, GP:GP + 2]),
                 reads=[('zp', NPT - 1, c) for c in range(8)], writes=['z_halo'])
            if stiles:
                store_T(lambda c: zs[:, c, 4:6, :].rearrange("p a b -> p (a b)"), [('zs', c) for c in range(8)], 32,
                        [(r * 16, 16, o_sb_s.rearrange("(s r) n -> r s n", r=2)[r]) for r in range(2)])
            if last:
                store_T(lambda c: zp[:, c, GP:GP + 2], [('zp', NPT - 1, c) for c in range(8)], 2, [(0, 2, o_sb_p[:, :])])
            out_proj(g_, tiles, ('b_out',), lambda m: vrow(m, lrow + 3), after)
            guard(['zp_halo'] + [('zp', ti, c) for ti in range(NPT) for c in range(8)] + [(nm, c) for nm in ('zs', 'zs_st') for c in range(8)])

        def mixer_c(g_, tiles, after):
            lrow = R_NG + 2 * 6
            guard(G_IDS)
            stiles = [t for t in tiles if t.kind == 's']
            for q in range(2):
                (wv0, wid0), (wv1, wid1) = w_getn([(g_, 'c_in', q, 0), (g_, 'c_in', q, 1)])
                order = [(jj, t) for t in tiles for jj in range(4)] if q == 0 else [(jj, t) for jj in range(4) for t in tiles]
                for (jj, t) in order:
                    c = q * 4 + jj
                    need(t)
                    pump()
                    if True:
                        n = t.n
                        bu_ = bank()
                        mm(ps[bu_][:, 0:n], [(wv0[:, kc, jj * 128:(jj + 1) * 128], xn[:, kc, t.sl]) for kc in range(8)],
                           reads=ids('xn', t.idx) + [wid0], writes=[('ps', bu_)])
                        bv = bank()
                        mm(ps[bv][:, 0:n], [(wv1[:, kc, jj * 128:(jj + 1) * 128], xn[:, kc, t.sl]) for kc in range(8)],
                           reads=ids('xn', t.idx) + [wid1], writes=[('ps', bv)])
                        p.op('act', I.activation(out=ubuf[:, c, t.sl], in_=ps[bu_][:, 0:n], func=AF.Gelu),
                             reads=[('ps', bu_)], writes=[('u', t.idx, c)])
                        p.op('act', I.activation(out=f[:, c, t.sl], in_=ps[bv][:, 0:n], func=AF.Gelu),
                             reads=[('ps', bv)], writes=[('f', t.idx, c)])
                        p.op('dve', I.tensor_copy(out=cbb[t.idx][:, c, 0:n], in_=f[:, c, t.sl]),
                             reads=[('f', t.idx, c)], writes=[('cbb', t.idx, c)])
                        p.op('dve', I.tensor_tensor(out=sqb[t.idx][:, c, 0:n], in0=f[:, c, t.sl], in1=f[:, c, t.sl], op=ALU.mult),
                             reads=[('f', t.idx, c)], writes=[('sqb', t.idx, c)])
                w_done(2)
            for t in tiles:
                n = t.n
                ln_stats(t, None)
                for c in range(8):
                    p.op('act', I.activation(out=f[:, c, t.sl], in_=f[:, c, t.sl], func=AF.Identity,
                                                                 scale=vrow(c, R_CLG), bias=vrow(c, R_CLB)),
                         reads=[('f', t.idx, c), ('vT', c)], writes=[('f', t.idx, c)])
                if t.kind == 'p':
                    nblk = n // 128
                    for nb_ in range(nblk):
                        vt = vtok[t.idx * nblk + nb_]
                        vid = ('vtok', t.idx * nblk + nb_)
                        for half in range(2):
                            b = bank()
                            fns = [(I.transpose(ps[b][:, k4 * 128:(k4 + 1) * 128],
                                                                                              f[:, half * 4 + k4, t.off + nb_ * 128:t.off + (nb_ + 1) * 128],
                                                                                              ident[:, :])) for k4 in range(4)]
                            p.op('pe', fns, reads=ids('f', t.idx) + ['ident'], writes=[('ps', b)])
                            p.op('act', I.activation(out=vt[:, half * 512:(half + 1) * 512], in_=ps[b][:, :], func=AF.Copy),
                                 reads=[('ps', b)], writes=[(vid, half)])
                    for hh in range(8):
                        b = bank()
                        for nb_ in range(nblk):
                            vt = vtok[t.idx * nblk + nb_]
                            vid = ('vtok', t.idx * nblk + nb_)
                            mm(ps[b][:, nb_ * 128:(nb_ + 1) * 128], [(vt[:, hh * 128:(hh + 1) * 128], wmT[:, hh, :])],
                               reads=[(vid, hh // 4), ('wmT', hh)], writes=[('ps', b)])
                        k = sg_rr[0]
                        sg_rr[0] = (k + 1) % NSG
                        p.op('dve', I.tensor_tensor(
                            out=sg[k][:, 0:n].rearrange("p (a i) -> p a i", a=nblk), in0=ps[b][:, 0:n].rearrange("p (a i) -> p a i", a=nblk),
                            in1=bsb[:, hh, :].unsqueeze(1).broadcast_to([128, nblk, 128]), op=ALU.add),
                            reads=[('ps', b), 'cst'], writes=[('sg', k)])
                        p.op('dve', I.tensor_tensor(out=yb[:, hh, t.sl], in0=sg[k][:, 0:n], in1=ubuf[:, hh, t.sl], op=ALU.mult),
                             reads=[('sg', k), ('u', t.idx, hh)], writes=[('yb', t.idx, hh)])
                else:
                    for hh in range(8):
                        for i in range(4):
                            p.op('dve', I.scalar_tensor_tensor(out=accs[:, i * 16:(i + 1) * 16], in0=f[:, hh, t.off:t.off + 16],
                                                               scalar=wsb[:, hh, i * 4:i * 4 + 1], in1=bs4[:, hh, i:i + 1].broadcast_to([128, 16]),
                                                               op0=ALU.mult, op1=ALU.add),
                                 reads=[('f', t.idx, hh), 'cst'], writes=['accs'])
                            for j in range(1, i + 1):
                                p.op('dve', I.scalar_tensor_tensor(out=accs[:, i * 16:(i + 1) * 16],
                                                                                                   in0=f[:, hh, t.off + j * 16:t.off + (j + 1) * 16],
                                                                                                   scalar=wsb[:, hh, i * 4 + j:i * 4 + j + 1],
                                                                                                   in1=accs[:, i * 16:(i + 1) * 16], op0=ALU.mult, op1=ALU.add),
                                     reads=[('f', t.idx, hh), 'cst', 'accs'], writes=['accs'])
                        p.op('dve', I.tensor_tensor(out=yb[:, hh, t.sl], in0=accs[:, :], in1=ubuf[:, hh, t.sl], op=ALU.mult),
                             reads=['accs', ('u', t.idx, hh)], writes=[('yb', t.idx, hh)])
                    store_T(lambda c, t=t: f[:, c, t.sl], ids('f', t.idx), NS,
                            [(tt * 16, 16, o_sc_s.rearrange("(s r) n -> r s n", r=4)[tt]) for tt in range(4)])
            out_proj(g_, tiles, ('c_out',), lambda m: vrow(m, lrow + 3), after)
            guard([('u', ti, c) for ti in range(NTLX) for c in range(8)] + [(('vtok', i), hf) for i in range(GP // 128) for hf in range(2)] + ['accs'])

        def mixer_d(g_, tiles, after):
            lrow = R_NG + 3 * 6
            flush_all()
            guard(G_IDS)
            ptiles = [t for t in tiles if t.kind == 'p']
            stiles = [t for t in tiles if t.kind == 's']
            last = (g_ == NG - 1)
            p.op('dve', I.tensor_copy(out=xme_p[:, :, 0:15], in_=xm_halo[:, :, :]), reads=['xm_halo'], writes=['xme_halo'])
            for t in tiles:
                if t.kind == 'p':
                    prenorm(t, lrow + 2, lambda c, t=t: xme_p[:, c, 15 + t.off:15 + t.off + t.n], ids('xme', t.idx))
                else:
                    prenorm(t, lrow + 2, lambda c, t=t: xme_s[:, c, 15:19, :].rearrange("p a b -> p (a b)"), ids('xme', t.idx))
            if stiles:
                for blk in range(2):
                    load_T(spi[blk * 120:(blk + 1) * 120, :], 120,
                           lambda c, src, blk=blk: (xme_s[:, c, 0:15, blk * 8:(blk + 1) * 8].rearrange("p r s -> p s r"),
                                                    src.rearrange("p (s r) -> p s r", s=8)),
                           [('xmes_st', c) for c in range(8)])
            wv, wid = w_get((g_, 'd_w', 0))
            for t in tiles:
                n = t.n
                for gi, w in enumerate((2, 4, 8, 16)):
                    L = {2: 1, 4: 2, 8: 3, 16: 4}[w]
                    cs = slice(2 * gi, 2 * gi + 2)
                    if t.kind == 'p':
                        B0 = 15 + t.off

                        def xv(lo, hi, cs=cs, B0=B0):
                            return xme_p[:, cs, B0 + lo:B0 + hi]

                        def tv(i, lo, hi):
                            return ptmp[i][:, :, 15 + lo:15 + hi]
                        rd = [('xme', tt.idx, cc) for tt in ptiles if tt.idx <= t.idx for cc in (2 * gi, 2 * gi + 1)] + ['xme_halo']
                    else:
                        def xv(lo, hi, cs=cs):
                            return xme_s[:, cs, 15 + lo:15 + hi, :]

                        def tv(i, lo, hi):
                            return ptmp[i][:, :, 0:19 * 16].rearrange("p c (r s) -> p c r s", s=16)[:, :, 15 + lo:15 + hi, :]
                        rd = [('xme', t.idx, cc) for cc in (2 * gi, 2 * gi + 1)] + [('xmes_st', cc) for cc in (2 * gi, 2 * gi + 1)]
                    nn = n if t.kind == 'p' else 4
                    prev = None
                    for lv in range(1, L + 1):
                        lo = -(w - 2 ** lv)
                        sh = 2 ** (lv - 1)
                        dst = tv(lv % 2, lo, nn)
                        if lv == 1:
                            a_, b_ = xv(lo, nn), xv(lo - sh, nn - sh)
                            rds = rd
                        else:
                            a_, b_ = tv((lv - 1) % 2, lo, nn), tv((lv - 1) % 2, lo - sh, nn - sh)
                            rds = [('ptmp', (lv - 1) % 2)]
                        p.op('dve', I.tensor_tensor(out=dst, in0=a_, in1=b_, op=ALU.add),
                             reads=rds, writes=[('ptmp', lv % 2)])
                    pooled = tv(L % 2, 0, nn)
                    if t.kind == 'p':
                        dd = yb[:, cs, t.sl]
                    else:
                        dd = yb[:, cs, t.sl].rearrange("p c (r s) -> p c r s", s=16)
                    p.op('dve', I.scalar_tensor_tensor(out=dd, in0=pooled, scalar=1.0 / w, in1=xv(0, nn),
                                                                                                         op0=ALU.mult, op1=ALU.subtract),
                         reads=[('ptmp', L % 2)] + rd, writes=[('yb', t.idx, 2 * gi), ('yb', t.idx, 2 * gi + 1)])
                    if t.kind == 'p' and t.tok0 == 0:
                        icb = invcnt[:, gi, 0:w - 1].unsqueeze(1).broadcast_to([128, 2, w - 1])
                        p.op('dve', I.tensor_tensor(out=tv((L + 1) % 2, 0, w - 1), in0=tv(L % 2, 0, w - 1), in1=icb, op=ALU.mult),
                             reads=[('ptmp', L % 2), 'invcnt'], writes=[('ptmp', (L + 1) % 2)])
                        p.op('dve', I.tensor_tensor(out=yb[:, cs, t.off:t.off + w - 1], in0=tv((L + 1) % 2, 0, w - 1),
                                                                                                 in1=xv(0, w - 1), op=ALU.subtract),
                             reads=[('ptmp', (L + 1) % 2)] + rd, writes=[('yb', t.idx, 2 * gi), ('yb', t.idx, 2 * gi + 1)])
                for gi in range(4):
                    for dc in range(2):
                        m = 2 * gi + dc
                        b = bank()
                        mm(ps[b][:, 0:n], [(wv[:, gi, cc, dc * 128:(dc + 1) * 128], yb[:, 2 * gi + cc, t.sl]) for cc in range(2)],
                           reads=[('yb', t.idx, 2 * gi), ('yb', t.idx, 2 * gi + 1)] + wids(wid), writes=[('ps', b)])
                        p.op('dve', I.scalar_tensor_tensor(out=f[:, m, t.sl], in0=ps[b][:, 0:n], scalar=dg[:, m:m + 1],
                                                           in1=ones_f[:, 0:n], op0=ALU.mult, op1=ALU.mult),
                             reads=[('ps', b), 'dg', 'ones_f'], writes=[('f', t.idx, m)])
                        p.op('act', I.activation(out=sqb[t.idx][:, m, 0:n], in_=ps[b][:, 0:n], func=AF.Square,
                                                                               scale=vrow(m, R_DSC)),
                             reads=[('ps', b), ('vT', m)], writes=[('sqb', t.idx, m)])
            w_done()
            p.op('dve', I.tensor_copy(out=xm_halo[:, :, :], in_=xme_p[:, :, GP:GP + 15]),
                 reads=[('xme', NPT - 1, c) for c in range(8)], writes=['xm_halo'])
            if stiles:
                t = stiles[0]
                for blk in range(2):
                    r0 = 4 + blk * 8 if blk == 0 else 12
                    nr = 8 if blk == 0 else 7
                    store_T(lambda c, r0=r0, nr=nr: xme_s[:, c, r0:r0 + nr, :].rearrange("p a b -> p (a b)"),
                            [('xme', t.idx, c) for c in range(8)], nr * 16,
                            [(rr * 16, 16, o_sp_s.rearrange("(s r) n -> r s n", r=15)[r0 - 4 + rr]) for rr in range(nr)])
            if last:
                store_T(lambda c: xme_p[:, c, GP:GP + 15], [('xme', NPT - 1, c) for c in range(8)], 15, [(0, 15, o_sp_p[:, :])])
            for t in tiles:
                after(t)
            guard(['xme_halo'] + [('xme', ti, c) for ti in range(NTLX) for c in range(8)] + [('xmes_st', c) for c in range(8)] + [('ptmp', 0), ('ptmp', 1)])

        for g_ in range(NGR):
            tiles = [TileT('p', i, i * TS, TS, g_ * GP + i * TS) for i in range(NPT)]
            if g_ == 0:
                tiles.append(TileT('s', NPT, GP, NS, 0))
            for t in tiles:
                if t.kind == 'p':
                    for nb_ in range(t.n // 128):
                        r0 = t.tok0 + nb_ * 128
                        load_T(xp[r0:r0 + 128, :], 128,
                               lambda c, src, t=t, nb_=nb_: (h[:, c, t.off + nb_ * 128:t.off + (nb_ + 1) * 128], src),
                               ids('h', t.idx))
                else:
                    load_T(None, NS, lambda c, src, t=t: (h[:, c, t.sl], src), ids('h', t.idx),
                           src_list=[(tt * 16, 16, xs.rearrange("(s r) n -> r s n", r=4)[tt]) for tt in range(4)])
            assert DBG_LAYERS == 4 and DBG_SUB == ('f0', 'm', 'f1') and DBG_STOP == 99

            def pre_xn(t, row):
                prenorm(t, row, lambda c, t=t: xn[:, c, t.sl], ids('xn', t.idx))

            for t in tiles:
                pre_xn(t, R_NG + 0)
            for l in range(4):
                lrow = R_NG + l * 6

                def after_f0(t, l=l, lrow=lrow):
                    start_chain(t, lrow + 2 if l < 3 else None)

                def after_m(t, lrow=lrow):
                    start_chain(t, lrow + 4)

                def after_f1(t, l=l, lrow=lrow):
                    start_chain(t, lrow + 6 if l < 3 else None)

                ffn(g_, l, 0, tiles, after_f0)
                tap("L%dF0" % l, tiles, g_)
                [mixer_a, mixer_b, mixer_c, mixer_d][l](g_, tiles, after_m)
                tap("L%dM" % l, tiles, g_)
                ffn(g_, l, 1, tiles, after_f1)
                tap("L%dF1" % l, tiles, g_)
            flush_all()
            for t in tiles:
                if t.kind == 'p':
                    for nb_ in range(t.n // 128):
                        r0 = t.tok0 + nb_ * 128
                        store_T(lambda c, t=t, nb_=nb_: h[:, c, t.off + nb_ * 128:t.off + (nb_ + 1) * 128], ids('h', t.idx), 128,
                                [(0, 128, yp[r0:r0 + 128, :])])
                else:
                    store_T(lambda c, t=t: h[:, c, t.sl], ids('h', t.idx), NS,
                            [(tt * 16, 16, ys.rearrange("(s r) n -> r s n", r=4)[tt]) for tt in range(4)])
        assert DBG_STOP < 99 or WS.cur == len(plan), (WS.cur, len(plan))
        p.wait_all('sp', out_toks)
        print('stream sizes', {k: len(v) for k, v in p.streams.items()}, 'counts', p.ecount, 'dma', {str(k): v for k, v in p.dcount.items() if v > 2000})
        p.emit()
    return nc


_NC_CACHE = {}


def kernel(x_prompt, x_sample, state_conv_a, state_conv_b, state_pool,
           norm_g, ffn_w_in, ffn_w_down,
           a_w_in, a_conv_w, a_conv_b, a_ln_g, a_ln_b, a_w_out,
           b_w_in, b_conv_w, b_w_out,
           c_w_in, c_ln_g, c_ln_b, c_ws, c_bs, c_w_out,
           d_w_group, d_scale):
    f32 = lambda a: np.ascontiguousarray(np.asarray(a, dtype=np.float32))
    x_prompt, x_sample = f32(x_prompt), f32(x_sample)
    state_conv_a, state_conv_b, state_pool = f32(state_conv_a), f32(state_conv_b), f32(state_pool)
    vecs = np.concatenate([
        f32(norm_g).reshape(24, D), f32(a_conv_w).reshape(31, D), f32(a_conv_b).reshape(1, D),
        f32(a_ln_g).reshape(1, D), f32(a_ln_b).reshape(1, D), f32(b_conv_w).reshape(3, D),
        f32(c_ln_g).reshape(1, D), f32(c_ln_b).reshape(1, D), f32(d_scale).reshape(1, D)], axis=0)
    assert vecs.shape == (64, D)
    shared = {
        "vecs": f32(vecs), "ffn_w_in": f32(ffn_w_in), "ffn_w_down": f32(ffn_w_down),
        "a_w_in": f32(a_w_in), "a_w_out": f32(a_w_out), "b_w_in": f32(b_w_in), "b_w_out": f32(b_w_out),
        "c_w_in": f32(c_w_in), "c_w_out": f32(c_w_out), "c_ws": f32(c_ws), "c_bs": f32(c_bs),
        "d_w_group": f32(d_w_group),
    }
    in_maps = []
    for i in range(NCORES):
        m = dict(shared)
        m["xp"] = x_prompt[i]
        m["xs"] = x_sample[16 * i:16 * (i + 1)].reshape(NS, D)
        m["sa"] = state_conv_a[16 * i:16 * (i + 1)].reshape(16 * 30, D)
        m["sb"] = state_conv_b[16 * i:16 * (i + 1)].reshape(16 * 2, D)
        m["sp"] = state_pool[16 * i:16 * (i + 1)].reshape(16 * 15, D)
        in_maps.append(m)
    key = (TS, NPT, tuple(sorted(DEBUG_TAPS)) if DEBUG_TAPS else None, DBG_NG, DBG_LAYERS, DBG_SUB, DBG_STOP, DBG_VAR)
    if key not in _NC_CACHE:
        _NC_CACHE[key] = build_program()
    nc = _NC_CACHE[key]
    ncr = NCORES if DBG_CORES is None else DBG_CORES
    res = run_bass_kernel_spmd(nc, in_maps[:ncr], core_ids=list(range(ncr)))
    R = list(res.results)
    while len(R) < NCORES:
        R.append(R[0])

    def cat(name, shape):
        return np.stack([np.asarray(R[i][name], dtype=np.float32).reshape(shape) for i in range(NCORES)], axis=0)
    y_prompt = cat("yp", (SEQ, D))
    y_sample = cat("ys", (16, 4, D)).reshape(128, 4, D)
    sa_p = cat("o_sa_p", (30, D))
    sa_s = cat("o_sa_s", (16, 30, D)).reshape(128, 30, D)
    sb_p = cat("o_sb_p", (2, D))
    sb_s = cat("o_sb_s", (16, 2, D)).reshape(128, 2, D)
    sc_s = cat("o_sc_s", (16, 4, D)).reshape(128, 4, D)
    sp_p = cat("o_sp_p", (15, D))
    sp_s = cat("o_sp_s", (16, 15, D)).reshape(128, 15, D)
    if DEBUG_TAPS:
        kernel.last_taps = {nm: [np.asarray(R[i]["tap_" + nm]) for i in range(NCORES)] for nm in DEBUG_TAPS}
    return (y_prompt, y_sample, sa_p, sa_s, sb_p, sb_s, sc_s, sp_p, sp_s)
```

```python
import contextlib
import numpy as np
import concourse.bass as bass
import concourse.mybir as mybir
from concourse.bass_utils import run_bass_kernel_spmd

F32 = mybir.dt.float32
BF16 = mybir.dt.bfloat16
AF = mybir.ActivationFunctionType
ALU = mybir.AluOpType

SAME_ENGINE_SYNC = True
NCORES = 8
D = 1024
DFF = 2816
KC = 8
NJ = 22
SEQ = 2048
NS = 64
EPS = 1e-6
TS = 512
NPT = 1
GP = TS * NPT
NG = SEQ // GP
NT = GP + NS
RING = 4
SLOT_BYTES = 11264
DEBUG_TAPS = None
DBG_NG = None
DBG_CORES = None
DBG_LAYERS = 4
DBG_STOP = 99
DBG_VAR = ''
DBG_SUB = ('f0', 'm', 'f1')

R_NG = 0
R_ACW = 24
R_ACB = 55
R_ALG = 56
R_ALB = 57
R_BCW = 58
R_CLG = 61
R_CLB = 62
R_DSC = 63


class Prog:
    ENG = ('pe', 'act', 'dve', 'pool', 'sp')

    def __init__(self, nc, stack):
        self.nc = nc
        self.stack = stack
        self.streams = {e: [] for e in self.ENG}
        self.semh = {}
        self.ecount = {e: 0 for e in self.ENG}
        for e in self.ENG:
            self.semh[('e', e)] = stack.enter_context(nc.semaphore("es_" + e))
        self.waited = {e: {} for e in self.ENG}
        self.dcount = {}
        self.lastw = {}
        self.readers = {}

    def _deps(self, reads, writes, extra):
        deps = {}

        def add(t):
            if t is None:
                return
            sk, v = t
            if deps.get(sk, 0) < v:
                deps[sk] = v
        for t in extra:
            add(t)
        for b in reads:
            add(self.lastw.get(b))
        for b in writes:
            add(self.lastw.get(b))
            for sk, v in self.readers.get(b, {}).items():
                add((sk, v))
        return deps

    def _emit_waits(self, eng, deps):
        for sk in sorted(deps, key=str):
            v = deps[sk]
            if sk == ('e', eng) and (eng in ('pe', 'sp') or not SAME_ENGINE_SYNC):
                continue
            if self.waited[eng].get(sk, 0) >= v:
                continue
            self.waited[eng][sk] = v
            sem = self.semh[sk]
            self.streams[eng].append(lambda e, sem=sem, v=v: e.wait_ge(sem, v))

    def _record(self, tok, reads, writes):
        sk, v = tok
        for b in writes:
            self.lastw[b] = tok
            self.readers[b] = {}
        for b in reads:
            d = self.readers.setdefault(b, {})
            if d.get(sk, 0) < v:
                d[sk] = v

    def op(self, eng, fns, reads=(), writes=(), extra=()):
        writes = list(writes) + [b for b in reads if isinstance(b, tuple) and b[0] == 'ps']
        deps = self._deps(reads, writes, extra)
        self._emit_waits(eng, deps)
        if callable(fns):
            fns = [fns]
        self.ecount[eng] += 1
        c = self.ecount[eng]
        sem = self.semh[('e', eng)]
        for f in fns[:-1]:
            self.streams[eng].append(f)
        last = fns[-1]
        self.streams[eng].append(lambda e, last=last, sem=sem: last(e).then_inc(sem, 1))
        tok = (('e', eng), c)
        self._record(tok, reads, writes)
        return tok

    def dma(self, q, out, in_, key, reads=(), writes=(), extra=()):
        deps = self._deps(reads, writes, extra)
        self._emit_waits(q, deps)
        sk = ('d', key)
        if sk not in self.semh:
            self.semh[sk] = self.stack.enter_context(self.nc.semaphore("ds_%d" % len(self.semh)))
            self.dcount[sk] = 0
        self.dcount[sk] += 16
        sem = self.semh[sk]
        self.streams[q].append(lambda e, out=out, in_=in_, sem=sem: e.dma_start(out=out, in_=in_).then_inc(sem, 16))
        tok = (sk, self.dcount[sk])
        self._record(tok, reads, writes)
        return tok

    def dma_multi(self, q, pairs, key, reads=(), writes=()):
        deps = self._deps(reads, writes, ())
        self._emit_waits(q, deps)
        sk = ('d', key)
        if sk not in self.semh:
            self.semh[sk] = self.stack.enter_context(self.nc.semaphore("ds_%d" % len(self.semh)))
            self.dcount[sk] = 0
        sem = self.semh[sk]
        for out, in_ in pairs:
            self.dcount[sk] += 16
            self.streams[q].append(lambda e, out=out, in_=in_, sem=sem: e.dma_start(out=out, in_=in_).then_inc(sem, 16))
        tok = (sk, self.dcount[sk])
        self._record(tok, reads, writes)
        return tok

    def wait_all(self, eng, toks):
        deps = {}
        for sk, v in toks:
            if deps.get(sk, 0) < v:
                deps[sk] = v
        self._emit_waits(eng, deps)

    def barrier(self):
        engs = ('pe', 'act', 'dve', 'pool')
        toks = [(('e', e), self.ecount[e]) for e in engs if self.ecount[e] > 0]
        for e in engs:
            self.wait_all(e, [t for t in toks if t[0] != ('e', e)])

    def emit(self):
        streams = self.streams
        with self.nc.Block() as block:
            @block.tensor
            def _(e):
                for f in streams['pe']:
                    f(e)

            @block.scalar
            def _(e):
                for f in streams['act']:
                    f(e)

            @block.vector
            def _(e):
                for f in streams['dve']:
                    f(e)

            @block.gpsimd
            def _(e):
                for f in streams['pool']:
                    f(e)

            @block.sync
            def _(e):
                for f in streams['sp']:
                    f(e)


class _I:
    def __getattr__(self, name):
        def mk(*args, **kwargs):
            return lambda e: getattr(e, name)(*args, **kwargs)
        return mk


I = _I()


class Arena:
    def __init__(self, big, nbytes):
        self.big = big
        self.nbytes = nbytes
        self.top = 0

    def at(self, off, shape, dt, parts=128):
        n = int(np.prod(shape))
        esz = 4 if dt == F32 else 2
        nb = n * esz
        assert off % 4 == 0 and nb % 4 == 0, (off, nb)
        assert off + nb <= self.nbytes, ("SBUF overflow", off, nb, self.nbytes)
        v = self.big[0:parts, off // 4:(off + nb) // 4]
        if dt != F32:
            v = v.bitcast(dt)
        if len(shape) == 2:
            v = v.rearrange("p (a b) -> p a b", a=shape[0])
        elif len(shape) == 3:
            v = v.rearrange("p (a b c) -> p a b c", a=shape[0], b=shape[1])
        elif len(shape) == 4:
            v = v.rearrange("p (a b c d) -> p a b c d", a=shape[0], b=shape[1], c=shape[2])
        return v

    def alloc(self, shape, dt, parts=128):
        n = int(np.prod(shape))
        esz = 4 if dt == F32 else 2
        nb = (n * esz + 31) // 32 * 32
        off = self.top
        self.top += nb
        return self.at(off, shape, dt, parts), off

    def reserve(self, nbytes):
        off = self.top
        self.top += (nbytes + 31) // 32 * 32
        assert self.top <= self.nbytes, ("SBUF overflow", self.top)
        return off


class TileT:
    def __init__(self, kind, idx, off, n, tok0):
        self.kind = kind
        self.idx = idx
        self.off = off
        self.n = n
        self.tok0 = tok0

    @property
    def sl(self):
        return slice(self.off, self.off + self.n)


def ids(name, t, n=8):
    return [(name, t, c) for c in range(n)]


def build_program():
    nc = bass.Bass("TRN2", target_bir_lowering=False)

    def din(name, shape):
        return nc.dram_tensor(name, shape, F32, kind="ExternalInput").ap()

    def dout(name, shape):
        return nc.dram_tensor(name, shape, F32, kind="ExternalOutput").ap()

    xp = din("xp", [SEQ, D])
    xs = din("xs", [NS, D])
    sa = din("sa", [16 * 30, D])
    sbi = din("sb", [16 * 2, D])
    spi = din("sp", [16 * 15, D])
    vecs = din("vecs", [64, D])
    ffn_w_in = din("ffn_w_in", [4, 2, D, 2 * DFF])
    ffn_w_down = din("ffn_w_down", [4, 2, DFF, D])
    a_w_in = din("a_w_in", [D, 2 * D])
    a_w_out = din("a_w_out", [D, D])
    b_w_in = din("b_w_in", [D, 3 * D])
    b_w_out = din("b_w_out", [D, D])
    c_w_in = din("c_w_in", [D, 2 * D])
    c_w_out = din("c_w_out", [D, D])
    c_ws = din("c_ws", [8, 128, 128])
    c_bs = din("c_bs", [8, 128])
    d_w_group = din("d_w_group", [4, 256, 256])

    yp = dout("yp", [SEQ, D])
    ys = dout("ys", [NS, D])
    o_sa_p = dout("o_sa_p", [30, D])
    o_sa_s = dout("o_sa_s", [16 * 30, D])
    o_sb_p = dout("o_sb_p", [2, D])
    o_sb_s = dout("o_sb_s", [16 * 2, D])
    o_sc_s = dout("o_sc_s", [NS, D])
    o_sp_p = dout("o_sp_p", [15, D])
    o_sp_s = dout("o_sp_s", [16 * 15, D])
    taps = {}
    if DEBUG_TAPS:
        for nm in sorted(DEBUG_TAPS):
            taps[nm] = dout("tap_" + nm, [128, 8, SEQ + NS])

    with contextlib.ExitStack() as st:
        p = Prog(nc, st)
        SB_BYTES = 211968
        big = st.enter_context(nc.sbuf_tensor("big", [128, SB_BYTES // 4], F32))
        A = Arena(big, SB_BYTES)
        ps = [st.enter_context(nc.psum_tensor("ps%d" % i, [128, 512], F32)) for i in range(8)]
        bank_rr = [0]

        def bank():
            b = bank_rr[0]
            bank_rr[0] = (b + 1) % 8
            return b

        ident, _ = A.alloc([128], F32)
        identb, _ = A.alloc([128], BF16)
        ones_b, _ = A.alloc([128], BF16)
        neg_half, _ = A.alloc([TS], F32)
        ones_f, _ = A.alloc([TS], F32)
        vT, _ = A.alloc([8, 64], F32)
        hg, _ = A.alloc([8, 8], F32)
        dg, _ = A.alloc([8], F32)
        wmT, _ = A.alloc([8, 128], BF16)
        bsb, _ = A.alloc([8, 128], F32)
        wsb, _ = A.alloc([8, 16], F32)
        bs4, _ = A.alloc([8, 4], F32)
        invcnt, _ = A.alloc([4, 16], F32)
        eps_t, _ = A.alloc([8], F32)
        dummy, _ = A.alloc([8], F32)
        glu_halo, _ = A.alloc([8, 30], BF16)
        z_halo, _ = A.alloc([8, 2], F32)
        xm_halo, _ = A.alloc([8, 15], F32)
        gluf_tail, _ = A.alloc([8, 30], F32)
        h, _ = A.alloc([8, NT], F32)
        xn, _ = A.alloc([8, NT], BF16)
        f, _ = A.alloc([8, NT], F32)
        yb, _ = A.alloc([8, NT], BF16)
        NTL = NPT + 1
        tsz = [TS] * NPT + [NS]
        sqb = [A.alloc([8, tsz[i]], BF16)[0] for i in range(NTL)]
        cbb = [A.alloc([8, tsz[i]], BF16)[0] for i in range(NTL)]
        st_t = [A.alloc([max(tsz[i], 8)], F32)[0] for i in range(NTL)]
        st_r = [A.alloc([tsz[i]], F32)[0] for i in range(NTL)]
        st_m = [A.alloc([tsz[i]], F32)[0] for i in range(NTL)]
        st_n = [A.alloc([tsz[i]], F32)[0] for i in range(NTL)]
        NSG = 3
        sg = [A.alloc([TS], F32)[0] for _ in range(NSG)]
        sg2 = [A.alloc([TS], F32)[0] for _ in range(NSG)]
        stage_in = [A.alloc([D], F32)[0] for _ in range(2)]
        stage_out = [A.alloc([D], F32)[0] for _ in range(2)]
        ring_off = [A.reserve(SLOT_BYTES) for _ in range(RING)]
        R0 = A.reserve(36 * 1024)
        assert A.top <= SB_BYTES, A.top
        print('SBUF bytes used', A.top, 'of', SB_BYTES)
        gbuf = A.at(R0, [NJ, NT], BF16)
        o = R0
        glub = A.at(o, [8, 30 + GP], BF16); o += 8 * (30 + GP) * 2
        exts = A.at(o, [8, 34, 16], BF16); o += 8 * 34 * 16 * 2
        glus = A.at(o, [8, NS], F32); o += 8 * NS * 4
        diag = []
        for i in range(2):
            diag.append(A.at(o, [31, 128], BF16)); o += 31 * 128 * 2
        assert o <= R0 + 36 * 1024, o - R0
        o = R0
        zp = A.at(o, [8, 2 + GP], F32); o += 8 * (2 + GP) * 4
        zs = A.at(o, [8, 6, 16], F32); o += 8 * 6 * 16 * 4
        assert o <= R0 + 36 * 1024
        o = R0
        ubuf = A.at(o, [8, NT], F32); o += 8 * NT * 4
        vtok = []
        for i in range(GP // 128):
            vtok.append(A.at(o, [D], BF16)); o += D * 2
        accs = A.at(o, [NS], F32); o += NS * 4
        assert o <= R0 + 36 * 1024, o - R0
        o = R0
        xme_p = A.at(o, [8, 15 + GP], F32); o += 8 * (15 + GP) * 4
        xme_s = A.at(o, [8, 19, 16], F32); o += 8 * 19 * 16 * 4
        ptmp = []
        for i in range(2):
            ptmp.append(A.at(o, [2, max(15 + TS, 304)], F32)); o += 2 * max(15 + TS, 304) * 4
        assert o <= R0 + 36 * 1024, o - R0

        def vrow(c, r):
            return vT[:, c, r:r + 1]

        si_rr = [0]
        so_rr = [0]
        out_toks = []

        def load_T(src, R, dst_fn, dst_ids, eng='dve', src_list=None):
            s = si_rr[0]
            si_rr[0] ^= 1
            stg = stage_in[s]
            sid = ('stgin', s)
            if src_list is None:
                src_list = [(0, R, src)]
            for (r0, nr, ap) in src_list:
                p.dma('sp', stg[r0:r0 + nr, :], ap, ('ldin', s, r0), writes=[sid])
            for half in range(2):
                b = bank()
                fns = [(I.transpose(ps[b][:, k4 * R:(k4 + 1) * R],
                                                          stg[0:R, (half * 4 + k4) * 128:(half * 4 + k4 + 1) * 128],
                                                          ident[0:R, 0:R])) for k4 in range(4)]
                p.op('pe', fns, reads=[sid, 'ident'], writes=[('ps', b)])
                for k4 in range(4):
                    c = half * 4 + k4
                    dst, src_view = dst_fn(c, ps[b][:, k4 * R:(k4 + 1) * R])
                    p.op(eng, I.tensor_copy(out=dst, in_=src_view),
                         reads=[('ps', b)], writes=[dst_ids[c]])

        def store_T(src_fn, src_ids, R, dsts, eng='act'):
            s = so_rr[0]
            so_rr[0] ^= 1
            stg = stage_out[s]
            sid = ('stgout', s)
            for half in range(2):
                b = bank()
                fns = [(I.transpose(ps[b][0:R, k4 * 128:(k4 + 1) * 128],
                                                          src_fn(half * 4 + k4), ident[:, :])) for k4 in range(4)]
                p.op('pe', fns, reads=[src_ids[half * 4 + k4] for k4 in range(4)] + ['ident'], writes=[('ps', b)])
                if eng == 'act':
                    p.op('act', I.activation(out=stg[0:R, half * 512:(half + 1) * 512],
                                                                       in_=ps[b][0:R, :], func=AF.Copy),
                         reads=[('ps', b)], writes=[sid])
                else:
                    p.op('dve', I.tensor_copy(out=stg[0:R, half * 512:(half + 1) * 512],
                                                                        in_=ps[b][0:R, :]),
                         reads=[('ps', b)], writes=[sid])
            for (r0, nr, ap) in dsts:
                out_toks.append(p.dma('sp', ap, stg[r0:r0 + nr, :], ('st', s, r0), reads=[sid]))

        def mm(bank_ap, pairs, reads, writes):
            n = len(pairs)
            fns = [(I.matmul(bank_ap, lhsT=l, rhs=r, start=(i == 0), stop=(i == n - 1)))
                   for i, (l, r) in enumerate(pairs)]
            return p.op('pe', fns, reads=reads, writes=writes)

        def wplan_group():
            plan = []

            def cols(w2d, c0, width):
                return w2d.rearrange("(kc p) n -> p kc n", p=128)[:, :, c0:c0 + width]

            for l in range(DBG_LAYERS):
                for s_ in range(2):
                    if s_ == 1 and 'm' in DBG_SUB:
                        if l == 0:
                            for q in range(2):
                                plan.append((('a_in', q, 0), cols(a_w_in, q * 512, 512), [8, 512]))
                                plan.append((('a_in', q, 1), cols(a_w_in, D + q * 512, 512), [8, 512]))
                            for q in range(2):
                                plan.append((('a_out', q), cols(a_w_out, q * 512, 512), [8, 512]))
                        elif l == 1:
                            for q in range(2):
                                for part in range(3):
                                    plan.append((('b_in', q, part), cols(b_w_in, part * D + q * 512, 512), [8, 512]))
                            for q in range(2):
                                plan.append((('b_out', q), cols(b_w_out, q * 512, 512), [8, 512]))
                        elif l == 2:
                            for q in range(2):
                                plan.append((('c_in', q, 0), cols(c_w_in, q * 512, 512), [8, 512]))
                                plan.append((('c_in', q, 1), cols(c_w_in, D + q * 512, 512), [8, 512]))
                            for q in range(2):
                                plan.append((('c_out', q), cols(c_w_out, q * 512, 512), [8, 512]))
                        else:
                            plan.append((('d_w', 0), d_w_group.rearrange("g (cc p) n -> p g cc n", p=128), [4, 2, 256]))
                    if ('f%d' % s_) not in DBG_SUB:
                        continue
                    for q in range(6):
                        width = 512 if q < 5 else 256
                        plan.append((('f_in', l, s_, q, 0), cols(ffn_w_in[l, s_], q * 512, width), [8, width]))
                        plan.append((('f_in', l, s_, q, 1), cols(ffn_w_in[l, s_], DFF + q * 512, width), [8, width]))
                    for q in range(2):
                        for kh in range(2):
                            plan.append((('f_dn', l, s_, q, kh),
                                         ffn_w_down[l, s_].rearrange("(kc p) n -> p kc n", p=128)[:, kh * 11:(kh + 1) * 11, q * 512:(q + 1) * 512], [11, 512]))
            return plan

        plan = []
        NGR = NG if DBG_NG is None else DBG_NG
        for g_ in range(NGR):
            plan += [((g_,) + k, ap, shp) for (k, ap, shp) in wplan_group()]

        NBLK = len(wplan_group())
        SLOT_ELEMS = SLOT_BYTES // 2
        wscr = None
        if NGR > 1:
            wscr = nc.dram_tensor("wscr", [NBLK, 128, SLOT_ELEMS], BF16, kind="Internal").ap()

        class WS:
            nxt = 0
            cur = 0
            views = {}

        def w_issue():
            k = WS.nxt
            key, src, shp = plan[k]
            s = k % RING
            kk = k % NBLK
            nel = int(np.prod(shp))
            view = A.at(ring_off[s], shp, BF16)
            flat = A.at(ring_off[s], [nel], BF16)
            if key[0] == 0:
                if shp[0] > 8:
                    pairs = [(view[:, k0:min(k0 + 8, shp[0]), :], src[:, k0:min(k0 + 8, shp[0]), :]) for k0 in range(0, shp[0], 8)]
                else:
                    pairs = [(view, src)]
                p.dma_multi('pool', pairs, ('w', s), writes=[('wslot', s)])
                if wscr is not None:
                    p.dma('sp', wscr[kk][:, 0:nel], flat, ('wb', s), reads=[('wslot', s)], writes=[('wscr', kk)])
            else:
                p.dma('sp', flat, wscr[kk][:, 0:nel], ('wh', s), reads=[('wscr', kk)], writes=[('wslot', s)])
            WS.views[k] = view
            WS.nxt += 1

        def w_getn(keys):
            out = []
            for i, key in enumerate(keys):
                k = WS.cur + i
                assert plan[k][0] == key, (plan[k][0], key)
                while WS.nxt <= k:
                    w_issue()
                out.append((WS.views[k], ('wslot', k % RING)))
            return out

        def w_get(key):
            return w_getn([key])[0]

        def wids(wid):
            return [wid]

        def w_done(n=1):
            for _ in range(n):
                WS.cur += 1
            while WS.nxt < len(plan) and WS.nxt < WS.cur + RING:
                w_issue()

        p.op('pool', I.memset(ident[:, :], 0.0), writes=['ident'])
        p.op('pool', I.affine_select(out=ident[:, :], in_=ident[:, :], compare_op=ALU.not_equal, fill=1.0,
                                     base=0, pattern=[[-1, 128]], channel_multiplier=1),
             reads=['ident'], writes=['ident'])
        p.op('dve', I.tensor_copy(out=identb[:, :], in_=ident[:, :]), reads=['ident'], writes=['identb'])
        p.op('dve', I.memset(ones_b[:, :], 1.0 / 1024.0), writes=['ones'])
        p.op('dve', I.memset(neg_half[:, :], -0.5), writes=['nh'])
        p.op('dve', I.memset(ones_f[:, :], 1.0), writes=['ones_f'])
        p.op('dve', I.memset(eps_t[:, :], EPS), writes=['eps_t'])
        p.op('dve', I.memset(glu_halo[:, :, :], 0.0), writes=['glu_halo'])
        p.op('dve', I.memset(z_halo[:, :, :], 0.0), writes=['z_halo'])
        p.op('dve', I.memset(xm_halo[:, :, :], 0.0), writes=['xm_halo'])
        fns = []
        for gi, w in enumerate((2, 4, 8, 16)):
            for t in range(16):
                fns.append(I.memset(invcnt[:, gi, t:t + 1], 1.0 / min(w, t + 1)))
        p.op('dve', fns, writes=['invcnt'])
        for i in range(min(RING, len(plan))):
            w_issue()
        vids = [('vT', c) for c in range(8)]
        load_T(vecs[:, :], 64, lambda c, src: (vT[:, c, :], src), vids)
        for l in range(4):
            for wi, i in enumerate((1, 5)):
                p.op('dve', I.tensor_scalar(out=hg[:, :, l * 2 + wi], in0=vT[:, :, R_NG + l * 6 + i],
                                                                      scalar1=0.5, scalar2=None, op0=ALU.mult),
                     reads=vids, writes=['hg'])
        p.op('dve', I.tensor_tensor(out=dg[:, :], in0=vT[:, :, R_DSC], in1=vT[:, :, R_NG + 3 * 6 + 3], op=ALU.mult),
             reads=vids, writes=['dg'])
        for hh in range(8):
            s = si_rr[0]
            si_rr[0] ^= 1
            stg = stage_in[s]
            sid = ('stgin', s)
            p.dma('sp', stg[:, 0:128], c_ws[hh], ('ldin', s, 0), writes=[sid])
            b = bank()
            p.op('pe', I.transpose(ps[b][:, 0:128], stg[:, 0:128], ident[:, :]),
                 reads=[sid, 'ident'], writes=[('ps', b)])
            p.op('dve', I.tensor_copy(out=wmT[:, hh, :], in_=ps[b][:, 0:128]),
                 reads=[('ps', b)], writes=[('wmT', hh)])
            p.op('pool', I.affine_select(out=wmT[:, hh, :], in_=wmT[:, hh, :], compare_op=ALU.is_ge, fill=0.0,
                                                          base=0, pattern=[[1, 128]], channel_multiplier=-1),
                 reads=[('wmT', hh)], writes=[('wmT', hh)])
        with nc.allow_non_contiguous_dma(reason="tiny broadcast loads"):
            cpairs = []
            for hh in range(8):
                cpairs.append((bsb[:, hh, :], c_bs[hh:hh + 1, :].partition_broadcast(128)))
                cpairs.append((bs4[:, hh, :], c_bs[hh:hh + 1, 0:4].partition_broadcast(128)))
                for i in range(4):
                    cpairs.append((wsb[:, hh, i * 4:(i + 1) * 4], c_ws[hh, i:i + 1, 0:4].partition_broadcast(128)))
            p.dma_multi('sp', cpairs, 'cst', writes=['cst'])
        out_toks.append(p.dma('sp', o_sa_s.rearrange("(s r) n -> s r n", r=30)[:, 0:26, :],
                              sa.rearrange("(s r) n -> s r n", r=30)[:, 4:30, :], 'd2d'))

        def tap(name, tiles, g_):
            if not DEBUG_TAPS or name not in DEBUG_TAPS:
                return
            flush_all()
            for t in tiles:
                col = g_ * GP + t.off if t.kind == 'p' else SEQ
                out_toks.append(p.dma('sp', taps[name][:, :, col:col + t.n], h[:, :, t.sl], ('tap', name, t.idx), reads=ids('h', t.idx)))

        NTLX = NPT + 1
        G_IDS = [('g', ti, j) for ti in range(NTLX) for j in range(NJ)]

        def guard(write_ids):
            tok = p.op('dve', I.memset(dummy[:, 0:1], 0.0), writes=list(write_ids) + ['dummy'])
            p.wait_all('act', [tok])
            p.wait_all('pool', [tok])

        def tok_sums(t, srcs, reads):
            n = t.n
            b = bank()
            fns = []
            for i, src in enumerate(srcs):
                for blk in range((n + 127) // 128):
                    P = min(128, n - blk * 128)
                    for c in range(8):
                        fns.append(I.matmul(ps[b][0:P, i * 4 + blk:i * 4 + blk + 1], lhsT=src[:, c, blk * 128:blk * 128 + P],
                                            rhs=ones_b[:, 0:1], start=(c == 0), stop=(c == 7)))
            p.op('pe', fns, reads=reads + ['ones'], writes=[('ps', b)])
            return b

        def bcast_tok(t, col, dst, dst_id):
            n = t.n
            nblk = (n + 127) // 128
            for blk in range(nblk):
                P = min(128, n - blk * 128)
                p.op('dve', I.scalar_tensor_tensor(out=st_m[t.idx][0:P, blk * 128:blk * 128 + P], in0=ident[0:P, 0:P],
                                                   scalar=st_t[t.idx][0:P, col + blk:col + blk + 1], in1=ones_f[0:P, 0:P],
                                                   op0=ALU.mult, op1=ALU.mult),
                     reads=[('st_t', t.idx), 'ident', 'ones_f'], writes=[('st_m', t.idx, blk)])
            b2 = bank()
            fns = []
            for blk in range(nblk):
                P = min(128, n - blk * 128)
                fns.append(I.matmul(ps[b2][:, blk * 128:blk * 128 + P], lhsT=ones_f[0:P, 0:128],
                                    rhs=st_m[t.idx][0:P, blk * 128:blk * 128 + P], start=True, stop=True))
            p.op('pe', fns, reads=[('st_m', t.idx, blk) for blk in range(nblk)] + ['ones_f'], writes=[('ps', b2)])
            if dst is None:
                return ps[b2][:, 0:n], ('ps', b2)
            p.op('act', I.activation(out=dst, in_=ps[b2][:, 0:n], func=AF.Copy), reads=[('ps', b2)], writes=[dst_id])
            return dst, dst_id

        def stats_rstd_gen(t, sq_src, sq_ids):
            n = t.n
            nblk = (n + 127) // 128
            P = min(128, n)
            b = tok_sums(t, [sq_src], sq_ids)
            p.op('dve', I.tensor_scalar(out=st_t[t.idx][0:P, 0:nblk], in0=ps[b][0:P, 0:nblk], scalar1=EPS, scalar2=None, op0=ALU.add),
                 reads=[('ps', b)], writes=[('st_t', t.idx)])
            p.op('pool', I.tensor_tensor(out=st_t[t.idx][0:P, 0:nblk], in0=st_t[t.idx][0:P, 0:nblk], in1=neg_half[0:P, 0:nblk], op=ALU.pow),
                 reads=[('st_t', t.idx), 'nh'], writes=[('st_t', t.idx)])
            yield
            yield
            RST[t.idx] = bcast_tok(t, 0, None, None)

        RST = {}

        def prenorm_gen(t, grow, dst_fn, dst_ids):
            n = t.n
            for hf in range(2):
                cs = slice(4 * hf, 4 * hf + 4)
                p.op('act', I.activation(out=sqb[t.idx][:, cs, 0:n], in_=h[:, cs, t.sl], func=AF.Square),
                     reads=ids('h', t.idx)[cs], writes=ids('sqb', t.idx)[cs])
            yield
            yield
            yield from stats_rstd_gen(t, sqb[t.idx], ids('sqb', t.idx))
            yield
            for c in range(8):
                rap, rid = RST[t.idx]
                p.op('dve', I.scalar_tensor_tensor(out=dst_fn(c), in0=h[:, c, t.sl], scalar=vrow(c, grow),
                                                   in1=rap, op0=ALU.mult, op1=ALU.mult),
                     reads=[('h', t.idx, c), rid, ('vT', c)], writes=[dst_ids[c]])

        def prenorm(t, grow, dst_fn, dst_ids):
            for _ in prenorm_gen(t, grow, dst_fn, dst_ids):
                pass

        def postnorm_gen(t):
            n = t.n
            yield
            yield from stats_rstd_gen(t, sqb[t.idx], ids('sqb', t.idx))
            yield
            rap, rid = RST[t.idx]
            rb = rap.unsqueeze(1).broadcast_to([128, 4, n])
            for hf in range(2):
                cs = slice(4 * hf, 4 * hf + 4)
                p.op('dve', I.tensor_tensor(out=f[:, cs, t.sl], in0=f[:, cs, t.sl], in1=rb, op=ALU.mult),
                     reads=ids('f', t.idx)[cs] + [rid], writes=ids('f', t.idx)[cs])
                p.op('dve', I.tensor_tensor(out=h[:, cs, t.sl], in0=h[:, cs, t.sl], in1=f[:, cs, t.sl], op=ALU.add),
                     reads=ids('f', t.idx)[cs] + ids('h', t.idx)[cs], writes=ids('h', t.idx)[cs])

        def postnorm_residual(t):
            for _ in postnorm_gen(t):
                pass

        pending = {}

        def chain_gen(t, pre_row):
            yield from postnorm_gen(t)
            if pre_row is not None:
                yield from prenorm_gen(t, pre_row, lambda c, t=t: xn[:, c, t.sl], ids('xn', t.idx))

        def start_chain(t, pre_row):
            need(t)
            pending[t.idx] = chain_gen(t, pre_row)

        def pump():
            for k in list(pending):
                try:
                    next(pending[k])
                except StopIteration:
                    del pending[k]

        def need(t):
            g = pending.pop(t.idx, None)
            if g is not None:
                for _ in g:
                    pass

        def flush_all():
            for k in list(pending):
                g = pending.pop(k)
                for _ in g:
                    pass

        def out_proj(g_, tiles, wkey, gain_fn, after):
            for q in range(2):
                wv, wid = w_get((g_,) + wkey + (q,))
                order = [(mmi, t) for mmi in range(4) for t in tiles] if q == 0 else [(mmi, t) for t in tiles for mmi in range(4)]
                for (mmi, t) in order:
                    m = q * 4 + mmi
                    b = bank()
                    mm(ps[b][:, 0:t.n], [(wv[:, kc, mmi * 128:(mmi + 1) * 128], yb[:, kc, t.sl]) for kc in range(8)],
                       reads=ids('yb', t.idx) + wids(wid), writes=[('ps', b)])
                    pump()
                    p.op('dve', I.scalar_tensor_tensor(out=f[:, m, t.sl], in0=ps[b][:, 0:t.n], scalar=gain_fn(m),
                                                       in1=ones_f[:, 0:t.n], op0=ALU.mult, op1=ALU.mult),
                         reads=[('ps', b), 'hg', 'dg', 'ones_f'] + vids, writes=[('f', t.idx, m)])
                    p.op('act', I.activation(out=sqb[t.idx][:, m, 0:t.n], in_=ps[b][:, 0:t.n], func=AF.Square),
                         reads=[('ps', b)], writes=[('sqb', t.idx, m)])
                    if q == 1 and mmi == 3:
                        after(t)
                w_done()

        sg_rr = [0]

        def ffn(g_, l, s_, tiles, after):
            for q in range(6):
                (gv, gid), (uv, uid) = w_getn([(g_, 'f_in', l, s_, q, 0), (g_, 'f_in', l, s_, q, 1)])
                njj = 4 if q < 5 else 2
                order = [(jj, t) for t in tiles for jj in range(njj)] if q == 0 else [(jj, t) for jj in range(njj) for t in tiles]
                for (jj, t) in order:
                    j = q * 4 + jj
                    n = t.n
                    need(t)
                    bg_ = bank()
                    mm(ps[bg_][:, 0:n], [(gv[:, kc, jj * 128:(jj + 1) * 128], xn[:, kc, t.sl]) for kc in range(8)],
                       reads=ids('xn', t.idx) + [gid], writes=[('ps', bg_)])
                    bu_ = bank()
                    mm(ps[bu_][:, 0:n], [(uv[:, kc, jj * 128:(jj + 1) * 128], xn[:, kc, t.sl]) for kc in range(8)],
                       reads=ids('xn', t.idx) + [uid], writes=[('ps', bu_)])
                    pump()
                    k = sg_rr[0]
                    sg_rr[0] = (k + 1) % NSG
                    p.op('act', I.activation(out=sg[k][:, 0:n], in_=ps[bg_][:, 0:n], func=AF.Silu),
                         reads=[('ps', bg_)], writes=[('sg', k)])
                    p.op('dve', I.tensor_tensor(out=gbuf[:, j, t.sl], in0=ps[bu_][:, 0:n], in1=sg[k][:, 0:n], op=ALU.mult),
                         reads=[('ps', bu_), ('sg', k)], writes=[('g', t.idx, j)])
                w_done(2)
            gi = l * 2 + s_
            for q in range(2):
                wvs = w_getn([(g_, 'f_dn', l, s_, q, 0), (g_, 'f_dn', l, s_, q, 1)])
                order = [(mmi, t) for mmi in range(4) for t in tiles] if q == 0 else [(mmi, t) for t in tiles for mmi in range(4)]
                for (mmi, t) in order:
                    m = q * 4 + mmi
                    n = t.n
                    b = bank()
                    mm(ps[b][:, 0:n], [(wvs[j // 11][0][:, j % 11, mmi * 128:(mmi + 1) * 128], gbuf[:, j, t.sl]) for j in range(NJ)],
                       reads=ids('g', t.idx, NJ) + [wvs[0][1], wvs[1][1]], writes=[('ps', b)])
                    pump()
                    p.op('dve', I.scalar_tensor_tensor(out=f[:, m, t.sl], in0=ps[b][:, 0:n], scalar=hg[:, m, gi:gi + 1],
                                                       in1=ones_f[:, 0:n], op0=ALU.mult, op1=ALU.mult),
                         reads=[('ps', b), 'hg', 'ones_f'], writes=[('f', t.idx, m)])
                    p.op('act', I.activation(out=sqb[t.idx][:, m, 0:n], in_=ps[b][:, 0:n], func=AF.Square),
                         reads=[('ps', b)], writes=[('sqb', t.idx, m)])
                    if q == 1 and mmi == 3:
                        after(t)
                w_done(2)

        def ln_stats(t, src_f_ids):
            n = t.n
            nblk = (n + 127) // 128
            P = min(128, n)
            S = st_t[t.idx]
            b = tok_sums(t, [cbb[t.idx], sqb[t.idx]], ids('cbb', t.idx) + ids('sqb', t.idx))
            p.op('dve', I.tensor_copy(out=S[0:P, 8:8 + nblk], in_=ps[b][0:P, 0:nblk]), reads=[('ps', b)], writes=[('st_t', t.idx)])
            p.op('dve', I.tensor_tensor(out=S[0:P, 16:16 + nblk], in0=S[0:P, 8:8 + nblk], in1=S[0:P, 8:8 + nblk], op=ALU.mult),
                 reads=[('st_t', t.idx)], writes=[('st_t', t.idx)])
            p.op('dve', I.scalar_tensor_tensor(out=S[0:P, 0:nblk], in0=ps[b][0:P, 4:4 + nblk], scalar=EPS, in1=S[0:P, 16:16 + nblk],
                                               op0=ALU.add, op1=ALU.subtract),
                 reads=[('ps', b), ('st_t', t.idx)], writes=[('st_t', t.idx)])
            p.op('pool', I.tensor_tensor(out=S[0:P, 0:nblk], in0=S[0:P, 0:nblk], in1=neg_half[0:P, 0:nblk], op=ALU.pow),
                 reads=[('st_t', t.idx), 'nh'], writes=[('st_t', t.idx)])
            p.op('dve', I.scalar_tensor_tensor(out=S[0:P, 16:16 + nblk], in0=S[0:P, 8:8 + nblk], scalar=-1.0, in1=S[0:P, 0:nblk],
                                               op0=ALU.mult, op1=ALU.mult),
                 reads=[('st_t', t.idx)], writes=[('st_t', t.idx)])
            bcast_tok(t, 0, st_r[t.idx][:, 0:n], ('st_r', t.idx))
            bcast_tok(t, 16, st_n[t.idx][:, 0:n], ('st_n', t.idx))
            rb = st_r[t.idx][:, 0:n].unsqueeze(1).broadcast_to([128, 8, n])
            nb_ = st_n[t.idx][:, 0:n].unsqueeze(1).broadcast_to([128, 8, n])
            p.op('dve', I.tensor_tensor(out=f[:, :, t.sl], in0=f[:, :, t.sl], in1=rb, op=ALU.mult),
                 reads=ids('f', t.idx) + [('st_r', t.idx)], writes=ids('f', t.idx))
            p.op('dve', I.tensor_tensor(out=f[:, :, t.sl], in0=f[:, :, t.sl], in1=nb_, op=ALU.add),
                 reads=ids('f', t.idx) + [('st_n', t.idx)], writes=ids('f', t.idx))

        def mixer_a(g_, tiles, after):
            lrow = R_NG + 0 * 6
            guard(G_IDS)
            ptiles = [t for t in tiles if t.kind == 'p']
            stiles = [t for t in tiles if t.kind == 's']
            last = (g_ == NG - 1)
            p.op('dve', I.tensor_copy(out=glub[:, :, 0:30], in_=glu_halo[:, :, :]), reads=['glu_halo'], writes=['glub_halo'])
            for q in range(2):
                (wv0, wid0), (wv1, wid1) = w_getn([(g_, 'a_in', q, 0), (g_, 'a_in', q, 1)])
                order = [(jj, t) for t in tiles for jj in range(4)] if q == 0 else [(jj, t) for jj in range(4) for t in tiles]
                for (jj, t) in order:
                    c = q * 4 + jj
                    need(t)
                    pump()
                    if True:
                        n = t.n
                        bv = bank()
                        mm(ps[bv][:, 0:n], [(wv0[:, kc, jj * 128:(jj + 1) * 128], xn[:, kc, t.sl]) for kc in range(8)],
                           reads=ids('xn', t.idx) + [wid0], writes=[('ps', bv)])
                        bg_ = bank()
                        mm(ps[bg_][:, 0:n], [(wv1[:, kc, jj * 128:(jj + 1) * 128], xn[:, kc, t.sl]) for kc in range(8)],
                           reads=ids('xn', t.idx) + [wid1], writes=[('ps', bg_)])
                        k = sg_rr[0]
                        sg_rr[0] = (k + 1) % NSG
                        p.op('act', I.activation(out=sg[k][:, 0:n], in_=ps[bg_][:, 0:n], func=AF.Tanh, scale=0.5),
                             reads=[('ps', bg_)], writes=[('sg', k)])
                        p.op('dve', I.scalar_tensor_tensor(out=sg2[k][:, 0:n], in0=sg[k][:, 0:n], scalar=1.0,
                                                                                       in1=ps[bv][:, 0:n], op0=ALU.add, op1=ALU.mult),
                             reads=[('ps', bv), ('sg', k)], writes=[('sg2', k)])
                        if t.kind == 'p':
                            p.op('act', I.activation(out=glub[:, c, 30 + t.off:30 + t.off + n], in_=sg2[k][:, 0:n],
                                                                                   func=AF.Copy, scale=0.5),
                                 reads=[('sg2', k)], writes=[('glub', t.idx, c)])
                            if last and t.idx == NPT - 1:
                                p.op('act', I.activation(out=gluf_tail[:, c, :], in_=sg2[k][:, n - 30:n], func=AF.Copy, scale=0.5),
                                     reads=[('sg2', k)], writes=[('gluf_tail', c)])
                        else:
                            p.op('act', I.activation(out=exts[:, c, 30:34, :], in_=sg2[k][:, 0:n].rearrange("p (a b) -> p a b", a=4),
                                                                              func=AF.Copy, scale=0.5),
                                 reads=[('sg2', k)], writes=[('exts', c)])
                            p.op('act', I.activation(out=glus[:, c, :], in_=sg2[k][:, 0:n], func=AF.Copy, scale=0.5),
                                 reads=[('sg2', k)], writes=[('glus', c)])
                w_done(2)
            if stiles:
                for blk in range(4):
                    load_T(sa[blk * 120:(blk + 1) * 120, :], 120,
                           lambda c, src, blk=blk: (exts[:, c, 0:30, blk * 4:(blk + 1) * 4].rearrange("p r s -> p s r"),
                                                    src.rearrange("p (s r) -> p s r", s=4)),
                           [('exts_st', c) for c in range(8)])
            def build_diag(c):
                p.op('dve', I.tensor_tensor(out=diag[c % 2][:, :, :], in0=identb[:, :].unsqueeze(1).broadcast_to([128, 31, 128]),
                                            in1=vT[:, c, R_ACW:R_ACW + 31].unsqueeze(2).broadcast_to([128, 31, 128]), op=ALU.mult),
                     reads=['identb', ('vT', c)], writes=[('diag', c % 2)])

            build_diag(0)
            for c in range(8):
                dc_ = diag[c % 2]
                did = ('diag', c % 2)
                if c + 1 < 8:
                    build_diag(c + 1)
                for t in tiles:
                    n = t.n
                    b = bank()
                    if t.kind == 'p':
                        pairs = [(dc_[:, k, :], glub[:, c, t.off + k:t.off + k + n]) for k in range(31)]
                        rd = [('glub', tt.idx, c) for tt in ptiles if tt.idx <= t.idx] + ['glub_halo']
                    else:
                        pairs = [(dc_[:, k, :], exts[:, c, k:k + 4, :].rearrange("p a b -> p (a b)")) for k in range(31)]
                        rd = [('exts', c), ('exts_st', c)]
                    mm(ps[b][:, 0:n], pairs, reads=rd + [did], writes=[('ps', b)])
                    p.op('act', I.activation(out=f[:, c, t.sl], in_=ps[b][:, 0:n], func=AF.Identity, bias=vrow(c, R_ACB)),
                         reads=[('ps', b), ('vT', c)], writes=[('f', t.idx, c)])
                    p.op('act', I.activation(out=sqb[t.idx][:, c, 0:n], in_=ps[b][:, 0:n], func=AF.Square, bias=vrow(c, R_ACB)),
                         reads=[('ps', b), ('vT', c)], writes=[('sqb', t.idx, c)])
                    p.op('dve', I.tensor_copy(out=cbb[t.idx][:, c, 0:n], in_=f[:, c, t.sl]),
                         reads=[('f', t.idx, c)], writes=[('cbb', t.idx, c)])
            p.op('dve', I.tensor_copy(out=glu_halo[:, :, :], in_=glub[:, :, GP:GP + 30]),
                 reads=[('glub', NPT - 1, c) for c in range(8)], writes=['glu_halo'])
            for t in tiles:
                n = t.n
                ln_stats(t, None)
                for c in range(8):
                    p.op('act', I.activation(out=yb[:, c, t.sl], in_=f[:, c, t.sl], func=AF.Silu,
                                                                 scale=vrow(c, R_ALG), bias=vrow(c, R_ALB)),
                         reads=[('f', t.idx, c), ('vT', c)], writes=[('yb', t.idx, c)])
            if stiles:
                store_T(lambda c: glus[:, c, :], [('glus', c) for c in range(8)], NS,
                        [(tt * 16, 16, o_sa_s.rearrange("(s r) n -> r s n", r=30)[26 + tt]) for tt in range(4)])
            if last:
                store_T(lambda c: gluf_tail[:, c, :], [('gluf_tail', c) for c in range(8)], 30, [(0, 30, o_sa_p[:, :])])
            out_proj(g_, tiles, ('a_out',), lambda m: vrow(m, lrow + 3), after)
            guard(['glub_halo'] + [('glub', ti, c) for ti in range(NPT) for c in range(8)] + [(nm, c) for nm in ('exts', 'exts_st', 'glus') for c in range(8)] + [('diag', 0), ('diag', 1)])

        def mixer_b(g_, tiles, after):
            lrow = R_NG + 1 * 6
            guard(G_IDS)
            ptiles = [t for t in tiles if t.kind == 'p']
            stiles = [t for t in tiles if t.kind == 's']
            last = (g_ == NG - 1)
            p.op('dve', I.tensor_copy(out=zp[:, :, 0:2], in_=z_halo[:, :, :]), reads=['z_halo'], writes=['zp_halo'])
            if stiles:
                load_T(sbi[:, :], 32,
                       lambda c, src: (zs[:, c, 0:2, :].rearrange("p r s -> p s r"), src.rearrange("p (s r) -> p s r", s=16)),
                       [('zs_st', c) for c in range(8)])
            for q in range(2):
                wv3 = w_getn([(g_, 'b_in', q, part) for part in range(3)])
                order = [(jj, t) for t in tiles for jj in range(4)] if q == 0 else [(jj, t) for jj in range(4) for t in tiles]
                for (jj, t) in order:
                    c = q * 4 + jj
                    need(t)
                    pump()
                    if True:
                        n = t.n
                        banks3 = []
                        for part in range(3):
                            b = bank()
                            mm(ps[b][:, 0:n], [(wv3[part][0][:, kc, jj * 128:(jj + 1) * 128], xn[:, kc, t.sl]) for kc in range(8)],
                               reads=ids('xn', t.idx) + [wv3[part][1]], writes=[('ps', b)])
                            banks3.append(b)
                        bB, bC, bX = banks3
                        k = sg_rr[0]
                        sg_rr[0] = (k + 1) % NSG
                        p.op('act', I.activation(out=sg[k][:, 0:n], in_=ps[bX][:, 0:n], func=AF.Copy),
                             reads=[('ps', bX)], writes=[('sg', k)])
                        if t.kind == 'p':
                            zdst = zp[:, c, 2 + t.off:2 + t.off + n]
                            zid = ('zp', t.idx, c)
                        else:
                            zdst = zs[:, c, 2:6, :].rearrange("p a b -> p (a b)")
                            zid = ('zs', c)
                        p.op('dve', I.tensor_tensor(out=zdst, in0=ps[bC][:, 0:n], in1=sg[k][:, 0:n], op=ALU.mult),
                             reads=[('ps', bC), ('sg', k)], writes=[zid])
                        if t.kind == 'p':
                            def zsh(s_, c=c, t=t, n=n):
                                return zp[:, c, t.off + s_:t.off + s_ + n]
                            rd = [('zp', tt.idx, c) for tt in ptiles if tt.idx <= t.idx] + ['zp_halo']
                        else:
                            def zsh(s_, c=c):
                                return zs[:, c, s_:s_ + 4, :].rearrange("p a b -> p (a b)")
                            rd = [('zs', c), ('zs_st', c)]
                        p.op('dve', I.scalar_tensor_tensor(out=sg2[k][:, 0:n], in0=zsh(0), scalar=vrow(c, R_BCW + 0),
                                                           in1=ones_f[:, 0:n], op0=ALU.mult, op1=ALU.mult),
                             reads=rd + [('vT', c), 'ones_f'], writes=[('sg2', k)])
                        for s_ in (1, 2):
                            p.op('dve', I.scalar_tensor_tensor(out=sg2[k][:, 0:n], in0=zsh(s_), scalar=vrow(c, R_BCW + s_),
                                                                                                        in1=sg2[k][:, 0:n], op0=ALU.mult, op1=ALU.add),
                                 reads=rd + [('vT', c), ('sg2', k)], writes=[('sg2', k)])
                        p.op('dve', I.tensor_tensor(out=yb[:, c, t.sl], in0=ps[bB][:, 0:n], in1=sg2[k][:, 0:n], op=ALU.mult),
                             reads=[('ps', bB), ('sg2', k)], writes=[('yb', t.idx, c)])
                w_done(3)
            p.op('dve', I.tensor_copy(out=z_halo[:, :, :], in_=zp[:, :, GP:GP + 2]),
                 reads=[('zp', NPT - 1, c) for c in range(8)], writes=['z_halo'])
            if stiles:
                store_T(lambda c: zs[:, c, 4:6, :].rearrange("p a b -> p (a b)"), [('zs', c) for c in range(8)], 32,
                        [(r * 16, 16, o_sb_s.rearrange("(s r) n -> r s n", r=2)[r]) for r in range(2)])
            if last:
                store_T(lambda c: zp[:, c, GP:GP + 2], [('zp', NPT - 1, c) for c in range(8)], 2, [(0, 2, o_sb_p[:, :])])
            out_proj(g_, tiles, ('b_out',), lambda m: vrow(m, lrow + 3), after)
            guard(['zp_halo'] + [('zp', ti, c) for ti in range(NPT) for c in range(8)] + [(nm, c) for nm in ('zs', 'zs_st') for c in range(8)])

        def mixer_c(g_, tiles, after):
            lrow = R_NG + 2 * 6
            guard(G_IDS)
            stiles = [t for t in tiles if t.kind == 's']
            for q in range(2):
                (wv0, wid0), (wv1, wid1) = w_getn([(g_, 'c_in', q, 0), (g_, 'c_in', q, 1)])
                order = [(jj, t) for t in tiles for jj in range(4)] if q == 0 else [(jj, t) for jj in range(4) for t in tiles]
                for (jj, t) in order:
                    c = q * 4 + jj
                    need(t)
                    pump()
                    if True:
                        n = t.n
                        bu_ = bank()
                        mm(ps[bu_][:, 0:n], [(wv0[:, kc, jj * 128:(jj + 1) * 128], xn[:, kc, t.sl]) for kc in range(8)],
                           reads=ids('xn', t.idx) + [wid0], writes=[('ps', bu_)])
                        bv = bank()
                        mm(ps[bv][:, 0:n], [(wv1[:, kc, jj * 128:(jj + 1) * 128], xn[:, kc, t.sl]) for kc in range(8)],
                           reads=ids('xn', t.idx) + [wid1], writes=[('ps', bv)])
                        p.op('act', I.activation(out=ubuf[:, c, t.sl], in_=ps[bu_][:, 0:n], func=AF.Gelu),
                             reads=[('ps', bu_)], writes=[('u', t.idx, c)])
                        p.op('act', I.activation(out=f[:, c, t.sl], in_=ps[bv][:, 0:n], func=AF.Gelu),
                             reads=[('ps', bv)], writes=[('f', t.idx, c)])
                        p.op('dve', I.tensor_copy(out=cbb[t.idx][:, c, 0:n], in_=f[:, c, t.sl]),
                             reads=[('f', t.idx, c)], writes=[('cbb', t.idx, c)])
                        p.op('dve', I.tensor_tensor(out=sqb[t.idx][:, c, 0:n], in0=f[:, c, t.sl], in1=f[:, c, t.sl], op=ALU.mult),
                             reads=[('f', t.idx, c)], writes=[('sqb', t.idx, c)])
                w_done(2)
            for t in tiles:
                n = t.n
                ln_stats(t, None)
                for c in range(8):
                    p.op('act', I.activation(out=f[:, c, t.sl], in_=f[:, c, t.sl], func=AF.Identity,
                                                                 scale=vrow(c, R_CLG), bias=vrow(c, R_CLB)),
                         reads=[('f', t.idx, c), ('vT', c)], writes=[('f', t.idx, c)])
                if t.kind == 'p':
                    nblk = n // 128
                    for nb_ in range(nblk):
                        vt = vtok[t.idx * nblk + nb_]
                        vid = ('vtok', t.idx * nblk + nb_)
                        for half in range(2):
                            b = bank()
                            fns = [(I.transpose(ps[b][:, k4 * 128:(k4 + 1) * 128],
                                                                                              f[:, half * 4 + k4, t.off + nb_ * 128:t.off + (nb_ + 1) * 128],
                                                                                              ident[:, :])) for k4 in range(4)]
                            p.op('pe', fns, reads=ids('f', t.idx) + ['ident'], writes=[('ps', b)])
                            p.op('act', I.activation(out=vt[:, half * 512:(half + 1) * 512], in_=ps[b][:, :], func=AF.Copy),
                                 reads=[('ps', b)], writes=[(vid, half)])
                    for hh in range(8):
                        b = bank()
                        for nb_ in range(nblk):
                            vt = vtok[t.idx * nblk + nb_]
                            vid = ('vtok', t.idx * nblk + nb_)
                            mm(ps[b][:, nb_ * 128:(nb_ + 1) * 128], [(vt[:, hh * 128:(hh + 1) * 128], wmT[:, hh, :])],
                               reads=[(vid, hh // 4), ('wmT', hh)], writes=[('ps', b)])
                        k = sg_rr[0]
                        sg_rr[0] = (k + 1) % NSG
                        p.op('dve', I.tensor_tensor(
                            out=sg[k][:, 0:n].rearrange("p (a i) -> p a i", a=nblk), in0=ps[b][:, 0:n].rearrange("p (a i) -> p a i", a=nblk),
                            in1=bsb[:, hh, :].unsqueeze(1).broadcast_to([128, nblk, 128]), op=ALU.add),
                            reads=[('ps', b), 'cst'], writes=[('sg', k)])
                        p.op('dve', I.tensor_tensor(out=yb[:, hh, t.sl], in0=sg[k][:, 0:n], in1=ubuf[:, hh, t.sl], op=ALU.mult),
                             reads=[('sg', k), ('u', t.idx, hh)], writes=[('yb', t.idx, hh)])
                else:
                    for hh in range(8):
                        for i in range(4):
                            p.op('dve', I.scalar_tensor_tensor(out=accs[:, i * 16:(i + 1) * 16], in0=f[:, hh, t.off:t.off + 16],
                                                               scalar=wsb[:, hh, i * 4:i * 4 + 1], in1=bs4[:, hh, i:i + 1].broadcast_to([128, 16]),
                                                               op0=ALU.mult, op1=ALU.add),
                                 reads=[('f', t.idx, hh), 'cst'], writes=['accs'])
                            for j in range(1, i + 1):
                                p.op('dve', I.scalar_tensor_tensor(out=accs[:, i * 16:(i + 1) * 16],
                                                                                                   in0=f[:, hh, t.off + j * 16:t.off + (j + 1) * 16],
                                                                                                   scalar=wsb[:, hh, i * 4 + j:i * 4 + j + 1],
                                                                                                   in1=accs[:, i * 16:(i + 1) * 16], op0=ALU.mult, op1=ALU.add),
                                     reads=[('f', t.idx, hh), 'cst', 'accs'], writes=['accs'])
                        p.op('dve', I.tensor_tensor(out=yb[:, hh, t.sl], in0=accs[:, :], in1=ubuf[:, hh, t.sl], op=ALU.mult),
                             reads=['accs', ('u', t.idx, hh)], writes=[('yb', t.idx, hh)])
                    store_T(lambda c, t=t: f[:, c, t.sl], ids('f', t.idx), NS,
                            [(tt * 16, 16, o_sc_s.rearrange("(s r) n -> r s n", r=4)[tt]) for tt in range(4)])
            out_proj(g_, tiles, ('c_out',), lambda m: vrow(m, lrow + 3), after)
            guard([('u', ti, c) for ti in range(NTLX) for c in range(8)] + [(('vtok', i), hf) for i in range(GP // 128) for hf in range(2)] + ['accs'])

        def mixer_d(g_, tiles, after):
            lrow = R_NG + 3 * 6
            flush_all()
            guard(G_IDS)
            ptiles = [t for t in tiles if t.kind == 'p']
            stiles = [t for t in tiles if t.kind == 's']
            last = (g_ == NG - 1)
            p.op('dve', I.tensor_copy(out=xme_p[:, :, 0:15], in_=xm_halo[:, :, :]), reads=['xm_halo'], writes=['xme_halo'])
            for t in tiles:
                if t.kind == 'p':
                    prenorm(t, lrow + 2, lambda c, t=t: xme_p[:, c, 15 + t.off:15 + t.off + t.n], ids('xme', t.idx))
                else:
                    prenorm(t, lrow + 2, lambda c, t=t: xme_s[:, c, 15:19, :].rearrange("p a b -> p (a b)"), ids('xme', t.idx))
            if stiles:
                for blk in range(2):
                    load_T(spi[blk * 120:(blk + 1) * 120, :], 120,
                           lambda c, src, blk=blk: (xme_s[:, c, 0:15, blk * 8:(blk + 1) * 8].rearrange("p r s -> p s r"),
                                                    src.rearrange("p (s r) -> p s r", s=8)),
                           [('xmes_st', c) for c in range(8)])
            wv, wid = w_get((g_, 'd_w', 0))
            for t in tiles:
                n = t.n
                for gi, w in enumerate((2, 4, 8, 16)):
                    L = {2: 1, 4: 2, 8: 3, 16: 4}[w]
                    cs = slice(2 * gi, 2 * gi + 2)
                    if t.kind == 'p':
                        B0 = 15 + t.off

                        def xv(lo, hi, cs=cs, B0=B0):
                            return xme_p[:, cs, B0 + lo:B0 + hi]

                        def tv(i, lo, hi):
                            return ptmp[i][:, :, 15 + lo:15 + hi]
                        rd = [('xme', tt.idx, cc) for tt in ptiles if tt.idx <= t.idx for cc in (2 * gi, 2 * gi + 1)] + ['xme_halo']
                    else:
                        def xv(lo, hi, cs=cs):
                            return xme_s[:, cs, 15 + lo:15 + hi, :]

                        def tv(i, lo, hi):
                            return ptmp[i][:, :, 0:19 * 16].rearrange("p c (r s) -> p c r s", s=16)[:, :, 15 + lo:15 + hi, :]
                        rd = [('xme', t.idx, cc) for cc in (2 * gi, 2 * gi + 1)] + [('xmes_st', cc) for cc in (2 * gi, 2 * gi + 1)]
                    nn = n if t.kind == 'p' else 4
                    prev = None
                    for lv in range(1, L + 1):
                        lo = -(w - 2 ** lv)
                        sh = 2 ** (lv - 1)
                        dst = tv(lv % 2, lo, nn)
                        if lv == 1:
                            a_, b_ = xv(lo, nn), xv(lo - sh, nn - sh)
                            rds = rd
                        else:
                            a_, b_ = tv((lv - 1) % 2, lo, nn), tv((lv - 1) % 2, lo - sh, nn - sh)
                            rds = [('ptmp', (lv - 1) % 2)]
                        p.op('dve', I.tensor_tensor(out=dst, in0=a_, in1=b_, op=ALU.add),
                             reads=rds, writes=[('ptmp', lv % 2)])
                    pooled = tv(L % 2, 0, nn)
                    if t.kind == 'p':
                        dd = yb[:, cs, t.sl]
                    else:
                        dd = yb[:, cs, t.sl].rearrange("p c (r s) -> p c r s", s=16)
                    p.op('dve', I.scalar_tensor_tensor(out=dd, in0=pooled, scalar=1.0 / w, in1=xv(0, nn),
                                                                                                         op0=ALU.mult, op1=ALU.subtract),
                         reads=[('ptmp', L % 2)] + rd, writes=[('yb', t.idx, 2 * gi), ('yb', t.idx, 2 * gi + 1)])
                    if t.kind == 'p' and t.tok0 == 0:
                        icb = invcnt[:, gi, 0:w - 1].unsqueeze(1).broadcast_to([128, 2, w - 1])
                        p.op('dve', I.tensor_tensor(out=tv((L + 1) % 2, 0, w - 1), in0=tv(L % 2, 0, w - 1), in1=icb, op=ALU.mult),
                             reads=[('ptmp', L % 2), 'invcnt'], writes=[('ptmp', (L + 1) % 2)])
                        p.op('dve', I.tensor_tensor(out=yb[:, cs, t.off:t.off + w - 1], in0=tv((L + 1) % 2, 0, w - 1),
                                                                                                 in1=xv(0, w - 1), op=ALU.subtract),
                             reads=[('ptmp', (L + 1) % 2)] + rd, writes=[('yb', t.idx, 2 * gi), ('yb', t.idx, 2 * gi + 1)])
                for gi in range(4):
                    for dc in range(2):
                        m = 2 * gi + dc
                        b = bank()
                        mm(ps[b][:, 0:n], [(wv[:, gi, cc, dc * 128:(dc + 1) * 128], yb[:, 2 * gi + cc, t.sl]) for cc in range(2)],
                           reads=[('yb', t.idx, 2 * gi), ('yb', t.idx, 2 * gi + 1)] + wids(wid), writes=[('ps', b)])
                        p.op('dve', I.scalar_tensor_tensor(out=f[:, m, t.sl], in0=ps[b][:, 0:n], scalar=dg[:, m:m + 1],
                                                           in1=ones_f[:, 0:n], op0=ALU.mult, op1=ALU.mult),
                             reads=[('ps', b), 'dg', 'ones_f'], writes=[('f', t.idx, m)])
                        p.op('act', I.activation(out=sqb[t.idx][:, m, 0:n], in_=ps[b][:, 0:n], func=AF.Square,
                                                                               scale=vrow(m, R_DSC)),
                             reads=[('ps', b), ('vT', m)], writes=[('sqb', t.idx, m)])
            w_done()
            p.op('dve', I.tensor_copy(out=xm_halo[:, :, :], in_=xme_p[:, :, GP:GP + 15]),
                 reads=[('xme', NPT - 1, c) for c in range(8)], writes=['xm_halo'])
            if stiles:
                t = stiles[0]
                for blk in range(2):
                    r0 = 4 + blk * 8 if blk == 0 else 12
                    nr = 8 if blk == 0 else 7
                    store_T(lambda c, r0=r0, nr=nr: xme_s[:, c, r0:r0 + nr, :].rearrange("p a b -> p (a b)"),
                            [('xme', t.idx, c) for c in range(8)], nr * 16,
                            [(rr * 16, 16, o_sp_s.rearrange("(s r) n -> r s n", r=15)[r0 - 4 + rr]) for rr in range(nr)])
            if last:
                store_T(lambda c: xme_p[:, c, GP:GP + 15], [('xme', NPT - 1, c) for c in range(8)], 15, [(0, 15, o_sp_p[:, :])])
            for t in tiles:
                after(t)
            guard(['xme_halo'] + [('xme', ti, c) for ti in range(NTLX) for c in range(8)] + [('xmes_st', c) for c in range(8)] + [('ptmp', 0), ('ptmp', 1)])

        for g_ in range(NGR):
            tiles = [TileT('p', i, i * TS, TS, g_ * GP + i * TS) for i in range(NPT)]
            if g_ == 0:
                tiles.append(TileT('s', NPT, GP, NS, 0))
            for t in tiles:
                if t.kind == 'p':
                    for nb_ in range(t.n // 128):
                        r0 = t.tok0 + nb_ * 128
                        load_T(xp[r0:r0 + 128, :], 128,
                               lambda c, src, t=t, nb_=nb_: (h[:, c, t.off + nb_ * 128:t.off + (nb_ + 1) * 128], src),
                               ids('h', t.idx))
                else:
                    load_T(None, NS, lambda c, src, t=t: (h[:, c, t.sl], src), ids('h', t.idx),
                           src_list=[(tt * 16, 16, xs.rearrange("(s r) n -> r s n", r=4)[tt]) for tt in range(4)])
            assert DBG_LAYERS == 4 and DBG_SUB == ('f0', 'm', 'f1') and DBG_STOP == 99

            def pre_xn(t, row):
                prenorm(t, row, lambda c, t=t: xn[:, c, t.sl], ids('xn', t.idx))

            for t in tiles:
                pre_xn(t, R_NG + 0)
            for l in range(4):
                lrow = R_NG + l * 6

                def after_f0(t, l=l, lrow=lrow):
                    start_chain(t, lrow + 2 if l < 3 else None)

                def after_m(t, lrow=lrow):
                    start_chain(t, lrow + 4)

                def after_f1(t, l=l, lrow=lrow):
                    start_chain(t, lrow + 6 if l < 3 else None)

                ffn(g_, l, 0, tiles, after_f0)
                tap("L%dF0" % l, tiles, g_)
                [mixer_a, mixer_b, mixer_c, mixer_d][l](g_, tiles, after_m)
                tap("L%dM" % l, tiles, g_)
                ffn(g_, l, 1, tiles, after_f1)
                tap("L%dF1" % l, tiles, g_)
            flush_all()
            for t in tiles:
                if t.kind == 'p':
                    for nb_ in range(t.n // 128):
                        r0 = t.tok0 + nb_ * 128
                        store_T(lambda c, t=t, nb_=nb_: h[:, c, t.off + nb_ * 128:t.off + (nb_ + 1) * 128], ids('h', t.idx), 128,
                                [(0, 128, yp[r0:r0 + 128, :])])
                else:
                    store_T(lambda c, t=t: h[:, c, t.sl], ids('h', t.idx), NS,
                            [(tt * 16, 16, ys.rearrange("(s r) n -> r s n", r=4)[tt]) for tt in range(4)])
        assert DBG_STOP < 99 or WS.cur == len(plan), (WS.cur, len(plan))
        p.wait_all('sp', out_toks)
        print('stream sizes', {k: len(v) for k, v in p.streams.items()}, 'counts', p.ecount, 'dma', {str(k): v for k, v in p.dcount.items() if v > 2000})
        p.emit()
    return nc


_NC_CACHE = {}


def kernel(x_prompt, x_sample, state_conv_a, state_conv_b, state_pool,
           norm_g, ffn_w_in, ffn_w_down,
           a_w_in, a_conv_w, a_conv_b, a_ln_g, a_ln_b, a_w_out,
           b_w_in, b_conv_w, b_w_out,
           c_w_in, c_ln_g, c_ln_b, c_ws, c_bs, c_w_out,
           d_w_group, d_scale):
    f32 = lambda a: np.ascontiguousarray(np.asarray(a, dtype=np.float32))
    x_prompt, x_sample = f32(x_prompt), f32(x_sample)
    state_conv_a, state_conv_b, state_pool = f32(state_conv_a), f32(state_conv_b), f32(state_pool)
    vecs = np.concatenate([
        f32(norm_g).reshape(24, D), f32(a_conv_w).reshape(31, D), f32(a_conv_b).reshape(1, D),
        f32(a_ln_g).reshape(1, D), f32(a_ln_b).reshape(1, D), f32(b_conv_w).reshape(3, D),
        f32(c_ln_g).reshape(1, D), f32(c_ln_b).reshape(1, D), f32(d_scale).reshape(1, D)], axis=0)
    assert vecs.shape == (64, D)
    shared = {
        "vecs": f32(vecs), "ffn_w_in": f32(ffn_w_in), "ffn_w_down": f32(ffn_w_down),
        "a_w_in": f32(a_w_in), "a_w_out": f32(a_w_out), "b_w_in": f32(b_w_in), "b_w_out": f32(b_w_out),
        "c_w_in": f32(c_w_in), "c_w_out": f32(c_w_out), "c_ws": f32(c_ws), "c_bs": f32(c_bs),
        "d_w_group": f32(d_w_group),
    }
    in_maps = []
    for i in range(NCORES):
        m = dict(shared)
        m["xp"] = x_prompt[i]
        m["xs"] = x_sample[16 * i:16 * (i + 1)].reshape(NS, D)
        m["sa"] = state_conv_a[16 * i:16 * (i + 1)].reshape(16 * 30, D)
        m["sb"] = state_conv_b[16 * i:16 * (i + 1)].reshape(16 * 2, D)
        m["sp"] = state_pool[16 * i:16 * (i + 1)].reshape(16 * 15, D)
        in_maps.append(m)
    key = (TS, NPT, tuple(sorted(DEBUG_TAPS)) if DEBUG_TAPS else None, DBG_NG, DBG_LAYERS, DBG_SUB, DBG_STOP, DBG_VAR)
    if key not in _NC_CACHE:
        _NC_CACHE[key] = build_program()
    nc = _NC_CACHE[key]
    ncr = NCORES if DBG_CORES is None else DBG_CORES
    res = run_bass_kernel_spmd(nc, in_maps[:ncr], core_ids=list(range(ncr)))
    R = list(res.results)
    while len(R) < NCORES:
        R.append(R[0])

    def cat(name, shape):
        return np.stack([np.asarray(R[i][name], dtype=np.float32).reshape(shape) for i in range(NCORES)], axis=0)
    y_prompt = cat("yp", (SEQ, D))
    y_sample = cat("ys", (16, 4, D)).reshape(128, 4, D)
    sa_p = cat("o_sa_p", (30, D))
    sa_s = cat("o_sa_s", (16, 30, D)).reshape(128, 30, D)
    sb_p = cat("o_sb_p", (2, D))
    sb_s = cat("o_sb_s", (16, 2, D)).reshape(128, 2, D)
    sc_s = cat("o_sc_s", (16, 4, D)).reshape(128, 4, D)
    sp_p = cat("o_sp_p", (15, D))
    sp_s = cat("o_sp_s", (16, 15, D)).reshape(128, 15, D)
    if DEBUG_TAPS:
        kernel.last_taps = {nm: [np.asarray(R[i]["tap_" + nm]) for i in range(NCORES)] for nm in DEBUG_TAPS}
    return (y_prompt, y_sample, sa_p, sa_s, sb_p, sb_s, sc_s, sp_p, sp_s)
```

```python
import contextlib
import numpy as np
import concourse.bass as bass
import concourse.mybir as mybir
from concourse.bass_utils import run_bass_kernel_spmd

F32 = mybir.dt.float32
BF16 = mybir.dt.bfloat16
AF = mybir.ActivationFunctionType
ALU = mybir.AluOpType

SAME_ENGINE_SYNC = True
NCORES = 8
D = 1024
DFF = 2816
KC = 8
NJ = 22
SEQ = 2048
NS = 64
EPS = 1e-6
TS = 512
NPT = 1
GP = TS * NPT
NG = SEQ // GP
NT = GP + NS
RING = 4
SLOT_BYTES = 11264
DEBUG_TAPS = None
DBG_NG = None
DBG_CORES = None
DBG_LAYERS = 4
DBG_STOP = 99
DBG_VAR = ''
DBG_SUB = ('f0', 'm', 'f1')

R_NG = 0
R_ACW = 24
R_ACB = 55
R_ALG = 56
R_ALB = 57
R_BCW = 58
R_CLG = 61
R_CLB = 62
R_DSC = 63


class Prog:
    ENG = ('pe', 'act', 'dve', 'pool', 'sp')

    def __init__(self, nc, stack):
        self.nc = nc
        self.stack = stack
        self.streams = {e: [] for e in self.ENG}
        self.semh = {}
        self.ecount = {e: 0 for e in self.ENG}
        for e in self.ENG:
            self.semh[('e', e)] = stack.enter_context(nc.semaphore("es_" + e))
        self.waited = {e: {} for e in self.ENG}
        self.dcount = {}
        self.lastw = {}
        self.readers = {}

    def _deps(self, reads, writes, extra):
        deps = {}

        def add(t):
            if t is None:
                return
            sk, v = t
            if deps.get(sk, 0) < v:
                deps[sk] = v
        for t in extra:
            add(t)
        for b in reads:
            add(self.lastw.get(b))
        for b in writes:
            add(self.lastw.get(b))
            for sk, v in self.readers.get(b, {}).items():
                add((sk, v))
        return deps

    def _emit_waits(self, eng, deps):
        for sk in sorted(deps, key=str):
            v = deps[sk]
            if sk == ('e', eng) and (eng in ('pe', 'sp') or not SAME_ENGINE_SYNC):
                continue
            if self.waited[eng].get(sk, 0) >= v:
                continue
            self.waited[eng][sk] = v
            sem = self.semh[sk]
            self.streams[eng].append(lambda e, sem=sem, v=v: e.wait_ge(sem, v))

    def _record(self, tok, reads, writes):
        sk, v = tok
        for b in writes:
            self.lastw[b] = tok
            self.readers[b] = {}
        for b in reads:
            d = self.readers.setdefault(b, {})
            if d.get(sk, 0) < v:
                d[sk] = v

    def op(self, eng, fns, reads=(), writes=(), extra=()):
        writes = list(writes) + [b for b in reads if isinstance(b, tuple) and b[0] == 'ps']
        deps = self._deps(reads, writes, extra)
        self._emit_waits(eng, deps)
        if callable(fns):
            fns = [fns]
        self.ecount[eng] += 1
        c = self.ecount[eng]
        sem = self.semh[('e', eng)]
        for f in fns[:-1]:
            self.streams[eng].append(f)
        last = fns[-1]
        self.streams[eng].append(lambda e, last=last, sem=sem: last(e).then_inc(sem, 1))
        tok = (('e', eng), c)
        self._record(tok, reads, writes)
        return tok

    def dma(self, q, out, in_, key, reads=(), writes=(), extra=()):
        deps = self._deps(reads, writes, extra)
        self._emit_waits(q, deps)
        sk = ('d', key)
        if sk not in self.semh:
            self.semh[sk] = self.stack.enter_context(self.nc.semaphore("ds_%d" % len(self.semh)))
            self.dcount[sk] = 0
        self.dcount[sk] += 16
        sem = self.semh[sk]
        self.streams[q].append(lambda e, out=out, in_=in_, sem=sem: e.dma_start(out=out, in_=in_).then_inc(sem, 16))
        tok = (sk, self.dcount[sk])
        self._record(tok, reads, writes)
        return tok

    def dma_multi(self, q, pairs, key, reads=(), writes=()):
        deps = self._deps(reads, writes, ())
        self._emit_waits(q, deps)
        sk = ('d', key)
        if sk not in self.semh:
            self.semh[sk] = self.stack.enter_context(self.nc.semaphore("ds_%d" % len(self.semh)))
            self.dcount[sk] = 0
        sem = self.semh[sk]
        for out, in_ in pairs:
            self.dcount[sk] += 16
            self.streams[q].append(lambda e, out=out, in_=in_, sem=sem: e.dma_start(out=out, in_=in_).then_inc(sem, 16))
        tok = (sk, self.dcount[sk])
        self._record(tok, reads, writes)
        return tok

    def wait_all(self, eng, toks):
        deps = {}
        for sk, v in toks:
            if deps.get(sk, 0) < v:
                deps[sk] = v
        self._emit_waits(eng, deps)

    def barrier(self):
        engs = ('pe', 'act', 'dve', 'pool')
        toks = [(('e', e), self.ecount[e]) for e in engs if self.ecount[e] > 0]
        for e in engs:
            self.wait_all(e, [t for t in toks if t[0] != ('e', e)])

    def emit(self):
        streams = self.streams
        with self.nc.Block() as block:
            @block.tensor
            def _(e):
                for f in streams['pe']:
                    f(e)

            @block.scalar
            def _(e):
                for f in streams['act']:
                    f(e)

            @block.vector
            def _(e):
                for f in streams['dve']:
                    f(e)

            @block.gpsimd
            def _(e):
                for f in streams['pool']:
                    f(e)

            @block.sync
            def _(e):
                for f in streams['sp']:
                    f(e)


class _I:
    def __getattr__(self, name):
        def mk(*args, **kwargs):
            return lambda e: getattr(e, name)(*args, **kwargs)
        return mk


I = _I()


class Arena:
    def __init__(self, big, nbytes):
        self.big = big
        self.nbytes = nbytes
        self.top = 0

    def at(self, off, shape, dt, parts=128):
        n = int(np.prod(shape))
        esz = 4 if dt == F32 else 2
        nb = n * esz
        assert off % 4 == 0 and nb % 4 == 0, (off, nb)
        assert off + nb <= self.nbytes, ("SBUF overflow", off, nb, self.nbytes)
        v = self.big[0:parts, off // 4:(off + nb) // 4]
        if dt != F32:
            v = v.bitcast(dt)
        if len(shape) == 2:
            v = v.rearrange("p (a b) -> p a b", a=shape[0])
        elif len(shape) == 3:
            v = v.rearrange("p (a b c) -> p a b c", a=shape[0], b=shape[1])
        elif len(shape) == 4:
            v = v.rearrange("p (a b c d) -> p a b c d", a=shape[0], b=shape[1], c=shape[2])
        return v

    def alloc(self, shape, dt, parts=128):
        n = int(np.prod(shape))
        esz = 4 if dt == F32 else 2
        nb = (n * esz + 31) // 32 * 32
        off = self.top
        self.top += nb
        return self.at(off, shape, dt, parts), off

    def reserve(self, nbytes):
        off = self.top
        self.top += (nbytes + 31) // 32 * 32
        assert self.top <= self.nbytes, ("SBUF overflow", self.top)
        return off


class TileT:
    def __init__(self, kind, idx, off, n, tok0):
        self.kind = kind
        self.idx = idx
        self.off = off
        self.n = n
        self.tok0 = tok0

    @property
    def sl(self):
        return slice(self.off, self.off + self.n)


def ids(name, t, n=8):
    return [(name, t, c) for c in range(n)]


def build_program():
    nc = bass.Bass("TRN2", target_bir_lowering=False)

    def din(name, shape):
        return nc.dram_tensor(name, shape, F32, kind="ExternalInput").ap()

    def dout(name, shape):
        return nc.dram_tensor(name, shape, F32, kind="ExternalOutput").ap()

    xp = din("xp", [SEQ, D])
    xs = din("xs", [NS, D])
    sa = din("sa", [16 * 30, D])
    sbi = din("sb", [16 * 2, D])
    spi = din("sp", [16 * 15, D])
    vecs = din("vecs", [64, D])
    ffn_w_in = din("ffn_w_in", [4, 2, D, 2 * DFF])
    ffn_w_down = din("ffn_w_down", [4, 2, DFF, D])
    a_w_in = din("a_w_in", [D, 2 * D])
    a_w_out = din("a_w_out", [D, D])
    b_w_in = din("b_w_in", [D, 3 * D])
    b_w_out = din("b_w_out", [D, D])
    c_w_in = din("c_w_in", [D, 2 * D])
    c_w_out = din("c_w_out", [D, D])
    c_ws = din("c_ws", [8, 128, 128])
    c_bs = din("c_bs", [8, 128])
    d_w_group = din("d_w_group", [4, 256, 256])

    yp = dout("yp", [SEQ, D])
    ys = dout("ys", [NS, D])
    o_sa_p = dout("o_sa_p", [30, D])
    o_sa_s = dout("o_sa_s", [16 * 30, D])
    o_sb_p = dout("o_sb_p", [2, D])
    o_sb_s = dout("o_sb_s", [16 * 2, D])
    o_sc_s = dout("o_sc_s", [NS, D])
    o_sp_p = dout("o_sp_p", [15, D])
    o_sp_s = dout("o_sp_s", [16 * 15, D])
    taps = {}
    if DEBUG_TAPS:
        for nm in sorted(DEBUG_TAPS):
            taps[nm] = dout("tap_" + nm, [128, 8, SEQ + NS])

    with contextlib.ExitStack() as st:
        p = Prog(nc, st)
        SB_BYTES = 211968
        big = st.enter_context(nc.sbuf_tensor("big", [128, SB_BYTES // 4], F32))
        A = Arena(big, SB_BYTES)
        ps = [st.enter_context(nc.psum_tensor("ps%d" % i, [128, 512], F32)) for i in range(8)]
        bank_rr = [0]

        def bank():
            b = bank_rr[0]
            bank_rr[0] = (b + 1) % 8
            return b

        ident, _ = A.alloc([128], F32)
        identb, _ = A.alloc([128], BF16)
        ones_b, _ = A.alloc([128], BF16)
        neg_half, _ = A.alloc([TS], F32)
        ones_f, _ = A.alloc([TS], F32)
        vT, _ = A.alloc([8, 64], F32)
        hg, _ = A.alloc([8, 8], F32)
        dg, _ = A.alloc([8], F32)
        wmT, _ = A.alloc([8, 128], BF16)
        bsb, _ = A.alloc([8, 128], F32)
        wsb, _ = A.alloc([8, 16], F32)
        bs4, _ = A.alloc([8, 4], F32)
        invcnt, _ = A.alloc([4, 16], F32)
        eps_t, _ = A.alloc([8], F32)
        dummy, _ = A.alloc([8], F32)
        glu_halo, _ = A.alloc([8, 30], BF16)
        z_halo, _ = A.alloc([8, 2], F32)
        xm_halo, _ = A.alloc([8, 15], F32)
        gluf_tail, _ = A.alloc([8, 30], F32)
        h, _ = A.alloc([8, NT], F32)
        xn, _ = A.alloc([8, NT], BF16)
        f, _ = A.alloc([8, NT], F32)
        yb, _ = A.alloc([8, NT], BF16)
        NTL = NPT + 1
        tsz = [TS] * NPT + [NS]
        sqb = [A.alloc([8, tsz[i]], BF16)[0] for i in range(NTL)]
        cbb = [A.alloc([8, tsz[i]], BF16)[0] for i in range(NTL)]
        st_t = [A.alloc([max(tsz[i], 8)], F32)[0] for i in range(NTL)]
        st_r = [A.alloc([tsz[i]], F32)[0] for i in range(NTL)]
        st_m = [A.alloc([tsz[i]], F32)[0] for i in range(NTL)]
        st_n = [A.alloc([tsz[i]], F32)[0] for i in range(NTL)]
        NSG = 3
        sg = [A.alloc([TS], F32)[0] for _ in range(NSG)]
        sg2 = [A.alloc([TS], F32)[0] for _ in range(NSG)]
        stage_in = [A.alloc([D], F32)[0] for _ in range(2)]
        stage_out = [A.alloc([D], F32)[0] for _ in range(2)]
        ring_off = [A.reserve(SLOT_BYTES) for _ in range(RING)]
        R0 = A.reserve(36 * 1024)
        assert A.top <= SB_BYTES, A.top
        print('SBUF bytes used', A.top, 'of', SB_BYTES)
        gbuf = A.at(R0, [NJ, NT], BF16)
        o = R0
        glub = A.at(o, [8, 30 + GP], BF16); o += 8 * (30 + GP) * 2
        exts = A.at(o, [8, 34, 16], BF16); o += 8 * 34 * 16 * 2
        glus = A.at(o, [8, NS], F32); o += 8 * NS * 4
        diag = []
        for i in range(2):
            diag.append(A.at(o, [31, 128], BF16)); o += 31 * 128 * 2
        assert o <= R0 + 36 * 1024, o - R0
        o = R0
        zp = A.at(o, [8, 2 + GP], F32); o += 8 * (2 + GP) * 4
        zs = A.at(o, [8, 6, 16], F32); o += 8 * 6 * 16 * 4
        assert o <= R0 + 36 * 1024
        o = R0
        ubuf = A.at(o, [8, NT], F32); o += 8 * NT * 4
        vtok = []
        for i in range(GP // 128):
            vtok.append(A.at(o, [D], BF16)); o += D * 2
        accs = A.at(o, [NS], F32); o += NS * 4
        assert o <= R0 + 36 * 1024, o - R0
        o = R0
        xme_p = A.at(o, [8, 15 + GP], F32); o += 8 * (15 + GP) * 4
        xme_s = A.at(o, [8, 19, 16], F32); o += 8 * 19 * 16 * 4
        ptmp = []
        for i in range(2):
            ptmp.append(A.at(o, [2, max(15 + TS, 304)], F32)); o += 2 * max(15 + TS, 304) * 4
        assert o <= R0 + 36 * 1024, o - R0

        def vrow(c, r):
            return vT[:, c, r:r + 1]

        si_rr = [0]
        so_rr = [0]
        out_toks = []

        def load_T(src, R, dst_fn, dst_ids, eng='dve', src_list=None):
            s = si_rr[0]
            si_rr[0] ^= 1
            stg = stage_in[s]
            sid = ('stgin', s)
            if src_list is None:
                src_list = [(0, R, src)]
            for (r0, nr, ap) in src_list:
                p.dma('sp', stg[r0:r0 + nr, :], ap, ('ldin', s, r0), writes=[sid])
            for half in range(2):
                b = bank()
                fns = [(I.transpose(ps[b][:, k4 * R:(k4 + 1) * R],
                                                          stg[0:R, (half * 4 + k4) * 128:(half * 4 + k4 + 1) * 128],
                                                          ident[0:R, 0:R])) for k4 in range(4)]
                p.op('pe', fns, reads=[sid, 'ident'], writes=[('ps', b)])
                for k4 in range(4):
                    c = half * 4 + k4
                    dst, src_view = dst_fn(c, ps[b][:, k4 * R:(k4 + 1) * R])
                    p.op(eng, I.tensor_copy(out=dst, in_=src_view),
                         reads=[('ps', b)], writes=[dst_ids[c]])

        def store_T(src_fn, src_ids, R, dsts, eng='act'):
            s = so_rr[0]
            so_rr[0] ^= 1
            stg = stage_out[s]
            sid = ('stgout', s)
            for half in range(2):
                b = bank()
                fns = [(I.transpose(ps[b][0:R, k4 * 128:(k4 + 1) * 128],
                                                          src_fn(half * 4 + k4), ident[:, :])) for k4 in range(4)]
                p.op('pe', fns, reads=[src_ids[half * 4 + k4] for k4 in range(4)] + ['ident'], writes=[('ps', b)])
                if eng == 'act':
                    p.op('act', I.activation(out=stg[0:R, half * 512:(half + 1) * 512],
                                                                       in_=ps[b][0:R, :], func=AF.Copy),
                         reads=[('ps', b)], writes=[sid])
                else:
                    p.op('dve', I.tensor_copy(out=stg[0:R, half * 512:(half + 1) * 512],
                                                                        in_=ps[b][0:R, :]),
                         reads=[('ps', b)], writes=[sid])
            for (r0, nr, ap) in dsts:
                out_toks.append(p.dma('sp', ap, stg[r0:r0 + nr, :], ('st', s, r0), reads=[sid]))

        def mm(bank_ap, pairs, reads, writes):
            n = len(pairs)
            fns = [(I.matmul(bank_ap, lhsT=l, rhs=r, start=(i == 0), stop=(i == n - 1)))
                   for i, (l, r) in enumerate(pairs)]
            return p.op('pe', fns, reads=reads, writes=writes)

        def wplan_group():
            plan = []

            def cols(w2d, c0, width):
                return w2d.rearrange("(kc p) n -> p kc n", p=128)[:, :, c0:c0 + width]

            for l in range(DBG_LAYERS):
                for s_ in range(2):
                    if s_ == 1 and 'm' in DBG_SUB:
                        if l == 0:
                            for q in range(2):
                                plan.append((('a_in', q, 0), cols(a_w_in, q * 512, 512), [8, 512]))
                                plan.append((('a_in', q, 1), cols(a_w_in, D + q * 512, 512), [8, 512]))
                            for q in range(2):
                                plan.append((('a_out', q), cols(a_w_out, q * 512, 512), [8, 512]))
                        elif l == 1:
                            for q in range(2):
                                for part in range(3):
                                    plan.append((('b_in', q, part), cols(b_w_in, part * D + q * 512, 512), [8, 512]))
                            for q in range(2):
                                plan.append((('b_out', q), cols(b_w_out, q * 512, 512), [8, 512]))
                        elif l == 2:
                            for q in range(2):
                                plan.append((('c_in', q, 0), cols(c_w_in, q * 512, 512), [8, 512]))
                                plan.append((('c_in', q, 1), cols(c_w_in, D + q * 512, 512), [8, 512]))
                            for q in range(2):
                                plan.append((('c_out', q), cols(c_w_out, q * 512, 512), [8, 512]))
                        else:
                            plan.append((('d_w', 0), d_w_group.rearrange("g (cc p) n -> p g cc n", p=128), [4, 2, 256]))
                    if ('f%d' % s_) not in DBG_SUB:
                        continue
                    for q in range(6):
                        width = 512 if q < 5 else 256
                        plan.append((('f_in', l, s_, q, 0), cols(ffn_w_in[l, s_], q * 512, width), [8, width]))
                        plan.append((('f_in', l, s_, q, 1), cols(ffn_w_in[l, s_], DFF + q * 512, width), [8, width]))
                    for q in range(2):
                        for kh in range(2):
                            plan.append((('f_dn', l, s_, q, kh),
                                         ffn_w_down[l, s_].rearrange("(kc p) n -> p kc n", p=128)[:, kh * 11:(kh + 1) * 11, q * 512:(q + 1) * 512], [11, 512]))
            return plan

        plan = []
        NGR = NG if DBG_NG is None else DBG_NG
        for g_ in range(NGR):
            plan += [((g_,) + k, ap, shp) for (k, ap, shp) in wplan_group()]

        NBLK = len(wplan_group())
        SLOT_ELEMS = SLOT_BYTES // 2
        wscr = None
        if NGR > 1:
            wscr = nc.dram_tensor("wscr", [NBLK, 128, SLOT_ELEMS], BF16, kind="Internal").ap()

        class WS:
            nxt = 0
            cur = 0
            views = {}

        def w_issue():
            k = WS.nxt
            key, src, shp = plan[k]
            s = k % RING
            kk = k % NBLK
            nel = int(np.prod(shp))
            view = A.at(ring_off[s], shp, BF16)
            flat = A.at(ring_off[s], [nel], BF16)
            if key[0] == 0:
                if shp[0] > 8:
                    pairs = [(view[:, k0:min(k0 + 8, shp[0]), :], src[:, k0:min(k0 + 8, shp[0]), :]) for k0 in range(0, shp[0], 8)]
                else:
                    pairs = [(view, src)]
                p.dma_multi('pool', pairs, ('w', s), writes=[('wslot', s)])
                if wscr is not None:
                    p.dma('sp', wscr[kk][:, 0:nel], flat, ('wb', s), reads=[('wslot', s)], writes=[('wscr', kk)])
            else:
                p.dma('sp', flat, wscr[kk][:, 0:nel], ('wh', s), reads=[('wscr', kk)], writes=[('wslot', s)])
            WS.views[k] = view
            WS.nxt += 1

        def w_getn(keys):
            out = []
            for i, key in enumerate(keys):
                k = WS.cur + i
                assert plan[k][0] == key, (plan[k][0], key)
                while WS.nxt <= k:
                    w_issue()
                out.append((WS.views[k], ('wslot', k % RING)))
            return out

        def w_get(key):
            return w_getn([key])[0]

        def wids(wid):
            return [wid]

        def w_done(n=1):
            for _ in range(n):
                WS.cur += 1
            while WS.nxt < len(plan) and WS.nxt < WS.cur + RING:
                w_issue()

        p.op('pool', I.memset(ident[:, :], 0.0), writes=['ident'])
        p.op('pool', I.affine_select(out=ident[:, :], in_=ident[:, :], compare_op=ALU.not_equal, fill=1.0,
                                     base=0, pattern=[[-1, 128]], channel_multiplier=1),
             reads=['ident'], writes=['ident'])
        p.op('dve', I.tensor_copy(out=identb[:, :], in_=ident[:, :]), reads=['ident'], writes=['identb'])
        p.op('dve', I.memset(ones_b[:, :], 1.0 / 1024.0), writes=['ones'])
        p.op('dve', I.memset(neg_half[:, :], -0.5), writes=['nh'])
        p.op('dve', I.memset(ones_f[:, :], 1.0), writes=['ones_f'])
        p.op('dve', I.memset(eps_t[:, :], EPS), writes=['eps_t'])
        p.op('dve', I.memset(glu_halo[:, :, :], 0.0), writes=['glu_halo'])
        p.op('dve', I.memset(z_halo[:, :, :], 0.0), writes=['z_halo'])
        p.op('dve', I.memset(xm_halo[:, :, :], 0.0), writes=['xm_halo'])
        fns = []
        for gi, w in enumerate((2, 4, 8, 16)):
            for t in range(16):
                fns.append(I.memset(invcnt[:, gi, t:t + 1], 1.0 / min(w, t + 1)))
        p.op('dve', fns, writes=['invcnt'])
        for i in range(min(RING, len(plan))):
            w_issue()
        vids = [('vT', c) for c in range(8)]
        load_T(vecs[:, :], 64, lambda c, src: (vT[:, c, :], src), vids)
        for l in range(4):
            for wi, i in enumerate((1, 5)):
                p.op('dve', I.tensor_scalar(out=hg[:, :, l * 2 + wi], in0=vT[:, :, R_NG + l * 6 + i],
                                                                      scalar1=0.5, scalar2=None, op0=ALU.mult),
                     reads=vids, writes=['hg'])
        p.op('dve', I.tensor_tensor(out=dg[:, :], in0=vT[:, :, R_DSC], in1=vT[:, :, R_NG + 3 * 6 + 3], op=ALU.mult),
             reads=vids, writes=['dg'])
        for hh in range(8):
            s = si_rr[0]
            si_rr[0] ^= 1
            stg = stage_in[s]
            sid = ('stgin', s)
            p.dma('sp', stg[:, 0:128], c_ws[hh], ('ldin', s, 0), writes=[sid])
            b = bank()
            p.op('pe', I.transpose(ps[b][:, 0:128], stg[:, 0:128], ident[:, :]),
                 reads=[sid, 'ident'], writes=[('ps', b)])
            p.op('dve', I.tensor_copy(out=wmT[:, hh, :], in_=ps[b][:, 0:128]),
                 reads=[('ps', b)], writes=[('wmT', hh)])
            p.op('pool', I.affine_select(out=wmT[:, hh, :], in_=wmT[:, hh, :], compare_op=ALU.is_ge, fill=0.0,
                                                          base=0, pattern=[[1, 128]], channel_multiplier=-1),
                 reads=[('wmT', hh)], writes=[('wmT', hh)])
        with nc.allow_non_contiguous_dma(reason="tiny broadcast loads"):
            cpairs = []
            for hh in range(8):
                cpairs.append((bsb[:, hh, :], c_bs[hh:hh + 1, :].partition_broadcast(128)))
                cpairs.append((bs4[:, hh, :], c_bs[hh:hh + 1, 0:4].partition_broadcast(128)))
                for i in range(4):
                    cpairs.append((wsb[:, hh, i * 4:(i + 1) * 4], c_ws[hh, i:i + 1, 0:4].partition_broadcast(128)))
            p.dma_multi('sp', cpairs, 'cst', writes=['cst'])
        out_toks.append(p.dma('sp', o_sa_s.rearrange("(s r) n -> s r n", r=30)[:, 0:26, :],
                              sa.rearrange("(s r) n -> s r n", r=30)[:, 4:30, :], 'd2d'))

        def tap(name, tiles, g_):
            if not DEBUG_TAPS or name not in DEBUG_TAPS:
                return
            flush_all()
            for t in tiles:
                col = g_ * GP + t.off if t.kind == 'p' else SEQ
                out_toks.append(p.dma('sp', taps[name][:, :, col:col + t.n], h[:, :, t.sl], ('tap', name, t.idx), reads=ids('h', t.idx)))

        NTLX = NPT + 1
        G_IDS = [('g', ti, j) for ti in range(NTLX) for j in range(NJ)]

        def guard(write_ids):
            tok = p.op('dve', I.memset(dummy[:, 0:1], 0.0), writes=list(write_ids) + ['dummy'])
            p.wait_all('act', [tok])
            p.wait_all('pool', [tok])

        def tok_sums(t, srcs, reads):
            n = t.n
            b = bank()
            fns = []
            for i, src in enumerate(srcs):
                for blk in range((n + 127) // 128):
                    P = min(128, n - blk * 128)
                    for c in range(8):
                        fns.append(I.matmul(ps[b][0:P, i * 4 + blk:i * 4 + blk + 1], lhsT=src[:, c, blk * 128:blk * 128 + P],
                                            rhs=ones_b[:, 0:1], start=(c == 0), stop=(c == 7)))
            p.op('pe', fns, reads=reads + ['ones'], writes=[('ps', b)])
            return b

        def bcast_tok(t, col, dst, dst_id):
            n = t.n
            nblk = (n + 127) // 128
            b2 = bank()
            fns = []
            for blk in range(nblk):
                P = min(128, n - blk * 128)
                fns.append(I.matmul(ps[b2][:, blk * 128:blk * 128 + P],
                                    lhsT=st_t[t.idx][0:P, col + blk:col + blk + 1].broadcast_to([P, 128]),
                                    rhs=ident[0:P, 0:P], start=True, stop=True))
            p.op('pe', fns, reads=[('st_t', t.idx), 'ident'], writes=[('ps', b2)])
            if dst is None:
                return ps[b2][:, 0:n], ('ps', b2)
            p.op('act', I.activation(out=dst, in_=ps[b2][:, 0:n], func=AF.Copy), reads=[('ps', b2)], writes=[dst_id])
            return dst, dst_id

        def stats_rstd_gen(t, sq_src, sq_ids):
            n = t.n
            nblk = (n + 127) // 128
            P = min(128, n)
            b = tok_sums(t, [sq_src], sq_ids)
            p.op('dve', I.tensor_scalar(out=st_t[t.idx][0:P, 0:nblk], in0=ps[b][0:P, 0:nblk], scalar1=EPS, scalar2=None, op0=ALU.add),
                 reads=[('ps', b)], writes=[('st_t', t.idx)])
            p.op('pool', I.tensor_tensor(out=st_t[t.idx][0:P, 0:nblk], in0=st_t[t.idx][0:P, 0:nblk], in1=neg_half[0:P, 0:nblk], op=ALU.pow),
                 reads=[('st_t', t.idx), 'nh'], writes=[('st_t', t.idx)])
            yield
            yield
            RST[t.idx] = bcast_tok(t, 0, None, None)

        RST = {}

        def prenorm_gen(t, grow, dst_fn, dst_ids):
            n = t.n
            for hf in range(2):
                cs = slice(4 * hf, 4 * hf + 4)
                p.op('act', I.activation(out=sqb[t.idx][:, cs, 0:n], in_=h[:, cs, t.sl], func=AF.Square),
                     reads=ids('h', t.idx)[cs], writes=ids('sqb', t.idx)[cs])
            yield
            yield
            yield from stats_rstd_gen(t, sqb[t.idx], ids('sqb', t.idx))
            yield
            for c in range(8):
                rap, rid = RST[t.idx]
                p.op('dve', I.scalar_tensor_tensor(out=dst_fn(c), in0=h[:, c, t.sl], scalar=vrow(c, grow),
                                                   in1=rap, op0=ALU.mult, op1=ALU.mult),
                     reads=[('h', t.idx, c), rid, ('vT', c)], writes=[dst_ids[c]])

        def prenorm(t, grow, dst_fn, dst_ids):
            for _ in prenorm_gen(t, grow, dst_fn, dst_ids):
                pass

        def postnorm_gen(t):
            n = t.n
            yield
            yield from stats_rstd_gen(t, sqb[t.idx], ids('sqb', t.idx))
            yield
            rap, rid = RST[t.idx]
            rb = rap.unsqueeze(1).broadcast_to([128, 4, n])
            for hf in range(2):
                cs = slice(4 * hf, 4 * hf + 4)
                p.op('dve', I.tensor_tensor(out=f[:, cs, t.sl], in0=f[:, cs, t.sl], in1=rb, op=ALU.mult),
                     reads=ids('f', t.idx)[cs] + [rid], writes=ids('f', t.idx)[cs])
                p.op('dve', I.tensor_tensor(out=h[:, cs, t.sl], in0=h[:, cs, t.sl], in1=f[:, cs, t.sl], op=ALU.add),
                     reads=ids('f', t.idx)[cs] + ids('h', t.idx)[cs], writes=ids('h', t.idx)[cs])

        def postnorm_residual(t):
            for _ in postnorm_gen(t):
                pass

        pending = {}

        def chain_gen(t, pre_row):
            yield from postnorm_gen(t)
            if pre_row is not None:
                yield from prenorm_gen(t, pre_row, lambda c, t=t: xn[:, c, t.sl], ids('xn', t.idx))

        def start_chain(t, pre_row):
            need(t)
            pending[t.idx] = chain_gen(t, pre_row)

        def pump():
            for k in list(pending):
                try:
                    next(pending[k])
                except StopIteration:
                    del pending[k]

        def need(t):
            g = pending.pop(t.idx, None)
            if g is not None:
                for _ in g:
                    pass

        def flush_all():
            for k in list(pending):
                g = pending.pop(k)
                for _ in g:
                    pass

        def out_proj(g_, tiles, wkey, gain_fn, after):
            for q in range(2):
                wv, wid = w_get((g_,) + wkey + (q,))
                order = [(mmi, t) for mmi in range(4) for t in tiles] if q == 0 else [(mmi, t) for t in tiles for mmi in range(4)]
                for (mmi, t) in order:
                    m = q * 4 + mmi
                    b = bank()
                    mm(ps[b][:, 0:t.n], [(wv[:, kc, mmi * 128:(mmi + 1) * 128], yb[:, kc, t.sl]) for kc in range(8)],
                       reads=ids('yb', t.idx) + wids(wid), writes=[('ps', b)])
                    pump()
                    p.op('dve', I.scalar_tensor_tensor(out=f[:, m, t.sl], in0=ps[b][:, 0:t.n], scalar=gain_fn(m),
                                                       in1=ones_f[:, 0:t.n], op0=ALU.mult, op1=ALU.mult),
                         reads=[('ps', b), 'hg', 'dg', 'ones_f'] + vids, writes=[('f', t.idx, m)])
                    p.op('act', I.activation(out=sqb[t.idx][:, m, 0:t.n], in_=ps[b][:, 0:t.n], func=AF.Square),
                         reads=[('ps', b)], writes=[('sqb', t.idx, m)])
                    if q == 1 and mmi == 3:
                        after(t)
                w_done()

        sg_rr = [0]

        def ffn(g_, l, s_, tiles, after):
            for q in range(6):
                (gv, gid), (uv, uid) = w_getn([(g_, 'f_in', l, s_, q, 0), (g_, 'f_in', l, s_, q, 1)])
                njj = 4 if q < 5 else 2
                order = [(jj, t) for t in tiles for jj in range(njj)] if q == 0 else [(jj, t) for jj in range(njj) for t in tiles]
                for (jj, t) in order:
                    j = q * 4 + jj
                    n = t.n
                    need(t)
                    bg_ = bank()
                    mm(ps[bg_][:, 0:n], [(gv[:, kc, jj * 128:(jj + 1) * 128], xn[:, kc, t.sl]) for kc in range(8)],
                       reads=ids('xn', t.idx) + [gid], writes=[('ps', bg_)])
                    bu_ = bank()
                    mm(ps[bu_][:, 0:n], [(uv[:, kc, jj * 128:(jj + 1) * 128], xn[:, kc, t.sl]) for kc in range(8)],
                       reads=ids('xn', t.idx) + [uid], writes=[('ps', bu_)])
                    pump()
                    k = sg_rr[0]
                    sg_rr[0] = (k + 1) % NSG
                    p.op('act', I.activation(out=sg[k][:, 0:n], in_=ps[bg_][:, 0:n], func=AF.Silu),
                         reads=[('ps', bg_)], writes=[('sg', k)])
                    p.op('dve', I.tensor_tensor(out=gbuf[:, j, t.sl], in0=ps[bu_][:, 0:n], in1=sg[k][:, 0:n], op=ALU.mult),
                         reads=[('ps', bu_), ('sg', k)], writes=[('g', t.idx, j)])
                w_done(2)
            gi = l * 2 + s_
            for q in range(2):
                wvs = w_getn([(g_, 'f_dn', l, s_, q, 0), (g_, 'f_dn', l, s_, q, 1)])
                order = [(mmi, t) for mmi in range(4) for t in tiles] if q == 0 else [(mmi, t) for t in tiles for mmi in range(4)]
                for (mmi, t) in order:
                    m = q * 4 + mmi
                    n = t.n
                    b = bank()
                    mm(ps[b][:, 0:n], [(wvs[j // 11][0][:, j % 11, mmi * 128:(mmi + 1) * 128], gbuf[:, j, t.sl]) for j in range(NJ)],
                       reads=ids('g', t.idx, NJ) + [wvs[0][1], wvs[1][1]], writes=[('ps', b)])
                    pump()
                    p.op('dve', I.scalar_tensor_tensor(out=f[:, m, t.sl], in0=ps[b][:, 0:n], scalar=hg[:, m, gi:gi + 1],
                                                       in1=ones_f[:, 0:n], op0=ALU.mult, op1=ALU.mult),
                         reads=[('ps', b), 'hg', 'ones_f'], writes=[('f', t.idx, m)])
                    p.op('act', I.activation(out=sqb[t.idx][:, m, 0:n], in_=ps[b][:, 0:n], func=AF.Square),
                         reads=[('ps', b)], writes=[('sqb', t.idx, m)])
                    if q == 1 and mmi == 3:
                        after(t)
                w_done(2)

        def ln_stats(t, src_f_ids):
            n = t.n
            nblk = (n + 127) // 128
            P = min(128, n)
            S = st_t[t.idx]
            b = tok_sums(t, [cbb[t.idx], sqb[t.idx]], ids('cbb', t.idx) + ids('sqb', t.idx))
            p.op('dve', I.tensor_copy(out=S[0:P, 8:8 + nblk], in_=ps[b][0:P, 0:nblk]), reads=[('ps', b)], writes=[('st_t', t.idx)])
            p.op('dve', I.tensor_tensor(out=S[0:P, 16:16 + nblk], in0=S[0:P, 8:8 + nblk], in1=S[0:P, 8:8 + nblk], op=ALU.mult),
                 reads=[('st_t', t.idx)], writes=[('st_t', t.idx)])
            p.op('dve', I.scalar_tensor_tensor(out=S[0:P, 0:nblk], in0=ps[b][0:P, 4:4 + nblk], scalar=EPS, in1=S[0:P, 16:16 + nblk],
                                               op0=ALU.add, op1=ALU.subtract),
                 reads=[('ps', b), ('st_t', t.idx)], writes=[('st_t', t.idx)])
            p.op('pool', I.tensor_tensor(out=S[0:P, 0:nblk], in0=S[0:P, 0:nblk], in1=neg_half[0:P, 0:nblk], op=ALU.pow),
                 reads=[('st_t', t.idx), 'nh'], writes=[('st_t', t.idx)])
            p.op('dve', I.scalar_tensor_tensor(out=S[0:P, 16:16 + nblk], in0=S[0:P, 8:8 + nblk], scalar=-1.0, in1=S[0:P, 0:nblk],
                                               op0=ALU.mult, op1=ALU.mult),
                 reads=[('st_t', t.idx)], writes=[('st_t', t.idx)])
            bcast_tok(t, 0, st_r[t.idx][:, 0:n], ('st_r', t.idx))
            bcast_tok(t, 16, st_n[t.idx][:, 0:n], ('st_n', t.idx))
            rb = st_r[t.idx][:, 0:n].unsqueeze(1).broadcast_to([128, 8, n])
            nb_ = st_n[t.idx][:, 0:n].unsqueeze(1).broadcast_to([128, 8, n])
            p.op('dve', I.tensor_tensor(out=f[:, :, t.sl], in0=f[:, :, t.sl], in1=rb, op=ALU.mult),
                 reads=ids('f', t.idx) + [('st_r', t.idx)], writes=ids('f', t.idx))
            p.op('dve', I.tensor_tensor(out=f[:, :, t.sl], in0=f[:, :, t.sl], in1=nb_, op=ALU.add),
                 reads=ids('f', t.idx) + [('st_n', t.idx)], writes=ids('f', t.idx))

        def mixer_a(g_, tiles, after):
            lrow = R_NG + 0 * 6
            guard(G_IDS)
            ptiles = [t for t in tiles if t.kind == 'p']
            stiles = [t for t in tiles if t.kind == 's']
            last = (g_ == NG - 1)
            p.op('dve', I.tensor_copy(out=glub[:, :, 0:30], in_=glu_halo[:, :, :]), reads=['glu_halo'], writes=['glub_halo'])
            for q in range(2):
                (wv0, wid0), (wv1, wid1) = w_getn([(g_, 'a_in', q, 0), (g_, 'a_in', q, 1)])
                order = [(jj, t) for t in tiles for jj in range(4)] if q == 0 else [(jj, t) for jj in range(4) for t in tiles]
                for (jj, t) in order:
                    c = q * 4 + jj
                    need(t)
                    pump()
                    if True:
                        n = t.n
                        bv = bank()
                        mm(ps[bv][:, 0:n], [(wv0[:, kc, jj * 128:(jj + 1) * 128], xn[:, kc, t.sl]) for kc in range(8)],
                           reads=ids('xn', t.idx) + [wid0], writes=[('ps', bv)])
                        bg_ = bank()
                        mm(ps[bg_][:, 0:n], [(wv1[:, kc, jj * 128:(jj + 1) * 128], xn[:, kc, t.sl]) for kc in range(8)],
                           reads=ids('xn', t.idx) + [wid1], writes=[('ps', bg_)])
                        k = sg_rr[0]
                        sg_rr[0] = (k + 1) % NSG
                        p.op('act', I.activation(out=sg[k][:, 0:n], in_=ps[bg_][:, 0:n], func=AF.Tanh, scale=0.5),
                             reads=[('ps', bg_)], writes=[('sg', k)])
                        p.op('dve', I.scalar_tensor_tensor(out=sg2[k][:, 0:n], in0=sg[k][:, 0:n], scalar=1.0,
                                                                                       in1=ps[bv][:, 0:n], op0=ALU.add, op1=ALU.mult),
                             reads=[('ps', bv), ('sg', k)], writes=[('sg2', k)])
                        if t.kind == 'p':
                            p.op('act', I.activation(out=glub[:, c, 30 + t.off:30 + t.off + n], in_=sg2[k][:, 0:n],
                                                                                   func=AF.Copy, scale=0.5),
                                 reads=[('sg2', k)], writes=[('glub', t.idx, c)])
                            if last and t.idx == NPT - 1:
                                p.op('act', I.activation(out=gluf_tail[:, c, :], in_=sg2[k][:, n - 30:n], func=AF.Copy, scale=0.5),
                                     reads=[('sg2', k)], writes=[('gluf_tail', c)])
                        else:
                            p.op('act', I.activation(out=exts[:, c, 30:34, :], in_=sg2[k][:, 0:n].rearrange("p (a b) -> p a b", a=4),
                                                                              func=AF.Copy, scale=0.5),
                                 reads=[('sg2', k)], writes=[('exts', c)])
                            p.op('act', I.activation(out=glus[:, c, :], in_=sg2[k][:, 0:n], func=AF.Copy, scale=0.5),
                                 reads=[('sg2', k)], writes=[('glus', c)])
                w_done(2)
            if stiles:
                for blk in range(4):
                    load_T(sa[blk * 120:(blk + 1) * 120, :], 120,
                           lambda c, src, blk=blk: (exts[:, c, 0:30, blk * 4:(blk + 1) * 4].rearrange("p r s -> p s r"),
                                                    src.rearrange("p (s r) -> p s r", s=4)),
                           [('exts_st', c) for c in range(8)])
            def build_diag(c):
                p.op('dve', I.tensor_tensor(out=diag[c % 2][:, :, :], in0=identb[:, :].unsqueeze(1).broadcast_to([128, 31, 128]),
                                            in1=vT[:, c, R_ACW:R_ACW + 31].unsqueeze(2).broadcast_to([128, 31, 128]), op=ALU.mult),
                     reads=['identb', ('vT', c)], writes=[('diag', c % 2)])

            build_diag(0)
            for c in range(8):
                dc_ = diag[c % 2]
                did = ('diag', c % 2)
                if c + 1 < 8:
                    build_diag(c + 1)
                for t in tiles:
                    n = t.n
                    b = bank()
                    if t.kind == 'p':
                        pairs = [(dc_[:, k, :], glub[:, c, t.off + k:t.off + k + n]) for k in range(31)]
                        rd = [('glub', tt.idx, c) for tt in ptiles if tt.idx <= t.idx] + ['glub_halo']
                    else:
                        pairs = [(dc_[:, k, :], exts[:, c, k:k + 4, :].rearrange("p a b -> p (a b)")) for k in range(31)]
                        rd = [('exts', c), ('exts_st', c)]
                    mm(ps[b][:, 0:n], pairs, reads=rd + [did], writes=[('ps', b)])
                    p.op('act', I.activation(out=f[:, c, t.sl], in_=ps[b][:, 0:n], func=AF.Identity, bias=vrow(c, R_ACB)),
                         reads=[('ps', b), ('vT', c)], writes=[('f', t.idx, c)])
                    p.op('act', I.activation(out=sqb[t.idx][:, c, 0:n], in_=ps[b][:, 0:n], func=AF.Square, bias=vrow(c, R_ACB)),
                         reads=[('ps', b), ('vT', c)], writes=[('sqb', t.idx, c)])
                    p.op('dve', I.tensor_copy(out=cbb[t.idx][:, c, 0:n], in_=f[:, c, t.sl]),
                         reads=[('f', t.idx, c)], writes=[('cbb', t.idx, c)])
            p.op('dve', I.tensor_copy(out=glu_halo[:, :, :], in_=glub[:, :, GP:GP + 30]),
                 reads=[('glub', NPT - 1, c) for c in range(8)], writes=['glu_halo'])
            for t in tiles:
                n = t.n
                ln_stats(t, None)
                for c in range(8):
                    p.op('act', I.activation(out=yb[:, c, t.sl], in_=f[:, c, t.sl], func=AF.Silu,
                                                                 scale=vrow(c, R_ALG), bias=vrow(c, R_ALB)),
                         reads=[('f', t.idx, c), ('vT', c)], writes=[('yb', t.idx, c)])
            if stiles:
                store_T(lambda c: glus[:, c, :], [('glus', c) for c in range(8)], NS,
                        [(tt * 16, 16, o_sa_s.rearrange("(s r) n -> r s n", r=30)[26 + tt]) for tt in range(4)])
            if last:
                store_T(lambda c: gluf_tail[:, c, :], [('gluf_tail', c) for c in range(8)], 30, [(0, 30, o_sa_p[:, :])])
            out_proj(g_, tiles, ('a_out',), lambda m: vrow(m, lrow + 3), after)
            guard(['glub_halo'] + [('glub', ti, c) for ti in range(NPT) for c in range(8)] + [(nm, c) for nm in ('exts', 'exts_st', 'glus') for c in range(8)] + [('diag', 0), ('diag', 1)])

        def mixer_b(g_, tiles, after):
            lrow = R_NG + 1 * 6
            guard(G_IDS)
            ptiles = [t for t in tiles if t.kind == 'p']
            stiles = [t for t in tiles if t.kind == 's']
            last = (g_ == NG - 1)
            p.op('dve', I.tensor_copy(out=zp[:, :, 0:2], in_=z_halo[:, :, :]), reads=['z_halo'], writes=['zp_halo'])
            if stiles:
                load_T(sbi[:, :], 32,
                       lambda c, src: (zs[:, c, 0:2, :].rearrange("p r s -> p s r"), src.rearrange("p (s r) -> p s r", s=16)),
                       [('zs_st', c) for c in range(8)])
            for q in range(2):
                wv3 = w_getn([(g_, 'b_in', q, part) for part in range(3)])
                order = [(jj, t) for t in tiles for jj in range(4)] if q == 0 else [(jj, t) for jj in range(4) for t in tiles]
                for (jj, t) in order:
                    c = q * 4 + jj
                    need(t)
                    pump()
                    if True:
                        n = t.n
                        banks3 = []
                        for part in range(3):
                            b = bank()
                            mm(ps[b][:, 0:n], [(wv3[part][0][:, kc, jj * 128:(jj + 1) * 128], xn[:, kc, t.sl]) for kc in range(8)],
                               reads=ids('xn', t.idx) + [wv3[part][1]], writes=[('ps', b)])
                            banks3.append(b)
                        bB, bC, bX = banks3
                        k = sg_rr[0]
                        sg_rr[0] = (k + 1) % NSG
                        p.op('act', I.activation(out=sg[k][:, 0:n], in_=ps[bX][:, 0:n], func=AF.Copy),
                             reads=[('ps', bX)], writes=[('sg', k)])
                        if t.kind == 'p':
                            zdst = zp[:, c, 2 + t.off:2 + t.off + n]
                            zid = ('zp', t.idx, c)
                        else:
                            zdst = zs[:, c, 2:6, :].rearrange("p a b -> p (a b)")
                            zid = ('zs', c)
                        p.op('dve', I.tensor_tensor(out=zdst, in0=ps[bC][:, 0:n], in1=sg[k][:, 0:n], op=ALU.mult),
                             reads=[('ps', bC), ('sg', k)], writes=[zid])
                        if t.kind == 'p':
                            def zsh(s_, c=c, t=t, n=n):
                                return zp[:, c, t.off + s_:t.off + s_ + n]
                            rd = [('zp', tt.idx, c) for tt in ptiles if tt.idx <= t.idx] + ['zp_halo']
                        else:
                            def zsh(s_, c=c):
                                return zs[:, c, s_:s_ + 4, :].rearrange("p a b -> p (a b)")
                            rd = [('zs', c), ('zs_st', c)]
                        p.op('dve', I.scalar_tensor_tensor(out=sg2[k][:, 0:n], in0=zsh(0), scalar=vrow(c, R_BCW + 0),
                                                           in1=ones_f[:, 0:n], op0=ALU.mult, op1=ALU.mult),
                             reads=rd + [('vT', c), 'ones_f'], writes=[('sg2', k)])
                        for s_ in (1, 2):
                            p.op('dve', I.scalar_tensor_tensor(out=sg2[k][:, 0:n], in0=zsh(s_), scalar=vrow(c, R_BCW + s_),
                                                                                                        in1=sg2[k][:, 0:n], op0=ALU.mult, op1=ALU.add),
                                 reads=rd + [('vT', c), ('sg2', k)], writes=[('sg2', k)])
                        p.op('dve', I.tensor_tensor(out=yb[:, c, t.sl], in0=ps[bB][:, 0:n], in1=sg2[k][:, 0:n], op=ALU.mult),
                             reads=[('ps', bB), ('sg2', k)], writes=[('yb', t.idx, c)])
                w_done(3)
            p.op('dve', I.tensor_copy(out=z_halo[:, :, :], in_=zp[:, :, GP:GP + 2]),
                 reads=[('zp', NPT - 1, c) for c in range(8)], writes=['z_halo'])
            if stiles:
                store_T(lambda c: zs[:, c, 4:6, :].rearrange("p a b -> p (a b)"), [('zs', c) for c in range(8)], 32,
                        [(r * 16, 16, o_sb_s.rearrange("(s r) n -> r s n", r=2)[r]) for r in range(2)])
            if last:
                store_T(lambda c: zp[:, c, GP:GP + 2], [('zp', NPT - 1, c) for c in range(8)], 2, [(0, 2, o_sb_p[:, :])])
            out_proj(g_, tiles, ('b_out',), lambda m: vrow(m, lrow + 3), after)
            guard(['zp_halo'] + [('zp', ti, c) for ti in range(NPT) for c in range(8)] + [(nm, c) for nm in ('zs', 'zs_st') for c in range(8)])

        def mixer_c(g_, tiles, after):
            lrow = R_NG + 2 * 6
            guard(G_IDS)
            stiles = [t for t in tiles if t.kind == 's']
            for q in range(2):
                (wv0, wid0), (wv1, wid1) = w_getn([(g_, 'c_in', q, 0), (g_, 'c_in', q, 1)])
                order = [(jj, t) for t in tiles for jj in range(4)] if q == 0 else [(jj, t) for jj in range(4) for t in tiles]
                for (jj, t) in order:
                    c = q * 4 + jj
                    need(t)
                    pump()
                    if True:
                        n = t.n
                        bu_ = bank()
                        mm(ps[bu_][:, 0:n], [(wv0[:, kc, jj * 128:(jj + 1) * 128], xn[:, kc, t.sl]) for kc in range(8)],
                           reads=ids('xn', t.idx) + [wid0], writes=[('ps', bu_)])
                        bv = bank()
                        mm(ps[bv][:, 0:n], [(wv1[:, kc, jj * 128:(jj + 1) * 128], xn[:, kc, t.sl]) for kc in range(8)],
                           reads=ids('xn', t.idx) + [wid1], writes=[('ps', bv)])
                        p.op('act', I.activation(out=ubuf[:, c, t.sl], in_=ps[bu_][:, 0:n], func=AF.Gelu),
                             reads=[('ps', bu_)], writes=[('u', t.idx, c)])
                        p.op('act', I.activation(out=f[:, c, t.sl], in_=ps[bv][:, 0:n], func=AF.Gelu),
                             reads=[('ps', bv)], writes=[('f', t.idx, c)])
                        p.op('dve', I.tensor_copy(out=cbb[t.idx][:, c, 0:n], in_=f[:, c, t.sl]),
                             reads=[('f', t.idx, c)], writes=[('cbb', t.idx, c)])
                        p.op('dve', I.tensor_tensor(out=sqb[t.idx][:, c, 0:n], in0=f[:, c, t.sl], in1=f[:, c, t.sl], op=ALU.mult),
                             reads=[('f', t.idx, c)], writes=[('sqb', t.idx, c)])
                w_done(2)
            for t in tiles:
                n = t.n
                ln_stats(t, None)
                for c in range(8):
                    p.op('act', I.activation(out=f[:, c, t.sl], in_=f[:, c, t.sl], func=AF.Identity,
                                                                 scale=vrow(c, R_CLG), bias=vrow(c, R_CLB)),
                         reads=[('f', t.idx, c), ('vT', c)], writes=[('f', t.idx, c)])
                if t.kind == 'p':
                    nblk = n // 128
                    for nb_ in range(nblk):
                        vt = vtok[t.idx * nblk + nb_]
                        vid = ('vtok', t.idx * nblk + nb_)
                        for half in range(2):
                            b = bank()
                            fns = [(I.transpose(ps[b][:, k4 * 128:(k4 + 1) * 128],
                                                                                              f[:, half * 4 + k4, t.off + nb_ * 128:t.off + (nb_ + 1) * 128],
                                                                                              ident[:, :])) for k4 in range(4)]
                            p.op('pe', fns, reads=ids('f', t.idx) + ['ident'], writes=[('ps', b)])
                            p.op('act', I.activation(out=vt[:, half * 512:(half + 1) * 512], in_=ps[b][:, :], func=AF.Copy),
                                 reads=[('ps', b)], writes=[(vid, half)])
                    for hh in range(8):
                        b = bank()
                        for nb_ in range(nblk):
                            vt = vtok[t.idx * nblk + nb_]
                            vid = ('vtok', t.idx * nblk + nb_)
                            mm(ps[b][:, nb_ * 128:(nb_ + 1) * 128], [(vt[:, hh * 128:(hh + 1) * 128], wmT[:, hh, :])],
                               reads=[(vid, hh // 4), ('wmT', hh)], writes=[('ps', b)])
                        k = sg_rr[0]
                        sg_rr[0] = (k + 1) % NSG
                        p.op('dve', I.tensor_tensor(
                            out=sg[k][:, 0:n].rearrange("p (a i) -> p a i", a=nblk), in0=ps[b][:, 0:n].rearrange("p (a i) -> p a i", a=nblk),
                            in1=bsb[:, hh, :].unsqueeze(1).broadcast_to([128, nblk, 128]), op=ALU.add),
                            reads=[('ps', b), 'cst'], writes=[('sg', k)])
                        p.op('dve', I.tensor_tensor(out=yb[:, hh, t.sl], in0=sg[k][:, 0:n], in1=ubuf[:, hh, t.sl], op=ALU.mult),
                             reads=[('sg', k), ('u', t.idx, hh)], writes=[('yb', t.idx, hh)])
                else:
                    for hh in range(8):
                        for i in range(4):
                            p.op('dve', I.scalar_tensor_tensor(out=accs[:, i * 16:(i + 1) * 16], in0=f[:, hh, t.off:t.off + 16],
                                                               scalar=wsb[:, hh, i * 4:i * 4 + 1], in1=bs4[:, hh, i:i + 1].broadcast_to([128, 16]),
                                                               op0=ALU.mult, op1=ALU.add),
                                 reads=[('f', t.idx, hh), 'cst'], writes=['accs'])
                            for j in range(1, i + 1):
                                p.op('dve', I.scalar_tensor_tensor(out=accs[:, i * 16:(i + 1) * 16],
                                                                                                   in0=f[:, hh, t.off + j * 16:t.off + (j + 1) * 16],
                                                                                                   scalar=wsb[:, hh, i * 4 + j:i * 4 + j + 1],
                                                                                                   in1=accs[:, i * 16:(i + 1) * 16], op0=ALU.mult, op1=ALU.add),
                                     reads=[('f', t.idx, hh), 'cst', 'accs'], writes=['accs'])
                        p.op('dve', I.tensor_tensor(out=yb[:, hh, t.sl], in0=accs[:, :], in1=ubuf[:, hh, t.sl], op=ALU.mult),
                             reads=['accs', ('u', t.idx, hh)], writes=[('yb', t.idx, hh)])
                    store_T(lambda c, t=t: f[:, c, t.sl], ids('f', t.idx), NS,
                            [(tt * 16, 16, o_sc_s.rearrange("(s r) n -> r s n", r=4)[tt]) for tt in range(4)])
            out_proj(g_, tiles, ('c_out',), lambda m: vrow(m, lrow + 3), after)
            guard([('u', ti, c) for ti in range(NTLX) for c in range(8)] + [(('vtok', i), hf) for i in range(GP // 128) for hf in range(2)] + ['accs'])

        def mixer_d(g_, tiles, after):
            lrow = R_NG + 3 * 6
            flush_all()
            guard(G_IDS)
            ptiles = [t for t in tiles if t.kind == 'p']
            stiles = [t for t in tiles if t.kind == 's']
            last = (g_ == NG - 1)
            p.op('dve', I.tensor_copy(out=xme_p[:, :, 0:15], in_=xm_halo[:, :, :]), reads=['xm_halo'], writes=['xme_halo'])
            for t in tiles:
                if t.kind == 'p':
                    prenorm(t, lrow + 2, lambda c, t=t: xme_p[:, c, 15 + t.off:15 + t.off + t.n], ids('xme', t.idx))
                else:
                    prenorm(t, lrow + 2, lambda c, t=t: xme_s[:, c, 15:19, :].rearrange("p a b -> p (a b)"), ids('xme', t.idx))
            if stiles:
                for blk in range(2):
                    load_T(spi[blk * 120:(blk + 1) * 120, :], 120,
                           lambda c, src, blk=blk: (xme_s[:, c, 0:15, blk * 8:(blk + 1) * 8].rearrange("p r s -> p s r"),
                                                    src.rearrange("p (s r) -> p s r", s=8)),
                           [('xmes_st', c) for c in range(8)])
            wv, wid = w_get((g_, 'd_w', 0))
            for t in tiles:
                n = t.n
                for gi, w in enumerate((2, 4, 8, 16)):
                    L = {2: 1, 4: 2, 8: 3, 16: 4}[w]
                    cs = slice(2 * gi, 2 * gi + 2)
                    if t.kind == 'p':
                        B0 = 15 + t.off

                        def xv(lo, hi, cs=cs, B0=B0):
                            return xme_p[:, cs, B0 + lo:B0 + hi]

                        def tv(i, lo, hi):
                            return ptmp[i][:, :, 15 + lo:15 + hi]
                        rd = [('xme', tt.idx, cc) for tt in ptiles if tt.idx <= t.idx for cc in (2 * gi, 2 * gi + 1)] + ['xme_halo']
                    else:
                        def xv(lo, hi, cs=cs):
                            return xme_s[:, cs, 15 + lo:15 + hi, :]

                        def tv(i, lo, hi):
                            return ptmp[i][:, :, 0:19 * 16].rearrange("p c (r s) -> p c r s", s=16)[:, :, 15 + lo:15 + hi, :]
                        rd = [('xme', t.idx, cc) for cc in (2 * gi, 2 * gi + 1)] + [('xmes_st', cc) for cc in (2 * gi, 2 * gi + 1)]
                    nn = n if t.kind == 'p' else 4
                    prev = None
                    for lv in range(1, L + 1):
                        lo = -(w - 2 ** lv)
                        sh = 2 ** (lv - 1)
                        dst = tv(lv % 2, lo, nn)
                        if lv == 1:
                            a_, b_ = xv(lo, nn), xv(lo - sh, nn - sh)
                            rds = rd
                        else:
                            a_, b_ = tv((lv - 1) % 2, lo, nn), tv((lv - 1) % 2, lo - sh, nn - sh)
                            rds = [('ptmp', (lv - 1) % 2)]
                        p.op('dve', I.tensor_tensor(out=dst, in0=a_, in1=b_, op=ALU.add),
                             reads=rds, writes=[('ptmp', lv % 2)])
                    pooled = tv(L % 2, 0, nn)
                    if t.kind == 'p':
                        dd = yb[:, cs, t.sl]
                    else:
                        dd = yb[:, cs, t.sl].rearrange("p c (r s) -> p c r s", s=16)
                    p.op('dve', I.scalar_tensor_tensor(out=dd, in0=pooled, scalar=1.0 / w, in1=xv(0, nn),
                                                                                                         op0=ALU.mult, op1=ALU.subtract),
                         reads=[('ptmp', L % 2)] + rd, writes=[('yb', t.idx, 2 * gi), ('yb', t.idx, 2 * gi + 1)])
                    if t.kind == 'p' and t.tok0 == 0:
                        icb = invcnt[:, gi, 0:w - 1].unsqueeze(1).broadcast_to([128, 2, w - 1])
                        p.op('dve', I.tensor_tensor(out=tv((L + 1) % 2, 0, w - 1), in0=tv(L % 2, 0, w - 1), in1=icb, op=ALU.mult),
                             reads=[('ptmp', L % 2), 'invcnt'], writes=[('ptmp', (L + 1) % 2)])
                        p.op('dve', I.tensor_tensor(out=yb[:, cs, t.off:t.off + w - 1], in0=tv((L + 1) % 2, 0, w - 1),
                                                                                                 in1=xv(0, w - 1), op=ALU.subtract),
                             reads=[('ptmp', (L + 1) % 2)] + rd, writes=[('yb', t.idx, 2 * gi), ('yb', t.idx, 2 * gi + 1)])
                for gi in range(4):
                    for dc in range(2):
                        m = 2 * gi + dc
                        b = bank()
                        mm(ps[b][:, 0:n], [(wv[:, gi, cc, dc * 128:(dc + 1) * 128], yb[:, 2 * gi + cc, t.sl]) for cc in range(2)],
                           reads=[('yb', t.idx, 2 * gi), ('yb', t.idx, 2 * gi + 1)] + wids(wid), writes=[('ps', b)])
                        p.op('dve', I.scalar_tensor_tensor(out=f[:, m, t.sl], in0=ps[b][:, 0:n], scalar=dg[:, m:m + 1],
                                                           in1=ones_f[:, 0:n], op0=ALU.mult, op1=ALU.mult),
                             reads=[('ps', b), 'dg', 'ones_f'], writes=[('f', t.idx, m)])
                        p.op('act', I.activation(out=sqb[t.idx][:, m, 0:n], in_=ps[b][:, 0:n], func=AF.Square,
                                                                               scale=vrow(m, R_DSC)),
                             reads=[('ps', b), ('vT', m)], writes=[('sqb', t.idx, m)])
            w_done()
            p.op('dve', I.tensor_copy(out=xm_halo[:, :, :], in_=xme_p[:, :, GP:GP + 15]),
                 reads=[('xme', NPT - 1, c) for c in range(8)], writes=['xm_halo'])
            if stiles:
                t = stiles[0]
                for blk in range(2):
                    r0 = 4 + blk * 8 if blk == 0 else 12
                    nr = 8 if blk == 0 else 7
                    store_T(lambda c, r0=r0, nr=nr: xme_s[:, c, r0:r0 + nr, :].rearrange("p a b -> p (a b)"),
                            [('xme', t.idx, c) for c in range(8)], nr * 16,
                            [(rr * 16, 16, o_sp_s.rearrange("(s r) n -> r s n", r=15)[r0 - 4 + rr]) for rr in range(nr)])
            if last:
                store_T(lambda c: xme_p[:, c, GP:GP + 15], [('xme', NPT - 1, c) for c in range(8)], 15, [(0, 15, o_sp_p[:, :])])
            for t in tiles:
                after(t)
            guard(['xme_halo'] + [('xme', ti, c) for ti in range(NTLX) for c in range(8)] + [('xmes_st', c) for c in range(8)] + [('ptmp', 0), ('ptmp', 1)])

        for g_ in range(NGR):
            tiles = [TileT('p', i, i * TS, TS, g_ * GP + i * TS) for i in range(NPT)]
            if g_ == 0:
                tiles.append(TileT('s', NPT, GP, NS, 0))
            for t in tiles:
                if t.kind == 'p':
                    for nb_ in range(t.n // 128):
                        r0 = t.tok0 + nb_ * 128
                        load_T(xp[r0:r0 + 128, :], 128,
                               lambda c, src, t=t, nb_=nb_: (h[:, c, t.off + nb_ * 128:t.off + (nb_ + 1) * 128], src),
                               ids('h', t.idx))
                else:
                    load_T(None, NS, lambda c, src, t=t: (h[:, c, t.sl], src), ids('h', t.idx),
                           src_list=[(tt * 16, 16, xs.rearrange("(s r) n -> r s n", r=4)[tt]) for tt in range(4)])
            assert DBG_LAYERS == 4 and DBG_SUB == ('f0', 'm', 'f1') and DBG_STOP == 99

            def pre_xn(t, row):
                prenorm(t, row, lambda c, t=t: xn[:, c, t.sl], ids('xn', t.idx))

            for t in tiles:
                pre_xn(t, R_NG + 0)
            for l in range(4):
                lrow = R_NG + l * 6

                def after_f0(t, l=l, lrow=lrow):
                    start_chain(t, lrow + 2 if l < 3 else None)

                def after_m(t, lrow=lrow):
                    start_chain(t, lrow + 4)

                def after_f1(t, l=l, lrow=lrow):
                    start_chain(t, lrow + 6 if l < 3 else None)

                ffn(g_, l, 0, tiles, after_f0)
                tap("L%dF0" % l, tiles, g_)
                [mixer_a, mixer_b, mixer_c, mixer_d][l](g_, tiles, after_m)
                tap("L%dM" % l, tiles, g_)
                ffn(g_, l, 1, tiles, after_f1)
                tap("L%dF1" % l, tiles, g_)
            flush_all()
            for t in tiles:
                if t.kind == 'p':
                    for nb_ in range(t.n // 128):
                        r0 = t.tok0 + nb_ * 128
                        store_T(lambda c, t=t, nb_=nb_: h[:, c, t.off + nb_ * 128:t.off + (nb_ + 1) * 128], ids('h', t.idx), 128,
                                [(0, 128, yp[r0:r0 + 128, :])])
                else:
                    store_T(lambda c, t=t: h[:, c, t.sl], ids('h', t.idx), NS,
                            [(tt * 16, 16, ys.rearrange("(s r) n -> r s n", r=4)[tt]) for tt in range(4)])
        assert DBG_STOP < 99 or WS.cur == len(plan), (WS.cur, len(plan))
        p.wait_all('sp', out_toks)
        print('stream sizes', {k: len(v) for k, v in p.streams.items()}, 'counts', p.ecount, 'dma', {str(k): v for k, v in p.dcount.items() if v > 2000})
        p.emit()
    return nc


_NC_CACHE = {}


def kernel(x_prompt, x_sample, state_conv_a, state_conv_b, state_pool,
           norm_g, ffn_w_in, ffn_w_down,
           a_w_in, a_conv_w, a_conv_b, a_ln_g, a_ln_b, a_w_out,
           b_w_in, b_conv_w, b_w_out,
           c_w_in, c_ln_g, c_ln_b, c_ws, c_bs, c_w_out,
           d_w_group, d_scale):
    f32 = lambda a: np.ascontiguousarray(np.asarray(a, dtype=np.float32))
    x_prompt, x_sample = f32(x_prompt), f32(x_sample)
    state_conv_a, state_conv_b, state_pool = f32(state_conv_a), f32(state_conv_b), f32(state_pool)
    vecs = np.concatenate([
        f32(norm_g).reshape(24, D), f32(a_conv_w).reshape(31, D), f32(a_conv_b).reshape(1, D),
        f32(a_ln_g).reshape(1, D), f32(a_ln_b).reshape(1, D), f32(b_conv_w).reshape(3, D),
        f32(c_ln_g).reshape(1, D), f32(c_ln_b).reshape(1, D), f32(d_scale).reshape(1, D)], axis=0)
    assert vecs.shape == (64, D)
    shared = {
        "vecs": f32(vecs), "ffn_w_in": f32(ffn_w_in), "ffn_w_down": f32(ffn_w_down),
        "a_w_in": f32(a_w_in), "a_w_out": f32(a_w_out), "b_w_in": f32(b_w_in), "b_w_out": f32(b_w_out),
        "c_w_in": f32(c_w_in), "c_w_out": f32(c_w_out), "c_ws": f32(c_ws), "c_bs": f32(c_bs),
        "d_w_group": f32(d_w_group),
    }
    in_maps = []
    for i in range(NCORES):
        m = dict(shared)
        m["xp"] = x_prompt[i]
        m["xs"] = x_sample[16 * i:16 * (i + 1)].reshape(NS, D)
        m["sa"] = state_conv_a[16 * i:16 * (i + 1)].reshape(16 * 30, D)
        m["sb"] = state_conv_b[16 * i:16 * (i + 1)].reshape(16 * 2, D)
        m["sp"] = state_pool[16 * i:16 * (i + 1)].reshape(16 * 15, D)
        in_maps.append(m)
    key = (TS, NPT, tuple(sorted(DEBUG_TAPS)) if DEBUG_TAPS else None, DBG_NG, DBG_LAYERS, DBG_SUB, DBG_STOP, DBG_VAR)
    if key not in _NC_CACHE:
        _NC_CACHE[key] = build_program()
    nc = _NC_CACHE[key]
    ncr = NCORES if DBG_CORES is None else DBG_CORES
    res = run_bass_kernel_spmd(nc, in_maps[:ncr], core_ids=list(range(ncr)))
    R = list(res.results)
    while len(R) < NCORES:
        R.append(R[0])

    def cat(name, shape):
        return np.stack([np.asarray(R[i][name], dtype=np.float32).reshape(shape) for i in range(NCORES)], axis=0)
    y_prompt = cat("yp", (SEQ, D))
    y_sample = cat("ys", (16, 4, D)).reshape(128, 4, D)
    sa_p = cat("o_sa_p", (30, D))
    sa_s = cat("o_sa_s", (16, 30, D)).reshape(128, 30, D)
    sb_p = cat("o_sb_p", (2, D))
    sb_s = cat("o_sb_s", (16, 2, D)).reshape(128, 2, D)
    sc_s = cat("o_sc_s", (16, 4, D)).reshape(128, 4, D)
    sp_p = cat("o_sp_p", (15, D))
    sp_s = cat("o_sp_s", (16, 15, D)).reshape(128, 15, D)
    if DEBUG_TAPS:
        kernel.last_taps = {nm: [np.asarray(R[i]["tap_" + nm]) for i in range(NCORES)] for nm in DEBUG_TAPS}
    return (y_prompt, y_sample, sa_p, sa_s, sb_p, sb_s, sc_s, sp_p, sp_s)
```

```python
import contextlib
import numpy as np
import concourse.bass as bass
import concourse.mybir as mybir
from concourse.bass_utils import run_bass_kernel_spmd

F32 = mybir.dt.float32
BF16 = mybir.dt.bfloat16
AF = mybir.ActivationFunctionType
ALU = mybir.AluOpType

SAME_ENGINE_SYNC = True
NCORES = 8
D = 1024
DFF = 2816
KC = 8
NJ = 22
SEQ = 2048
NS = 64
EPS = 1e-6
TS = 512
NPT = 1
GP = TS * NPT
NG = SEQ // GP
NT = GP + NS
RING = 4
SLOT_BYTES = 11264
DEBUG_TAPS = None
DBG_NG = None
DBG_CORES = None
DBG_LAYERS = 4
DBG_STOP = 99
DBG_VAR = ''
DBG_SUB = ('f0', 'm', 'f1')

R_NG = 0
R_ACW = 24
R_ACB = 55
R_ALG = 56
R_ALB = 57
R_BCW = 58
R_CLG = 61
R_CLB = 62
R_DSC = 63


class Prog:
    ENG = ('pe', 'act', 'dve', 'pool', 'sp')

    def __init__(self, nc, stack):
        self.nc = nc
        self.stack = stack
        self.streams = {e: [] for e in self.ENG}
        self.semh = {}
        self.ecount = {e: 0 for e in self.ENG}
        for e in self.ENG:
            self.semh[('e', e)] = stack.enter_context(nc.semaphore("es_" + e))
        self.waited = {e: {} for e in self.ENG}
        self.dcount = {}
        self.lastw = {}
        self.readers = {}

    def _deps(self, reads, writes, extra):
        deps = {}

        def add(t):
            if t is None:
                return
            sk, v = t
            if deps.get(sk, 0) < v:
                deps[sk] = v
        for t in extra:
            add(t)
        for b in reads:
            add(self.lastw.get(b))
        for b in writes:
            add(self.lastw.get(b))
            for sk, v in self.readers.get(b, {}).items():
                add((sk, v))
        return deps

    def _emit_waits(self, eng, deps):
        for sk in sorted(deps, key=str):
            v = deps[sk]
            if sk == ('e', eng) and (eng in ('pe', 'sp') or not SAME_ENGINE_SYNC):
                continue
            if self.waited[eng].get(sk, 0) >= v:
                continue
            self.waited[eng][sk] = v
            sem = self.semh[sk]
            self.streams[eng].append(lambda e, sem=sem, v=v: e.wait_ge(sem, v))

    def _record(self, tok, reads, writes):
        sk, v = tok
        for b in writes:
            self.lastw[b] = tok
            self.readers[b] = {}
        for b in reads:
            d = self.readers.setdefault(b, {})
            if d.get(sk, 0) < v:
                d[sk] = v

    def op(self, eng, fns, reads=(), writes=(), extra=()):
        writes = list(writes) + [b for b in reads if isinstance(b, tuple) and b[0] == 'ps']
        deps = self._deps(reads, writes, extra)
        self._emit_waits(eng, deps)
        if callable(fns):
            fns = [fns]
        self.ecount[eng] += 1
        c = self.ecount[eng]
        sem = self.semh[('e', eng)]
        for f in fns[:-1]:
            self.streams[eng].append(f)
        last = fns[-1]
        self.streams[eng].append(lambda e, last=last, sem=sem: last(e).then_inc(sem, 1))
        tok = (('e', eng), c)
        self._record(tok, reads, writes)
        return tok

    def dma(self, q, out, in_, key, reads=(), writes=(), extra=()):
        deps = self._deps(reads, writes, extra)
        self._emit_waits(q, deps)
        sk = ('d', key)
        if sk not in self.semh:
            self.semh[sk] = self.stack.enter_context(self.nc.semaphore("ds_%d" % len(self.semh)))
            self.dcount[sk] = 0
        self.dcount[sk] += 16
        sem = self.semh[sk]
        self.streams[q].append(lambda e, out=out, in_=in_, sem=sem: e.dma_start(out=out, in_=in_).then_inc(sem, 16))
        tok = (sk, self.dcount[sk])
        self._record(tok, reads, writes)
        return tok

    def dma_multi(self, q, pairs, key, reads=(), writes=()):
        deps = self._deps(reads, writes, ())
        self._emit_waits(q, deps)
        sk = ('d', key)
        if sk not in self.semh:
            self.semh[sk] = self.stack.enter_context(self.nc.semaphore("ds_%d" % len(self.semh)))
            self.dcount[sk] = 0
        sem = self.semh[sk]
        for out, in_ in pairs:
            self.dcount[sk] += 16
            self.streams[q].append(lambda e, out=out, in_=in_, sem=sem: e.dma_start(out=out, in_=in_).then_inc(sem, 16))
        tok = (sk, self.dcount[sk])
        self._record(tok, reads, writes)
        return tok

    def wait_all(self, eng, toks):
        deps = {}
        for sk, v in toks:
            if deps.get(sk, 0) < v:
                deps[sk] = v
        self._emit_waits(eng, deps)

    def barrier(self):
        engs = ('pe', 'act', 'dve', 'pool')
        toks = [(('e', e), self.ecount[e]) for e in engs if self.ecount[e] > 0]
        for e in engs:
            self.wait_all(e, [t for t in toks if t[0] != ('e', e)])

    def emit(self):
        streams = self.streams
        with self.nc.Block() as block:
            @block.tensor
            def _(e):
                for f in streams['pe']:
                    f(e)

            @block.scalar
            def _(e):
                for f in streams['act']:
                    f(e)

            @block.vector
            def _(e):
                for f in streams['dve']:
                    f(e)

            @block.gpsimd
            def _(e):
                for f in streams['pool']:
                    f(e)

            @block.sync
            def _(e):
                for f in streams['sp']:
                    f(e)


class _I:
    def __getattr__(self, name):
        def mk(*args, **kwargs):
            return lambda e: getattr(e, name)(*args, **kwargs)
        return mk


I = _I()


class Arena:
    def __init__(self, big, nbytes):
        self.big = big
        self.nbytes = nbytes
        self.top = 0

    def at(self, off, shape, dt, parts=128):
        n = int(np.prod(shape))
        esz = 4 if dt == F32 else 2
        nb = n * esz
        assert off % 4 == 0 and nb % 4 == 0, (off, nb)
        assert off + nb <= self.nbytes, ("SBUF overflow", off, nb, self.nbytes)
        v = self.big[0:parts, off // 4:(off + nb) // 4]
        if dt != F32:
            v = v.bitcast(dt)
        if len(shape) == 2:
            v = v.rearrange("p (a b) -> p a b", a=shape[0])
        elif len(shape) == 3:
            v = v.rearrange("p (a b c) -> p a b c", a=shape[0], b=shape[1])
        elif len(shape) == 4:
            v = v.rearrange("p (a b c d) -> p a b c d", a=shape[0], b=shape[1], c=shape[2])
        return v

    def alloc(self, shape, dt, parts=128):
        n = int(np.prod(shape))
        esz = 4 if dt == F32 else 2
        nb = (n * esz + 31) // 32 * 32
        off = self.top
        self.top += nb
        return self.at(off, shape, dt, parts), off

    def reserve(self, nbytes):
        off = self.top
        self.top += (nbytes + 31) // 32 * 32
        assert self.top <= self.nbytes, ("SBUF overflow", self.top)
        return off


class TileT:
    def __init__(self, kind, idx, off, n, tok0):
        self.kind = kind
        self.idx = idx
        self.off = off
        self.n = n
        self.tok0 = tok0

    @property
    def sl(self):
        return slice(self.off, self.off + self.n)


def ids(name, t, n=8):
    return [(name, t, c) for c in range(n)]


def build_program():
    nc = bass.Bass("TRN2", target_bir_lowering=False)

    def din(name, shape):
        return nc.dram_tensor(name, shape, F32, kind="ExternalInput").ap()

    def dout(name, shape):
        return nc.dram_tensor(name, shape, F32, kind="ExternalOutput").ap()

    xp = din("xp", [SEQ, D])
    xs = din("xs", [NS, D])
    sa = din("sa", [16 * 30, D])
    sbi = din("sb", [16 * 2, D])
    spi = din("sp", [16 * 15, D])
    vecs = din("vecs", [64, D])
    ffn_w_in = din("ffn_w_in", [4, 2, D, 2 * DFF])
    ffn_w_down = din("ffn_w_down", [4, 2, DFF, D])
    a_w_in = din("a_w_in", [D, 2 * D])
    a_w_out = din("a_w_out", [D, D])
    b_w_in = din("b_w_in", [D, 3 * D])
    b_w_out = din("b_w_out", [D, D])
    c_w_in = din("c_w_in", [D, 2 * D])
    c_w_out = din("c_w_out", [D, D])
    c_ws = din("c_ws", [8, 128, 128])
    c_bs = din("c_bs", [8, 128])
    d_w_group = din("d_w_group", [4, 256, 256])

    yp = dout("yp", [SEQ, D])
    ys = dout("ys", [NS, D])
    o_sa_p = dout("o_sa_p", [30, D])
    o_sa_s = dout("o_sa_s", [16 * 30, D])
    o_sb_p = dout("o_sb_p", [2, D])
    o_sb_s = dout("o_sb_s", [16 * 2, D])
    o_sc_s = dout("o_sc_s", [NS, D])
    o_sp_p = dout("o_sp_p", [15, D])
    o_sp_s = dout("o_sp_s", [16 * 15, D])
    taps = {}
    if DEBUG_TAPS:
        for nm in sorted(DEBUG_TAPS):
            taps[nm] = dout("tap_" + nm, [128, 8, SEQ + NS])

    with contextlib.ExitStack() as st:
        p = Prog(nc, st)
        SB_BYTES = 211968
        big = st.enter_context(nc.sbuf_tensor("big", [128, SB_BYTES // 4], F32))
        A = Arena(big, SB_BYTES)
        ps = [st.enter_context(nc.psum_tensor("ps%d" % i, [128, 512], F32)) for i in range(8)]
        bank_rr = [0]

        def bank():
            b = bank_rr[0]
            bank_rr[0] = (b + 1) % 8
            return b

        ident, _ = A.alloc([128], F32)
        identb, _ = A.alloc([128], BF16)
        ones_b, _ = A.alloc([128], BF16)
        neg_half, _ = A.alloc([TS], F32)
        ones_f, _ = A.alloc([TS], F32)
        vT, _ = A.alloc([8, 64], F32)
        hg, _ = A.alloc([8, 8], F32)
        dg, _ = A.alloc([8], F32)
        wmT, _ = A.alloc([8, 128], BF16)
        bsb, _ = A.alloc([8, 128], F32)
        wsb, _ = A.alloc([8, 16], F32)
        bs4, _ = A.alloc([8, 4], F32)
        invcnt, _ = A.alloc([4, 16], F32)
        eps_t, _ = A.alloc([8], F32)
        dummy, _ = A.alloc([8], F32)
        glu_halo, _ = A.alloc([8, 30], BF16)
        z_halo, _ = A.alloc([8, 2], F32)
        xm_halo, _ = A.alloc([8, 15], F32)
        gluf_tail, _ = A.alloc([8, 30], F32)
        h, _ = A.alloc([8, NT], F32)
        xn, _ = A.alloc([8, NT], BF16)
        f, _ = A.alloc([8, NT], F32)
        yb, _ = A.alloc([8, NT], BF16)
        NTL = NPT + 1
        tsz = [TS] * NPT + [NS]
        sqb = [A.alloc([8, tsz[i]], BF16)[0] for i in range(NTL)]
        cbb = [A.alloc([8, tsz[i]], BF16)[0] for i in range(NTL)]
        st_t = [A.alloc([max(tsz[i], 8)], F32)[0] for i in range(NTL)]
        st_r = [A.alloc([tsz[i]], F32)[0] for i in range(NTL)]
        st_m = [A.alloc([tsz[i]], F32)[0] for i in range(NTL)]
        st_n = [A.alloc([tsz[i]], F32)[0] for i in range(NTL)]
        NSG = 3
        sg = [A.alloc([TS], F32)[0] for _ in range(NSG)]
        sg2 = [A.alloc([TS], F32)[0] for _ in range(NSG)]
        stage_in = [A.alloc([D], F32)[0] for _ in range(2)]
        stage_out = [A.alloc([D], F32)[0] for _ in range(2)]
        ring_off = [A.reserve(SLOT_BYTES) for _ in range(RING)]
        R0 = A.reserve(36 * 1024)
        assert A.top <= SB_BYTES, A.top
        print('SBUF bytes used', A.top, 'of', SB_BYTES)
        gbuf = A.at(R0, [NJ, NT], BF16)
        o = R0
        glub = A.at(o, [8, 30 + GP], BF16); o += 8 * (30 + GP) * 2
        exts = A.at(o, [8, 34, 16], BF16); o += 8 * 34 * 16 * 2
        glus = A.at(o, [8, NS], F32); o += 8 * NS * 4
        diag = []
        for i in range(2):
            diag.append(A.at(o, [31, 128], BF16)); o += 31 * 128 * 2
        assert o <= R0 + 36 * 1024, o - R0
        o = R0
        zp = A.at(o, [8, 2 + GP], F32); o += 8 * (2 + GP) * 4
        zs = A.at(o, [8, 6, 16], F32); o += 8 * 6 * 16 * 4
        assert o <= R0 + 36 * 1024
        o = R0
        ubuf = A.at(o, [8, NT], F32); o += 8 * NT * 4
        vtok = []
        for i in range(GP // 128):
            vtok.append(A.at(o, [D], BF16)); o += D * 2
        accs = A.at(o, [NS], F32); o += NS * 4
        assert o <= R0 + 36 * 1024, o - R0
        o = R0
        xme_p = A.at(o, [8, 15 + GP], F32); o += 8 * (15 + GP) * 4
        xme_s = A.at(o, [8, 19, 16], F32); o += 8 * 19 * 16 * 4
        ptmp = []
        for i in range(2):
            ptmp.append(A.at(o, [2, max(15 + TS, 304)], F32)); o += 2 * max(15 + TS, 304) * 4
        assert o <= R0 + 36 * 1024, o - R0

        def vrow(c, r):
            return vT[:, c, r:r + 1]

        si_rr = [0]
        so_rr = [0]
        out_toks = []

        def load_T(src, R, dst_fn, dst_ids, eng='dve', src_list=None):
            s = si_rr[0]
            si_rr[0] ^= 1
            stg = stage_in[s]
            sid = ('stgin', s)
            if src_list is None:
                src_list = [(0, R, src)]
            for (r0, nr, ap) in src_list:
                p.dma('sp', stg[r0:r0 + nr, :], ap, ('ldin', s, r0), writes=[sid])
            for half in range(2):
                b = bank()
                fns = [(I.transpose(ps[b][:, k4 * R:(k4 + 1) * R],
                                                          stg[0:R, (half * 4 + k4) * 128:(half * 4 + k4 + 1) * 128],
                                                          ident[0:R, 0:R])) for k4 in range(4)]
                p.op('pe', fns, reads=[sid, 'ident'], writes=[('ps', b)])
                for k4 in range(4):
                    c = half * 4 + k4
                    dst, src_view = dst_fn(c, ps[b][:, k4 * R:(k4 + 1) * R])
                    p.op(eng, I.tensor_copy(out=dst, in_=src_view),
                         reads=[('ps', b)], writes=[dst_ids[c]])

        def store_T(src_fn, src_ids, R, dsts, eng='act'):
            s = so_rr[0]
            so_rr[0] ^= 1
            stg = stage_out[s]
            sid = ('stgout', s)
            for half in range(2):
                b = bank()
                fns = [(I.transpose(ps[b][0:R, k4 * 128:(k4 + 1) * 128],
                                                          src_fn(half * 4 + k4), ident[:, :])) for k4 in range(4)]
                p.op('pe', fns, reads=[src_ids[half * 4 + k4] for k4 in range(4)] + ['ident'], writes=[('ps', b)])
                if eng == 'act':
                    p.op('act', I.activation(out=stg[0:R, half * 512:(half + 1) * 512],
                                                                       in_=ps[b][0:R, :], func=AF.Copy),
                         reads=[('ps', b)], writes=[sid])
                else:
                    p.op('dve', I.tensor_copy(out=stg[0:R, half * 512:(half + 1) * 512],
                                                                        in_=ps[b][0:R, :]),
                         reads=[('ps', b)], writes=[sid])
            for (r0, nr, ap) in dsts:
                out_toks.append(p.dma('sp', ap, stg[r0:r0 + nr, :], ('st', s, r0), reads=[sid]))

        def mm(bank_ap, pairs, reads, writes):
            n = len(pairs)
            fns = [(I.matmul(bank_ap, lhsT=l, rhs=r, start=(i == 0), stop=(i == n - 1)))
                   for i, (l, r) in enumerate(pairs)]
            return p.op('pe', fns, reads=reads, writes=writes)

        def wplan_group():
            plan = []

            def cols(w2d, c0, width):
                return w2d.rearrange("(kc p) n -> p kc n", p=128)[:, :, c0:c0 + width]

            for l in range(DBG_LAYERS):
                for s_ in range(2):
                    if s_ == 1 and 'm' in DBG_SUB:
                        if l == 0:
                            for q in range(2):
                                plan.append((('a_in', q, 0), cols(a_w_in, q * 512, 512), [8, 512]))
                                plan.append((('a_in', q, 1), cols(a_w_in, D + q * 512, 512), [8, 512]))
                            for q in range(2):
                                plan.append((('a_out', q), cols(a_w_out, q * 512, 512), [8, 512]))
                        elif l == 1:
                            for q in range(2):
                                for part in range(3):
                                    plan.append((('b_in', q, part), cols(b_w_in, part * D + q * 512, 512), [8, 512]))
                            for q in range(2):
                                plan.append((('b_out', q), cols(b_w_out, q * 512, 512), [8, 512]))
                        elif l == 2:
                            for q in range(2):
                                plan.append((('c_in', q, 0), cols(c_w_in, q * 512, 512), [8, 512]))
                                plan.append((('c_in', q, 1), cols(c_w_in, D + q * 512, 512), [8, 512]))
                            for q in range(2):
                                plan.append((('c_out', q), cols(c_w_out, q * 512, 512), [8, 512]))
                        else:
                            plan.append((('d_w', 0), d_w_group.rearrange("g (cc p) n -> p g cc n", p=128), [4, 2, 256]))
                    if ('f%d' % s_) not in DBG_SUB:
                        continue
                    for q in range(6):
                        width = 512 if q < 5 else 256
                        plan.append((('f_in', l, s_, q, 0), cols(ffn_w_in[l, s_], q * 512, width), [8, width]))
                        plan.append((('f_in', l, s_, q, 1), cols(ffn_w_in[l, s_], DFF + q * 512, width), [8, width]))
                    for q in range(2):
                        for kh in range(2):
                            plan.append((('f_dn', l, s_, q, kh),
                                         ffn_w_down[l, s_].rearrange("(kc p) n -> p kc n", p=128)[:, kh * 11:(kh + 1) * 11, q * 512:(q + 1) * 512], [11, 512]))
            return plan

        plan = []
        NGR = NG if DBG_NG is None else DBG_NG
        for g_ in range(NGR):
            plan += [((g_,) + k, ap, shp) for (k, ap, shp) in wplan_group()]

        NBLK = len(wplan_group())
        SLOT_ELEMS = SLOT_BYTES // 2
        wscr = None
        if NGR > 1:
            wscr = nc.dram_tensor("wscr", [NBLK, 128, SLOT_ELEMS], BF16, kind="Internal").ap()

        class WS:
            nxt = 0
            cur = 0
            views = {}

        def w_issue():
            k = WS.nxt
            key, src, shp = plan[k]
            s = k % RING
            kk = k % NBLK
            nel = int(np.prod(shp))
            view = A.at(ring_off[s], shp, BF16)
            flat = A.at(ring_off[s], [nel], BF16)
            if key[0] == 0:
                if shp[0] > 8:
                    pairs = [(view[:, k0:min(k0 + 8, shp[0]), :], src[:, k0:min(k0 + 8, shp[0]), :]) for k0 in range(0, shp[0], 8)]
                else:
                    pairs = [(view, src)]
                p.dma_multi('pool', pairs, ('w', s), writes=[('wslot', s)])
                if wscr is not None:
                    p.dma('sp', wscr[kk][:, 0:nel], flat, ('wb', s), reads=[('wslot', s)], writes=[('wscr', kk)])
            else:
                p.dma('sp', flat, wscr[kk][:, 0:nel], ('wh', s), reads=[('wscr', kk)], writes=[('wslot', s)])
            WS.views[k] = view
            WS.nxt += 1

        def w_getn(keys):
            out = []
            for i, key in enumerate(keys):
                k = WS.cur + i
                assert plan[k][0] == key, (plan[k][0], key)
                while WS.nxt <= k:
                    w_issue()
                out.append((WS.views[k], ('wslot', k % RING)))
            return out

        def w_get(key):
            return w_getn([key])[0]

        def wids(wid):
            return [wid]

        def w_done(n=1):
            for _ in range(n):
                WS.cur += 1
            while WS.nxt < len(plan) and WS.nxt < WS.cur + RING:
                w_issue()

        p.op('pool', I.memset(ident[:, :], 0.0), writes=['ident'])
        p.op('pool', I.affine_select(out=ident[:, :], in_=ident[:, :], compare_op=ALU.not_equal, fill=1.0,
                                     base=0, pattern=[[-1, 128]], channel_multiplier=1),
             reads=['ident'], writes=['ident'])
        p.op('dve', I.tensor_copy(out=identb[:, :], in_=ident[:, :]), reads=['ident'], writes=['identb'])
        p.op('dve', I.memset(ones_b[:, :], 1.0 / 1024.0), writes=['ones'])
        p.op('dve', I.memset(neg_half[:, :], -0.5), writes=['nh'])
        p.op('dve', I.memset(ones_f[:, :], 1.0), writes=['ones_f'])
        p.op('dve', I.memset(eps_t[:, :], EPS), writes=['eps_t'])
        p.op('dve', I.memset(glu_halo[:, :, :], 0.0), writes=['glu_halo'])
        p.op('dve', I.memset(z_halo[:, :, :], 0.0), writes=['z_halo'])
        p.op('dve', I.memset(xm_halo[:, :, :], 0.0), writes=['xm_halo'])
        fns = []
        for gi, w in enumerate((2, 4, 8, 16)):
            for t in range(16):
                fns.append(I.memset(invcnt[:, gi, t:t + 1], 1.0 / min(w, t + 1)))
        p.op('dve', fns, writes=['invcnt'])
        for i in range(min(RING, len(plan))):
            w_issue()
        vids = [('vT', c) for c in range(8)]
        load_T(vecs[:, :], 64, lambda c, src: (vT[:, c, :], src), vids)
        for l in range(4):
            for wi, i in enumerate((1, 5)):
                p.op('dve', I.tensor_scalar(out=hg[:, :, l * 2 + wi], in0=vT[:, :, R_NG + l * 6 + i],
                                                                      scalar1=0.5, scalar2=None, op0=ALU.mult),
                     reads=vids, writes=['hg'])
        p.op('dve', I.tensor_tensor(out=dg[:, :], in0=vT[:, :, R_DSC], in1=vT[:, :, R_NG + 3 * 6 + 3], op=ALU.mult),
             reads=vids, writes=['dg'])
        for hh in range(8):
            s = si_rr[0]
            si_rr[0] ^= 1
            stg = stage_in[s]
            sid = ('stgin', s)
            p.dma('sp', stg[:, 0:128], c_ws[hh], ('ldin', s, 0), writes=[sid])
            b = bank()
            p.op('pe', I.transpose(ps[b][:, 0:128], stg[:, 0:128], ident[:, :]),
                 reads=[sid, 'ident'], writes=[('ps', b)])
            p.op('dve', I.tensor_copy(out=wmT[:, hh, :], in_=ps[b][:, 0:128]),
                 reads=[('ps', b)], writes=[('wmT', hh)])
            p.op('pool', I.affine_select(out=wmT[:, hh, :], in_=wmT[:, hh, :], compare_op=ALU.is_ge, fill=0.0,
                                                          base=0, pattern=[[1, 128]], channel_multiplier=-1),
                 reads=[('wmT', hh)], writes=[('wmT', hh)])
        with nc.allow_non_contiguous_dma(reason="tiny broadcast loads"):
            cpairs = []
            for hh in range(8):
                cpairs.append((bsb[:, hh, :], c_bs[hh:hh + 1, :].partition_broadcast(128)))
                cpairs.append((bs4[:, hh, :], c_bs[hh:hh + 1, 0:4].partition_broadcast(128)))
                for i in range(4):
                    cpairs.append((wsb[:, hh, i * 4:(i + 1) * 4], c_ws[hh, i:i + 1, 0:4].partition_broadcast(128)))
            p.dma_multi('sp', cpairs, 'cst', writes=['cst'])
        out_toks.append(p.dma('sp', o_sa_s.rearrange("(s r) n -> s r n", r=30)[:, 0:26, :],
                              sa.rearrange("(s r) n -> s r n", r=30)[:, 4:30, :], 'd2d'))

        def tap(name, tiles, g_):
            if not DEBUG_TAPS or name not in DEBUG_TAPS:
                return
            flush_all()
            for t in tiles:
                col = g_ * GP + t.off if t.kind == 'p' else SEQ
                out_toks.append(p.dma('sp', taps[name][:, :, col:col + t.n], h[:, :, t.sl], ('tap', name, t.idx), reads=ids('h', t.idx)))

        NTLX = NPT + 1
        G_IDS = [('g', ti, j) for ti in range(NTLX) for j in range(NJ)]

        def guard(write_ids):
            tok = p.op('dve', I.memset(dummy[:, 0:1], 0.0), writes=list(write_ids) + ['dummy'])
            p.wait_all('act', [tok])
            p.wait_all('pool', [tok])

        def tok_sums(t, srcs, reads):
            n = t.n
            b = bank()
            fns = []
            for i, src in enumerate(srcs):
                for blk in range((n + 127) // 128):
                    P = min(128, n - blk * 128)
                    for c in range(8):
                        fns.append(I.matmul(ps[b][0:P, i * 4 + blk:i * 4 + blk + 1], lhsT=src[:, c, blk * 128:blk * 128 + P],
                                            rhs=ones_b[:, 0:1], start=(c == 0), stop=(c == 7)))
            p.op('pe', fns, reads=reads + ['ones'], writes=[('ps', b)])
            return b

        def bcast_tok(t, col, dst, dst_id):
            n = t.n
            nblk = (n + 127) // 128
            b2 = bank()
            fns = []
            for blk in range(nblk):
                P = min(128, n - blk * 128)
                fns.append(I.matmul(ps[b2][:, blk * 128:blk * 128 + P],
                                    lhsT=st_t[t.idx][0:P, col + blk:col + blk + 1].broadcast_to([P, 128]),
                                    rhs=ident[0:P, 0:P], start=True, stop=True))
            p.op('pe', fns, reads=[('st_t', t.idx), 'ident'], writes=[('ps', b2)])
            if dst is None:
                return ps[b2][:, 0:n], ('ps', b2)
            p.op('act', I.activation(out=dst, in_=ps[b2][:, 0:n], func=AF.Copy), reads=[('ps', b2)], writes=[dst_id])
            return dst, dst_id

        def stats_rstd_gen(t, sq_src, sq_ids):
            n = t.n
            nblk = (n + 127) // 128
            P = min(128, n)
            b = tok_sums(t, [sq_src], sq_ids)
            p.op('dve', I.tensor_scalar(out=st_t[t.idx][0:P, 0:nblk], in0=ps[b][0:P, 0:nblk], scalar1=EPS, scalar2=None, op0=ALU.add),
                 reads=[('ps', b)], writes=[('st_t', t.idx)])
            p.op('pool', I.tensor_tensor(out=st_t[t.idx][0:P, 0:nblk], in0=st_t[t.idx][0:P, 0:nblk], in1=neg_half[0:P, 0:nblk], op=ALU.pow),
                 reads=[('st_t', t.idx), 'nh'], writes=[('st_t', t.idx)])
            yield
            yield
            RST[t.idx] = bcast_tok(t, 0, None, None)

        RST = {}

        def prenorm_gen(t, grow, dst_fn, dst_ids, split=False):
            n = t.n
            for hf in range(2):
                cs = slice(4 * hf, 4 * hf + 4)
                p.op('act', I.activation(out=sqb[t.idx][:, cs, 0:n], in_=h[:, cs, t.sl], func=AF.Square),
                     reads=ids('h', t.idx)[cs], writes=ids('sqb', t.idx)[cs])
            yield
            yield
            yield from stats_rstd_gen(t, sqb[t.idx], ids('sqb', t.idx))
            yield
            rap, rid = RST[t.idx]
            if split:
                cs = slice(0, 4)
                p.op('dve', I.tensor_tensor(out=f[:, cs, t.sl], in0=h[:, cs, t.sl], in1=rap.unsqueeze(1).broadcast_to([128, 4, n]), op=ALU.mult),
                     reads=ids('h', t.idx)[cs] + [rid], writes=ids('f', t.idx)[cs])
            for c in range(8):
                if split and c < 4:
                    p.op('act', I.activation(out=dst_fn(c), in_=f[:, c, t.sl], func=AF.Copy, scale=vrow(c, grow)),
                         reads=[('f', t.idx, c), ('vT', c)], writes=[dst_ids[c]])
                else:
                    p.op('dve', I.scalar_tensor_tensor(out=dst_fn(c), in0=h[:, c, t.sl], scalar=vrow(c, grow),
                                                       in1=rap, op0=ALU.mult, op1=ALU.mult),
                         reads=[('h', t.idx, c), rid, ('vT', c)], writes=[dst_ids[c]])

        def prenorm(t, grow, dst_fn, dst_ids):
            for _ in prenorm_gen(t, grow, dst_fn, dst_ids):
                pass

        def postnorm_gen(t):
            n = t.n
            yield
            yield from stats_rstd_gen(t, sqb[t.idx], ids('sqb', t.idx))
            yield
            rap, rid = RST[t.idx]
            rb = rap.unsqueeze(1).broadcast_to([128, 4, n])
            for hf in range(2):
                cs = slice(4 * hf, 4 * hf + 4)
                p.op('dve', I.tensor_tensor(out=f[:, cs, t.sl], in0=f[:, cs, t.sl], in1=rb, op=ALU.mult),
                     reads=ids('f', t.idx)[cs] + [rid], writes=ids('f', t.idx)[cs])
                p.op('dve', I.tensor_tensor(out=h[:, cs, t.sl], in0=h[:, cs, t.sl], in1=f[:, cs, t.sl], op=ALU.add),
                     reads=ids('f', t.idx)[cs] + ids('h', t.idx)[cs], writes=ids('h', t.idx)[cs])

        def postnorm_residual(t):
            for _ in postnorm_gen(t):
                pass

        pending = {}

        def chain_gen(t, pre_row):
            yield from postnorm_gen(t)
            if pre_row is not None:
                yield from prenorm_gen(t, pre_row, lambda c, t=t: xn[:, c, t.sl], ids('xn', t.idx), split=True)

        def start_chain(t, pre_row):
            need(t)
            pending[t.idx] = chain_gen(t, pre_row)

        def pump():
            for k in list(pending):
                try:
                    next(pending[k])
                except StopIteration:
                    del pending[k]

        def need(t):
            g = pending.pop(t.idx, None)
            if g is not None:
                for _ in g:
                    pass

        def flush_all():
            for k in list(pending):
                g = pending.pop(k)
                for _ in g:
                    pass

        def out_proj(g_, tiles, wkey, gain_fn, after):
            for q in range(2):
                wv, wid = w_get((g_,) + wkey + (q,))
                order = [(mmi, t) for mmi in range(4) for t in tiles] if q == 0 else [(mmi, t) for t in tiles for mmi in range(4)]
                for (mmi, t) in order:
                    m = q * 4 + mmi
                    b = bank()
                    mm(ps[b][:, 0:t.n], [(wv[:, kc, mmi * 128:(mmi + 1) * 128], yb[:, kc, t.sl]) for kc in range(8)],
                       reads=ids('yb', t.idx) + wids(wid), writes=[('ps', b)])
                    pump()
                    p.op('dve', I.scalar_tensor_tensor(out=f[:, m, t.sl], in0=ps[b][:, 0:t.n], scalar=gain_fn(m),
                                                       in1=ones_f[:, 0:t.n], op0=ALU.mult, op1=ALU.mult),
                         reads=[('ps', b), 'hg', 'dg', 'ones_f'] + vids, writes=[('f', t.idx, m)])
                    p.op('act', I.activation(out=sqb[t.idx][:, m, 0:t.n], in_=ps[b][:, 0:t.n], func=AF.Square),
                         reads=[('ps', b)], writes=[('sqb', t.idx, m)])
                    if q == 1 and mmi == 3:
                        after(t)
                w_done()

        sg_rr = [0]

        def ffn(g_, l, s_, tiles, after):
            for q in range(6):
                (gv, gid), (uv, uid) = w_getn([(g_, 'f_in', l, s_, q, 0), (g_, 'f_in', l, s_, q, 1)])
                njj = 4 if q < 5 else 2
                order = [(jj, t) for t in tiles for jj in range(njj)] if q == 0 else [(jj, t) for jj in range(njj) for t in tiles]
                for (jj, t) in order:
                    j = q * 4 + jj
                    n = t.n
                    need(t)
                    bg_ = bank()
                    mm(ps[bg_][:, 0:n], [(gv[:, kc, jj * 128:(jj + 1) * 128], xn[:, kc, t.sl]) for kc in range(8)],
                       reads=ids('xn', t.idx) + [gid], writes=[('ps', bg_)])
                    bu_ = bank()
                    mm(ps[bu_][:, 0:n], [(uv[:, kc, jj * 128:(jj + 1) * 128], xn[:, kc, t.sl]) for kc in range(8)],
                       reads=ids('xn', t.idx) + [uid], writes=[('ps', bu_)])
                    pump()
                    k = sg_rr[0]
                    sg_rr[0] = (k + 1) % NSG
                    p.op('act', I.activation(out=sg[k][:, 0:n], in_=ps[bg_][:, 0:n], func=AF.Silu),
                         reads=[('ps', bg_)], writes=[('sg', k)])
                    p.op('dve', I.tensor_tensor(out=gbuf[:, j, t.sl], in0=ps[bu_][:, 0:n], in1=sg[k][:, 0:n], op=ALU.mult),
                         reads=[('ps', bu_), ('sg', k)], writes=[('g', t.idx, j)])
                w_done(2)
            gi = l * 2 + s_
            for q in range(2):
                wvs = w_getn([(g_, 'f_dn', l, s_, q, 0), (g_, 'f_dn', l, s_, q, 1)])
                order = [(mmi, t) for mmi in range(4) for t in tiles] if q == 0 else [(mmi, t) for t in tiles for mmi in range(4)]
                for (mmi, t) in order:
                    m = q * 4 + mmi
                    n = t.n
                    b = bank()
                    mm(ps[b][:, 0:n], [(wvs[j // 11][0][:, j % 11, mmi * 128:(mmi + 1) * 128], gbuf[:, j, t.sl]) for j in range(NJ)],
                       reads=ids('g', t.idx, NJ) + [wvs[0][1], wvs[1][1]], writes=[('ps', b)])
                    pump()
                    p.op('dve', I.scalar_tensor_tensor(out=f[:, m, t.sl], in0=ps[b][:, 0:n], scalar=hg[:, m, gi:gi + 1],
                                                       in1=ones_f[:, 0:n], op0=ALU.mult, op1=ALU.mult),
                         reads=[('ps', b), 'hg', 'ones_f'], writes=[('f', t.idx, m)])
                    p.op('act', I.activation(out=sqb[t.idx][:, m, 0:n], in_=ps[b][:, 0:n], func=AF.Square),
                         reads=[('ps', b)], writes=[('sqb', t.idx, m)])
                    if q == 1 and mmi == 3:
                        after(t)
                w_done(2)

        def ln_stats(t, src_f_ids):
            n = t.n
            nblk = (n + 127) // 128
            P = min(128, n)
            S = st_t[t.idx]
            b = tok_sums(t, [cbb[t.idx], sqb[t.idx]], ids('cbb', t.idx) + ids('sqb', t.idx))
            p.op('dve', I.tensor_copy(out=S[0:P, 8:8 + nblk], in_=ps[b][0:P, 0:nblk]), reads=[('ps', b)], writes=[('st_t', t.idx)])
            p.op('dve', I.tensor_tensor(out=S[0:P, 16:16 + nblk], in0=S[0:P, 8:8 + nblk], in1=S[0:P, 8:8 + nblk], op=ALU.mult),
                 reads=[('st_t', t.idx)], writes=[('st_t', t.idx)])
            p.op('dve', I.scalar_tensor_tensor(out=S[0:P, 0:nblk], in0=ps[b][0:P, 4:4 + nblk], scalar=EPS, in1=S[0:P, 16:16 + nblk],
                                               op0=ALU.add, op1=ALU.subtract),
                 reads=[('ps', b), ('st_t', t.idx)], writes=[('st_t', t.idx)])
            p.op('pool', I.tensor_tensor(out=S[0:P, 0:nblk], in0=S[0:P, 0:nblk], in1=neg_half[0:P, 0:nblk], op=ALU.pow),
                 reads=[('st_t', t.idx), 'nh'], writes=[('st_t', t.idx)])
            p.op('dve', I.scalar_tensor_tensor(out=S[0:P, 16:16 + nblk], in0=S[0:P, 8:8 + nblk], scalar=-1.0, in1=S[0:P, 0:nblk],
                                               op0=ALU.mult, op1=ALU.mult),
                 reads=[('st_t', t.idx)], writes=[('st_t', t.idx)])
            bcast_tok(t, 0, st_r[t.idx][:, 0:n], ('st_r', t.idx))
            bcast_tok(t, 16, st_n[t.idx][:, 0:n], ('st_n', t.idx))
            rb = st_r[t.idx][:, 0:n].unsqueeze(1).broadcast_to([128, 8, n])
            nb_ = st_n[t.idx][:, 0:n].unsqueeze(1).broadcast_to([128, 8, n])
            p.op('dve', I.tensor_tensor(out=f[:, :, t.sl], in0=f[:, :, t.sl], in1=rb, op=ALU.mult),
                 reads=ids('f', t.idx) + [('st_r', t.idx)], writes=ids('f', t.idx))
            p.op('dve', I.tensor_tensor(out=f[:, :, t.sl], in0=f[:, :, t.sl], in1=nb_, op=ALU.add),
                 reads=ids('f', t.idx) + [('st_n', t.idx)], writes=ids('f', t.idx))

        def mixer_a(g_, tiles, after):
            lrow = R_NG + 0 * 6
            guard(G_IDS)
            ptiles = [t for t in tiles if t.kind == 'p']
            stiles = [t for t in tiles if t.kind == 's']
            last = (g_ == NG - 1)
            p.op('dve', I.tensor_copy(out=glub[:, :, 0:30], in_=glu_halo[:, :, :]), reads=['glu_halo'], writes=['glub_halo'])
            for q in range(2):
                (wv0, wid0), (wv1, wid1) = w_getn([(g_, 'a_in', q, 0), (g_, 'a_in', q, 1)])
                order = [(jj, t) for t in tiles for jj in range(4)] if q == 0 else [(jj, t) for jj in range(4) for t in tiles]
                for (jj, t) in order:
                    c = q * 4 + jj
                    need(t)
                    pump()
                    if True:
                        n = t.n
                        bv = bank()
                        mm(ps[bv][:, 0:n], [(wv0[:, kc, jj * 128:(jj + 1) * 128], xn[:, kc, t.sl]) for kc in range(8)],
                           reads=ids('xn', t.idx) + [wid0], writes=[('ps', bv)])
                        bg_ = bank()
                        mm(ps[bg_][:, 0:n], [(wv1[:, kc, jj * 128:(jj + 1) * 128], xn[:, kc, t.sl]) for kc in range(8)],
                           reads=ids('xn', t.idx) + [wid1], writes=[('ps', bg_)])
                        k = sg_rr[0]
                        sg_rr[0] = (k + 1) % NSG
                        p.op('act', I.activation(out=sg[k][:, 0:n], in_=ps[bg_][:, 0:n], func=AF.Tanh, scale=0.5),
                             reads=[('ps', bg_)], writes=[('sg', k)])
                        p.op('dve', I.scalar_tensor_tensor(out=sg2[k][:, 0:n], in0=sg[k][:, 0:n], scalar=1.0,
                                                                                       in1=ps[bv][:, 0:n], op0=ALU.add, op1=ALU.mult),
                             reads=[('ps', bv), ('sg', k)], writes=[('sg2', k)])
                        if t.kind == 'p':
                            p.op('act', I.activation(out=glub[:, c, 30 + t.off:30 + t.off + n], in_=sg2[k][:, 0:n],
                                                                                   func=AF.Copy, scale=0.5),
                                 reads=[('sg2', k)], writes=[('glub', t.idx, c)])
                            if last and t.idx == NPT - 1:
                                p.op('act', I.activation(out=gluf_tail[:, c, :], in_=sg2[k][:, n - 30:n], func=AF.Copy, scale=0.5),
                                     reads=[('sg2', k)], writes=[('gluf_tail', c)])
                        else:
                            p.op('act', I.activation(out=exts[:, c, 30:34, :], in_=sg2[k][:, 0:n].rearrange("p (a b) -> p a b", a=4),
                                                                              func=AF.Copy, scale=0.5),
                                 reads=[('sg2', k)], writes=[('exts', c)])
                            p.op('act', I.activation(out=glus[:, c, :], in_=sg2[k][:, 0:n], func=AF.Copy, scale=0.5),
                                 reads=[('sg2', k)], writes=[('glus', c)])
                w_done(2)
            if stiles:
                for blk in range(4):
                    load_T(sa[blk * 120:(blk + 1) * 120, :], 120,
                           lambda c, src, blk=blk: (exts[:, c, 0:30, blk * 4:(blk + 1) * 4].rearrange("p r s -> p s r"),
                                                    src.rearrange("p (s r) -> p s r", s=4)),
                           [('exts_st', c) for c in range(8)])
            def build_diag(c):
                p.op('dve', I.tensor_tensor(out=diag[c % 2][:, :, :], in0=identb[:, :].unsqueeze(1).broadcast_to([128, 31, 128]),
                                            in1=vT[:, c, R_ACW:R_ACW + 31].unsqueeze(2).broadcast_to([128, 31, 128]), op=ALU.mult),
                     reads=['identb', ('vT', c)], writes=[('diag', c % 2)])

            build_diag(0)
            for c in range(8):
                dc_ = diag[c % 2]
                did = ('diag', c % 2)
                if c + 1 < 8:
                    build_diag(c + 1)
                for t in tiles:
                    n = t.n
                    b = bank()
                    if t.kind == 'p':
                        pairs = [(dc_[:, k, :], glub[:, c, t.off + k:t.off + k + n]) for k in range(31)]
                        rd = [('glub', tt.idx, c) for tt in ptiles if tt.idx <= t.idx] + ['glub_halo']
                    else:
                        pairs = [(dc_[:, k, :], exts[:, c, k:k + 4, :].rearrange("p a b -> p (a b)")) for k in range(31)]
                        rd = [('exts', c), ('exts_st', c)]
                    mm(ps[b][:, 0:n], pairs, reads=rd + [did], writes=[('ps', b)])
                    p.op('act', I.activation(out=f[:, c, t.sl], in_=ps[b][:, 0:n], func=AF.Identity, bias=vrow(c, R_ACB)),
                         reads=[('ps', b), ('vT', c)], writes=[('f', t.idx, c)])
                    p.op('act', I.activation(out=sqb[t.idx][:, c, 0:n], in_=ps[b][:, 0:n], func=AF.Square, bias=vrow(c, R_ACB)),
                         reads=[('ps', b), ('vT', c)], writes=[('sqb', t.idx, c)])
                    p.op('dve', I.tensor_copy(out=cbb[t.idx][:, c, 0:n], in_=f[:, c, t.sl]),
                         reads=[('f', t.idx, c)], writes=[('cbb', t.idx, c)])
            p.op('dve', I.tensor_copy(out=glu_halo[:, :, :], in_=glub[:, :, GP:GP + 30]),
                 reads=[('glub', NPT - 1, c) for c in range(8)], writes=['glu_halo'])
            for t in tiles:
                n = t.n
                ln_stats(t, None)
                for c in range(8):
                    p.op('act', I.activation(out=yb[:, c, t.sl], in_=f[:, c, t.sl], func=AF.Silu,
                                                                 scale=vrow(c, R_ALG), bias=vrow(c, R_ALB)),
                         reads=[('f', t.idx, c), ('vT', c)], writes=[('yb', t.idx, c)])
            if stiles:
                store_T(lambda c: glus[:, c, :], [('glus', c) for c in range(8)], NS,
                        [(tt * 16, 16, o_sa_s.rearrange("(s r) n -> r s n", r=30)[26 + tt]) for tt in range(4)])
            if last:
                store_T(lambda c: gluf_tail[:, c, :], [('gluf_tail', c) for c in range(8)], 30, [(0, 30, o_sa_p[:, :])])
            out_proj(g_, tiles, ('a_out',), lambda m: vrow(m, lrow + 3), after)
            guard(['glub_halo'] + [('glub', ti, c) for ti in range(NPT) for c in range(8)] + [(nm, c) for nm in ('exts', 'exts_st', 'glus') for c in range(8)] + [('diag', 0), ('diag', 1)])

        def mixer_b(g_, tiles, after):
            lrow = R_NG + 1 * 6
            guard(G_IDS)
            ptiles = [t for t in tiles if t.kind == 'p']
            stiles = [t for t in tiles if t.kind == 's']
            last = (g_ == NG - 1)
            p.op('dve', I.tensor_copy(out=zp[:, :, 0:2], in_=z_halo[:, :, :]), reads=['z_halo'], writes=['zp_halo'])
            if stiles:
                load_T(sbi[:, :], 32,
                       lambda c, src: (zs[:, c, 0:2, :].rearrange("p r s -> p s r"), src.rearrange("p (s r) -> p s r", s=16)),
                       [('zs_st', c) for c in range(8)])
            for q in range(2):
                wv3 = w_getn([(g_, 'b_in', q, part) for part in range(3)])
                order = [(jj, t) for t in tiles for jj in range(4)] if q == 0 else [(jj, t) for jj in range(4) for t in tiles]
                for (jj, t) in order:
                    c = q * 4 + jj
                    need(t)
                    pump()
                    if True:
                        n = t.n
                        banks3 = []
                        for part in range(3):
                            b = bank()
                            mm(ps[b][:, 0:n], [(wv3[part][0][:, kc, jj * 128:(jj + 1) * 128], xn[:, kc, t.sl]) for kc in range(8)],
                               reads=ids('xn', t.idx) + [wv3[part][1]], writes=[('ps', b)])
                            banks3.append(b)
                        bB, bC, bX = banks3
                        k = sg_rr[0]
                        sg_rr[0] = (k + 1) % NSG
                        p.op('act', I.activation(out=sg[k][:, 0:n], in_=ps[bX][:, 0:n], func=AF.Copy),
                             reads=[('ps', bX)], writes=[('sg', k)])
                        if t.kind == 'p':
                            zdst = zp[:, c, 2 + t.off:2 + t.off + n]
                            zid = ('zp', t.idx, c)
                        else:
                            zdst = zs[:, c, 2:6, :].rearrange("p a b -> p (a b)")
                            zid = ('zs', c)
                        p.op('dve', I.tensor_tensor(out=zdst, in0=ps[bC][:, 0:n], in1=sg[k][:, 0:n], op=ALU.mult),
                             reads=[('ps', bC), ('sg', k)], writes=[zid])
                        if t.kind == 'p':
                            def zsh(s_, c=c, t=t, n=n):
                                return zp[:, c, t.off + s_:t.off + s_ + n]
                            rd = [('zp', tt.idx, c) for tt in ptiles if tt.idx <= t.idx] + ['zp_halo']
                        else:
                            def zsh(s_, c=c):
                                return zs[:, c, s_:s_ + 4, :].rearrange("p a b -> p (a b)")
                            rd = [('zs', c), ('zs_st', c)]
                        p.op('dve', I.scalar_tensor_tensor(out=sg2[k][:, 0:n], in0=zsh(0), scalar=vrow(c, R_BCW + 0),
                                                           in1=ones_f[:, 0:n], op0=ALU.mult, op1=ALU.mult),
                             reads=rd + [('vT', c), 'ones_f'], writes=[('sg2', k)])
                        for s_ in (1, 2):
                            p.op('dve', I.scalar_tensor_tensor(out=sg2[k][:, 0:n], in0=zsh(s_), scalar=vrow(c, R_BCW + s_),
                                                                                                        in1=sg2[k][:, 0:n], op0=ALU.mult, op1=ALU.add),
                                 reads=rd + [('vT', c), ('sg2', k)], writes=[('sg2', k)])
                        p.op('dve', I.tensor_tensor(out=yb[:, c, t.sl], in0=ps[bB][:, 0:n], in1=sg2[k][:, 0:n], op=ALU.mult),
                             reads=[('ps', bB), ('sg2', k)], writes=[('yb', t.idx, c)])
                w_done(3)
            p.op('dve', I.tensor_copy(out=z_halo[:, :, :], in_=zp[:, :, GP:GP + 2]),
                 reads=[('zp', NPT - 1, c) for c in range(8)], writes=['z_halo'])
            if stiles:
                store_T(lambda c: zs[:, c, 4:6, :].rearrange("p a b -> p (a b)"), [('zs', c) for c in range(8)], 32,
                        [(r * 16, 16, o_sb_s.rearrange("(s r) n -> r s n", r=2)[r]) for r in range(2)])
            if last:
                store_T(lambda c: zp[:, c, GP:GP + 2], [('zp', NPT - 1, c) for c in range(8)], 2, [(0, 2, o_sb_p[:, :])])
            out_proj(g_, tiles, ('b_out',), lambda m: vrow(m, lrow + 3), after)
            guard(['zp_halo'] + [('zp', ti, c) for ti in range(NPT) for c in range(8)] + [(nm, c) for nm in ('zs', 'zs_st') for c in range(8)])

        def mixer_c(g_, tiles, after):
            lrow = R_NG + 2 * 6
            guard(G_IDS)
            stiles = [t for t in tiles if t.kind == 's']
            for q in range(2):
                (wv0, wid0), (wv1, wid1) = w_getn([(g_, 'c_in', q, 0), (g_, 'c_in', q, 1)])
                order = [(jj, t) for t in tiles for jj in range(4)] if q == 0 else [(jj, t) for jj in range(4) for t in tiles]
                for (jj, t) in order:
                    c = q * 4 + jj
                    need(t)
                    pump()
                    if True:
                        n = t.n
                        bu_ = bank()
                        mm(ps[bu_][:, 0:n], [(wv0[:, kc, jj * 128:(jj + 1) * 128], xn[:, kc, t.sl]) for kc in range(8)],
                           reads=ids('xn', t.idx) + [wid0], writes=[('ps', bu_)])
                        bv = bank()
                        mm(ps[bv][:, 0:n], [(wv1[:, kc, jj * 128:(jj + 1) * 128], xn[:, kc, t.sl]) for kc in range(8)],
                           reads=ids('xn', t.idx) + [wid1], writes=[('ps', bv)])
                        p.op('act', I.activation(out=ubuf[:, c, t.sl], in_=ps[bu_][:, 0:n], func=AF.Gelu),
                             reads=[('ps', bu_)], writes=[('u', t.idx, c)])
                        p.op('act', I.activation(out=f[:, c, t.sl], in_=ps[bv][:, 0:n], func=AF.Gelu),
                             reads=[('ps', bv)], writes=[('f', t.idx, c)])
                        p.op('dve', I.tensor_copy(out=cbb[t.idx][:, c, 0:n], in_=f[:, c, t.sl]),
                             reads=[('f', t.idx, c)], writes=[('cbb', t.idx, c)])
                        p.op('dve', I.tensor_tensor(out=sqb[t.idx][:, c, 0:n], in0=f[:, c, t.sl], in1=f[:, c, t.sl], op=ALU.mult),
                             reads=[('f', t.idx, c)], writes=[('sqb', t.idx, c)])
                w_done(2)
            for t in tiles:
                n = t.n
                ln_stats(t, None)
                for c in range(8):
                    p.op('act', I.activation(out=f[:, c, t.sl], in_=f[:, c, t.sl], func=AF.Identity,
                                                                 scale=vrow(c, R_CLG), bias=vrow(c, R_CLB)),
                         reads=[('f', t.idx, c), ('vT', c)], writes=[('f', t.idx, c)])
                if t.kind == 'p':
                    nblk = n // 128
                    for nb_ in range(nblk):
                        vt = vtok[t.idx * nblk + nb_]
                        vid = ('vtok', t.idx * nblk + nb_)
                        for half in range(2):
                            b = bank()
                            fns = [(I.transpose(ps[b][:, k4 * 128:(k4 + 1) * 128],
                                                                                              f[:, half * 4 + k4, t.off + nb_ * 128:t.off + (nb_ + 1) * 128],
                                                                                              ident[:, :])) for k4 in range(4)]
                            p.op('pe', fns, reads=ids('f', t.idx) + ['ident'], writes=[('ps', b)])
                            p.op('act', I.activation(out=vt[:, half * 512:(half + 1) * 512], in_=ps[b][:, :], func=AF.Copy),
                                 reads=[('ps', b)], writes=[(vid, half)])
                    for hh in range(8):
                        b = bank()
                        for nb_ in range(nblk):
                            vt = vtok[t.idx * nblk + nb_]
                            vid = ('vtok', t.idx * nblk + nb_)
                            mm(ps[b][:, nb_ * 128:(nb_ + 1) * 128], [(vt[:, hh * 128:(hh + 1) * 128], wmT[:, hh, :])],
                               reads=[(vid, hh // 4), ('wmT', hh)], writes=[('ps', b)])
                        k = sg_rr[0]
                        sg_rr[0] = (k + 1) % NSG
                        p.op('dve', I.tensor_tensor(
                            out=sg[k][:, 0:n].rearrange("p (a i) -> p a i", a=nblk), in0=ps[b][:, 0:n].rearrange("p (a i) -> p a i", a=nblk),
                            in1=bsb[:, hh, :].unsqueeze(1).broadcast_to([128, nblk, 128]), op=ALU.add),
                            reads=[('ps', b), 'cst'], writes=[('sg', k)])
                        p.op('dve', I.tensor_tensor(out=yb[:, hh, t.sl], in0=sg[k][:, 0:n], in1=ubuf[:, hh, t.sl], op=ALU.mult),
                             reads=[('sg', k), ('u', t.idx, hh)], writes=[('yb', t.idx, hh)])
                else:
                    for hh in range(8):
                        for i in range(4):
                            p.op('dve', I.scalar_tensor_tensor(out=accs[:, i * 16:(i + 1) * 16], in0=f[:, hh, t.off:t.off + 16],
                                                               scalar=wsb[:, hh, i * 4:i * 4 + 1], in1=bs4[:, hh, i:i + 1].broadcast_to([128, 16]),
                                                               op0=ALU.mult, op1=ALU.add),
                                 reads=[('f', t.idx, hh), 'cst'], writes=['accs'])
                            for j in range(1, i + 1):
                                p.op('dve', I.scalar_tensor_tensor(out=accs[:, i * 16:(i + 1) * 16],
                                                                                                   in0=f[:, hh, t.off + j * 16:t.off + (j + 1) * 16],
                                                                                                   scalar=wsb[:, hh, i * 4 + j:i * 4 + j + 1],
                                                                                                   in1=accs[:, i * 16:(i + 1) * 16], op0=ALU.mult, op1=ALU.add),
                                     reads=[('f', t.idx, hh), 'cst', 'accs'], writes=['accs'])
                        p.op('dve', I.tensor_tensor(out=yb[:, hh, t.sl], in0=accs[:, :], in1=ubuf[:, hh, t.sl], op=ALU.mult),
                             reads=['accs', ('u', t.idx, hh)], writes=[('yb', t.idx, hh)])
                    store_T(lambda c, t=t: f[:, c, t.sl], ids('f', t.idx), NS,
                            [(tt * 16, 16, o_sc_s.rearrange("(s r) n -> r s n", r=4)[tt]) for tt in range(4)])
            out_proj(g_, tiles, ('c_out',), lambda m: vrow(m, lrow + 3), after)
            guard([('u', ti, c) for ti in range(NTLX) for c in range(8)] + [(('vtok', i), hf) for i in range(GP // 128) for hf in range(2)] + ['accs'])

        def mixer_d(g_, tiles, after):
            lrow = R_NG + 3 * 6
            flush_all()
            guard(G_IDS)
            ptiles = [t for t in tiles if t.kind == 'p']
            stiles = [t for t in tiles if t.kind == 's']
            last = (g_ == NG - 1)
            p.op('dve', I.tensor_copy(out=xme_p[:, :, 0:15], in_=xm_halo[:, :, :]), reads=['xm_halo'], writes=['xme_halo'])
            for t in tiles:
                if t.kind == 'p':
                    prenorm(t, lrow + 2, lambda c, t=t: xme_p[:, c, 15 + t.off:15 + t.off + t.n], ids('xme', t.idx))
                else:
                    prenorm(t, lrow + 2, lambda c, t=t: xme_s[:, c, 15:19, :].rearrange("p a b -> p (a b)"), ids('xme', t.idx))
            if stiles:
                for blk in range(2):
                    load_T(spi[blk * 120:(blk + 1) * 120, :], 120,
                           lambda c, src, blk=blk: (xme_s[:, c, 0:15, blk * 8:(blk + 1) * 8].rearrange("p r s -> p s r"),
                                                    src.rearrange("p (s r) -> p s r", s=8)),
                           [('xmes_st', c) for c in range(8)])
            wv, wid = w_get((g_, 'd_w', 0))
            for t in tiles:
                n = t.n
                for gi, w in enumerate((2, 4, 8, 16)):
                    L = {2: 1, 4: 2, 8: 3, 16: 4}[w]
                    cs = slice(2 * gi, 2 * gi + 2)
                    if t.kind == 'p':
                        B0 = 15 + t.off

                        def xv(lo, hi, cs=cs, B0=B0):
                            return xme_p[:, cs, B0 + lo:B0 + hi]

                        def tv(i, lo, hi):
                            return ptmp[i][:, :, 15 + lo:15 + hi]
                        rd = [('xme', tt.idx, cc) for tt in ptiles if tt.idx <= t.idx for cc in (2 * gi, 2 * gi + 1)] + ['xme_halo']
                    else:
                        def xv(lo, hi, cs=cs):
                            return xme_s[:, cs, 15 + lo:15 + hi, :]

                        def tv(i, lo, hi):
                            return ptmp[i][:, :, 0:19 * 16].rearrange("p c (r s) -> p c r s", s=16)[:, :, 15 + lo:15 + hi, :]
                        rd = [('xme', t.idx, cc) for cc in (2 * gi, 2 * gi + 1)] + [('xmes_st', cc) for cc in (2 * gi, 2 * gi + 1)]
                    nn = n if t.kind == 'p' else 4
                    prev = None
                    for lv in range(1, L + 1):
                        lo = -(w - 2 ** lv)
                        sh = 2 ** (lv - 1)
                        dst = tv(lv % 2, lo, nn)
                        if lv == 1:
                            a_, b_ = xv(lo, nn), xv(lo - sh, nn - sh)
                            rds = rd
                        else:
                            a_, b_ = tv((lv - 1) % 2, lo, nn), tv((lv - 1) % 2, lo - sh, nn - sh)
                            rds = [('ptmp', (lv - 1) % 2)]
                        p.op('dve', I.tensor_tensor(out=dst, in0=a_, in1=b_, op=ALU.add),
                             reads=rds, writes=[('ptmp', lv % 2)])
                    pooled = tv(L % 2, 0, nn)
                    if t.kind == 'p':
                        dd = yb[:, cs, t.sl]
                    else:
                        dd = yb[:, cs, t.sl].rearrange("p c (r s) -> p c r s", s=16)
                    p.op('dve', I.scalar_tensor_tensor(out=dd, in0=pooled, scalar=1.0 / w, in1=xv(0, nn),
                                                                                                         op0=ALU.mult, op1=ALU.subtract),
                         reads=[('ptmp', L % 2)] + rd, writes=[('yb', t.idx, 2 * gi), ('yb', t.idx, 2 * gi + 1)])
                    if t.kind == 'p' and t.tok0 == 0:
                        icb = invcnt[:, gi, 0:w - 1].unsqueeze(1).broadcast_to([128, 2, w - 1])
                        p.op('dve', I.tensor_tensor(out=tv((L + 1) % 2, 0, w - 1), in0=tv(L % 2, 0, w - 1), in1=icb, op=ALU.mult),
                             reads=[('ptmp', L % 2), 'invcnt'], writes=[('ptmp', (L + 1) % 2)])
                        p.op('dve', I.tensor_tensor(out=yb[:, cs, t.off:t.off + w - 1], in0=tv((L + 1) % 2, 0, w - 1),
                                                                                                 in1=xv(0, w - 1), op=ALU.subtract),
                             reads=[('ptmp', (L + 1) % 2)] + rd, writes=[('yb', t.idx, 2 * gi), ('yb', t.idx, 2 * gi + 1)])
                for gi in range(4):
                    for dc in range(2):
                        m = 2 * gi + dc
                        b = bank()
                        mm(ps[b][:, 0:n], [(wv[:, gi, cc, dc * 128:(dc + 1) * 128], yb[:, 2 * gi + cc, t.sl]) for cc in range(2)],
                           reads=[('yb', t.idx, 2 * gi), ('yb', t.idx, 2 * gi + 1)] + wids(wid), writes=[('ps', b)])
                        p.op('dve', I.scalar_tensor_tensor(out=f[:, m, t.sl], in0=ps[b][:, 0:n], scalar=dg[:, m:m + 1],
                                                           in1=ones_f[:, 0:n], op0=ALU.mult, op1=ALU.mult),
                             reads=[('ps', b), 'dg', 'ones_f'], writes=[('f', t.idx, m)])
                        p.op('act', I.activation(out=sqb[t.idx][:, m, 0:n], in_=ps[b][:, 0:n], func=AF.Square,
                                                                               scale=vrow(m, R_DSC)),
                             reads=[('ps', b), ('vT', m)], writes=[('sqb', t.idx, m)])
            w_done()
            p.op('dve', I.tensor_copy(out=xm_halo[:, :, :], in_=xme_p[:, :, GP:GP + 15]),
                 reads=[('xme', NPT - 1, c) for c in range(8)], writes=['xm_halo'])
            if stiles:
                t = stiles[0]
                for blk in range(2):
                    r0 = 4 + blk * 8 if blk == 0 else 12
                    nr = 8 if blk == 0 else 7
                    store_T(lambda c, r0=r0, nr=nr: xme_s[:, c, r0:r0 + nr, :].rearrange("p a b -> p (a b)"),
                            [('xme', t.idx, c) for c in range(8)], nr * 16,
                            [(rr * 16, 16, o_sp_s.rearrange("(s r) n -> r s n", r=15)[r0 - 4 + rr]) for rr in range(nr)])
            if last:
                store_T(lambda c: xme_p[:, c, GP:GP + 15], [('xme', NPT - 1, c) for c in range(8)], 15, [(0, 15, o_sp_p[:, :])])
            for t in tiles:
                after(t)
            guard(['xme_halo'] + [('xme', ti, c) for ti in range(NTLX) for c in range(8)] + [('xmes_st', c) for c in range(8)] + [('ptmp', 0), ('ptmp', 1)])

        for g_ in range(NGR):
            tiles = [TileT('p', i, i * TS, TS, g_ * GP + i * TS) for i in range(NPT)]
            if g_ == 0:
                tiles.append(TileT('s', NPT, GP, NS, 0))
            for t in tiles:
                if t.kind == 'p':
                    for nb_ in range(t.n // 128):
                        r0 = t.tok0 + nb_ * 128
                        load_T(xp[r0:r0 + 128, :], 128,
                               lambda c, src, t=t, nb_=nb_: (h[:, c, t.off + nb_ * 128:t.off + (nb_ + 1) * 128], src),
                               ids('h', t.idx))
                else:
                    load_T(None, NS, lambda c, src, t=t: (h[:, c, t.sl], src), ids('h', t.idx),
                           src_list=[(tt * 16, 16, xs.rearrange("(s r) n -> r s n", r=4)[tt]) for tt in range(4)])
            assert DBG_LAYERS == 4 and DBG_SUB == ('f0', 'm', 'f1') and DBG_STOP == 99

            def pre_xn(t, row):
                prenorm(t, row, lambda c, t=t: xn[:, c, t.sl], ids('xn', t.idx))

            for t in tiles:
                pre_xn(t, R_NG + 0)
            for l in range(4):
                lrow = R_NG + l * 6

                def after_f0(t, l=l, lrow=lrow):
                    start_chain(t, lrow + 2 if l < 3 else None)

                def after_m(t, lrow=lrow):
                    start_chain(t, lrow + 4)

                def after_f1(t, l=l, lrow=lrow):
                    start_chain(t, lrow + 6 if l < 3 else None)

                ffn(g_, l, 0, tiles, after_f0)
                tap("L%dF0" % l, tiles, g_)
                [mixer_a, mixer_b, mixer_c, mixer_d][l](g_, tiles, after_m)
                tap("L%dM" % l, tiles, g_)
                ffn(g_, l, 1, tiles, after_f1)
                tap("L%dF1" % l, tiles, g_)
            flush_all()
            for t in tiles:
                if t.kind == 'p':
                    for nb_ in range(t.n // 128):
                        r0 = t.tok0 + nb_ * 128
                        store_T(lambda c, t=t, nb_=nb_: h[:, c, t.off + nb_ * 128:t.off + (nb_ + 1) * 128], ids('h', t.idx), 128,
                                [(0, 128, yp[r0:r0 + 128, :])])
                else:
                    store_T(lambda c, t=t: h[:, c, t.sl], ids('h', t.idx), NS,
                            [(tt * 16, 16, ys.rearrange("(s r) n -> r s n", r=4)[tt]) for tt in range(4)])
        assert DBG_STOP < 99 or WS.cur == len(plan), (WS.cur, len(plan))
        p.wait_all('sp', out_toks)
        print('stream sizes', {k: len(v) for k, v in p.streams.items()}, 'counts', p.ecount, 'dma', {str(k): v for k, v in p.dcount.items() if v > 2000})
        p.emit()
    return nc


_NC_CACHE = {}


def kernel(x_prompt, x_sample, state_conv_a, state_conv_b, state_pool,
           norm_g, ffn_w_in, ffn_w_down,
           a_w_in, a_conv_w, a_conv_b, a_ln_g, a_ln_b, a_w_out,
           b_w_in, b_conv_w, b_w_out,
           c_w_in, c_ln_g, c_ln_b, c_ws, c_bs, c_w_out,
           d_w_group, d_scale):
    f32 = lambda a: np.ascontiguousarray(np.asarray(a, dtype=np.float32))
    x_prompt, x_sample = f32(x_prompt), f32(x_sample)
    state_conv_a, state_conv_b, state_pool = f32(state_conv_a), f32(state_conv_b), f32(state_pool)
    vecs = np.concatenate([
        f32(norm_g).reshape(24, D), f32(a_conv_w).reshape(31, D), f32(a_conv_b).reshape(1, D),
        f32(a_ln_g).reshape(1, D), f32(a_ln_b).reshape(1, D), f32(b_conv_w).reshape(3, D),
        f32(c_ln_g).reshape(1, D), f32(c_ln_b).reshape(1, D), f32(d_scale).reshape(1, D)], axis=0)
    assert vecs.shape == (64, D)
    shared = {
        "vecs": f32(vecs), "ffn_w_in": f32(ffn_w_in), "ffn_w_down": f32(ffn_w_down),
        "a_w_in": f32(a_w_in), "a_w_out": f32(a_w_out), "b_w_in": f32(b_w_in), "b_w_out": f32(b_w_out),
        "c_w_in": f32(c_w_in), "c_w_out": f32(c_w_out), "c_ws": f32(c_ws), "c_bs": f32(c_bs),
        "d_w_group": f32(d_w_group),
    }
    in_maps = []
    for i in range(NCORES):
        m = dict(shared)
        m["xp"] = x_prompt[i]
        m["xs"] = x_sample[16 * i:16 * (i + 1)].reshape(NS, D)
        m["sa"] = state_conv_a[16 * i:16 * (i + 1)].reshape(16 * 30, D)
        m["sb"] = state_conv_b[16 * i:16 * (i + 1)].reshape(16 * 2, D)
        m["sp"] = state_pool[16 * i:16 * (i + 1)].reshape(16 * 15, D)
        in_maps.append(m)
    key = (TS, NPT, tuple(sorted(DEBUG_TAPS)) if DEBUG_TAPS else None, DBG_NG, DBG_LAYERS, DBG_SUB, DBG_STOP, DBG_VAR)
    if key not in _NC_CACHE:
        _NC_CACHE[key] = build_program()
    nc = _NC_CACHE[key]
    ncr = NCORES if DBG_CORES is None else DBG_CORES
    res = run_bass_kernel_spmd(nc, in_maps[:ncr], core_ids=list(range(ncr)))
    R = list(res.results)
    while len(R) < NCORES:
        R.append(R[0])

    def cat(name, shape):
        return np.stack([np.asarray(R[i][name], dtype=np.float32).reshape(shape) for i in range(NCORES)], axis=0)
    y_prompt = cat("yp", (SEQ, D))
    y_sample = cat("ys", (16, 4, D)).reshape(128, 4, D)
    sa_p = cat("o_sa_p", (30, D))
    sa_s = cat("o_sa_s", (16, 30, D)).reshape(128, 30, D)
    sb_p = cat("o_sb_p", (2, D))
    sb_s = cat("o_sb_s", (16, 2, D)).reshape(128, 2, D)
    sc_s = cat("o_sc_s", (16, 4, D)).reshape(128, 4, D)
    sp_p = cat("o_sp_p", (15, D))
    sp_s = cat("o_sp_s", (16, 15, D)).reshape(128, 15, D)
    if DEBUG_TAPS:
        kernel.last_taps = {nm: [np.asarray(R[i]["tap_" + nm]) for i in range(NCORES)] for nm in DEBUG_TAPS}
    return (y_prompt, y_sample, sa_p, sa_s, sb_p, sb_s, sc_s, sp_p, sp_s)
```
